# Optimizing a Trainium2 kernel written in Bass

```python
import math
import jax, jax.numpy as jnp
from jax import lax
import numpy as np

D_MODEL = 2048
BATCH = 4
SEQ = 2048
DEPTH = 2
DEC_BATCH = 128
DEC_SEQ = 8
PAST_LEN = 16384
PAGE_SIZE = 128

D_DELTA = D_MODEL // 2
D_CONF = D_MODEL - D_DELTA
HEAD_DIM = 128
N_HEADS = D_DELTA // HEAD_DIM
QKV_CONV = 4
CONF_KERNEL = 31
CHUNK = 64
EPS = 1e-6
SPLIT_SIZES = (D_DELTA, D_DELTA, D_DELTA, D_DELTA, N_HEADS, N_HEADS, D_CONF, D_CONF, D_CONF)
D_IN_PROJ = 4 * D_DELTA + 2 * N_HEADS + 3 * D_CONF

kernel_name = "hybrid_gdn_conformer_adaln_step"


def rms_norm(x, g):
    xf = x.astype(jnp.float32)
    return xf * lax.rsqrt(jnp.mean(xf * xf, axis=-1, keepdims=True) + EPS) * g.astype(jnp.float32)


def layer_norm(x, g, b):
    xf = x.astype(jnp.float32)
    mu = jnp.mean(xf, axis=-1, keepdims=True)
    xc = xf - mu
    var = jnp.mean(xc * xc, axis=-1, keepdims=True)
    return xc * lax.rsqrt(var + EPS) * g.astype(jnp.float32) + b.astype(jnp.float32)


def l2_normalize(x):
    xf = x.astype(jnp.float32)
    return xf * lax.rsqrt(jnp.sum(xf * xf, axis=-1, keepdims=True) + EPS)


def split_columns(p):
    outs, start = [], 0
    for size in SPLIT_SIZES:
        outs.append(p[..., start:start + size])
        start += size
    return outs


def causal_dwconv(x, buf, w):
    K, C = w.shape
    xp = jnp.concatenate([buf.astype(x.dtype), x], axis=1)
    out = lax.conv_general_dilated(
        xp, w.astype(x.dtype)[:, None, :], window_strides=(1,), padding='VALID',
        dimension_numbers=('NWC', 'WIO', 'NWC'), feature_group_count=C)
    new_buf = xp[:, xp.shape[1] - (K - 1):]
    return out, new_buf


def gated_delta_rule(q, k, v, g, beta, S0):
    B, T, H, dk = q.shape
    dv = v.shape[-1]
    C = math.gcd(T, CHUNK)
    N = T // C

    def to_chunks(t):
        return t.reshape(B, N, C, H, t.shape[-1]).transpose(1, 0, 3, 2, 4)

    q = to_chunks(q) * (dk ** -0.5)
    k = to_chunks(k)
    v = to_chunks(v.astype(jnp.float32))
    g = g.reshape(B, N, C, H).transpose(1, 0, 3, 2)
    beta = beta.reshape(B, N, C, H).transpose(1, 0, 3, 2)
    gc = jnp.cumsum(g, axis=-1)
    causal = jnp.tril(jnp.ones((C, C), dtype=bool))
    strict = jnp.tril(jnp.ones((C, C), dtype=bool), -1)
    diff = gc[..., :, None] - gc[..., None, :]
    decay = jnp.exp(jnp.where(causal, diff, -jnp.inf))
    kk = jnp.einsum('nbhtd,nbhsd->nbhts', k, k)
    L = jnp.where(strict, beta[..., :, None] * kk * decay, 0.0)
    a_mat = jnp.eye(C, dtype=jnp.float32) + L
    rhs = jnp.concatenate([beta[..., None] * v, (beta * jnp.exp(gc))[..., None] * k], axis=-1)
    sol = lax.linalg.triangular_solve(a_mat, rhs, left_side=True, lower=True)
    Uv, W = sol[..., :dv], sol[..., dv:]
    attn = jnp.einsum('nbhtd,nbhsd->nbhts', q, k) * decay
    qG = q * jnp.exp(gc)[..., None]
    kdec = k * jnp.exp(gc[..., -1:] - gc)[..., None]
    gtot = jnp.exp(gc[..., -1])

    def step(S, xs):
        Uv_n, W_n, attn_n, qG_n, kdec_n, gt_n = xs
        U = Uv_n - jnp.einsum('bhck,bhkv->bhcv', W_n, S)
        o = jnp.einsum('bhck,bhkv->bhcv', qG_n, S) + jnp.einsum('bhcs,bhsv->bhcv', attn_n, U)
        S = gt_n[..., None, None] * S + jnp.einsum('bhck,bhcv->bhkv', kdec_n, U)
        return S, o

    S, o = lax.scan(step, S0.astype(jnp.float32), (Uv, W, attn, qG, kdec, gtot))
    o = o.transpose(1, 0, 3, 2, 4).reshape(B, T, H, dv)
    return o, S


def mixer_layer(x, c, S0, qkv_buf, glu_buf, norm_g, w_ada, b_ada, w_in, w_qkv_conv,
                a_log, dt_bias, head_norm_g, w_dw, b_dw, ln_g, ln_b, w_out):
    B, T, D = x.shape
    mod = jnp.dot(jax.nn.silu(c), w_ada) + b_ada
    shift, scale, gate = mod[:, :D], mod[:, D:2 * D], mod[:, 2 * D:]
    h = (rms_norm(x, norm_g) * (1.0 + scale[:, None, :]) + shift[:, None, :]).astype(x.dtype)
    proj = jnp.einsum('btd,de->bte', h, w_in)
    q, k, v, z_d, b_raw, a_raw, glu_a, glu_b, z_c = split_columns(proj)

    qkv = jnp.concatenate([q, k, v], axis=-1)
    qkv, new_qkv_buf = causal_dwconv(qkv, qkv_buf, w_qkv_conv)
    qkv = jax.nn.silu(qkv)
    q = l2_normalize(qkv[..., :D_DELTA].reshape(B, T, N_HEADS, HEAD_DIM))
    k = l2_normalize(qkv[..., D_DELTA:2 * D_DELTA].reshape(B, T, N_HEADS, HEAD_DIM))
    v = qkv[..., 2 * D_DELTA:].reshape(B, T, N_HEADS, HEAD_DIM)
    beta = jax.nn.sigmoid(b_raw.astype(jnp.float32))
    g = -jnp.exp(a_log.astype(jnp.float32)) * jax.nn.softplus(a_raw.astype(jnp.float32) + dt_bias.astype(jnp.float32))
    o, S_new = gated_delta_rule(q, k, v, g, beta, S0)
    o = rms_norm(o, head_norm_g).reshape(B, T, D_DELTA)
    o = (o * jax.nn.silu(z_d.astype(jnp.float32))).astype(x.dtype)

    u = glu_a * jax.nn.sigmoid(glu_b)
    u, new_glu_buf = causal_dwconv(u, glu_buf, w_dw)
    u = u + b_dw
    u = jax.nn.silu(layer_norm(u, ln_g, ln_b))
    u = (u * jax.nn.silu(z_c.astype(jnp.float32))).astype(x.dtype)

    mix = jnp.einsum('bte,ed->btd', jnp.concatenate([o, u], axis=-1), w_out)
    x = x + (gate[:, None, :] * mix).astype(x.dtype)
    return x, S_new, new_qkv_buf, new_glu_buf


def setup_inputs(seed: int = 0) -> dict:
    key = jax.random.key(seed)
    ks = jax.random.split(key, 24)
    f32 = jnp.float32
    D = D_MODEL
    x_prompt = jax.random.normal(ks[0], (BATCH, SEQ, D), f32)
    x_sample = jax.random.normal(ks[1], (DEC_BATCH, DEC_SEQ, D), f32)
    c_prompt = jax.random.normal(ks[2], (BATCH, D), f32)
    c_sample = jax.random.normal(ks[3], (DEC_BATCH, D), f32)
    state_delta = 0.1 * jax.random.normal(ks[4], (DEPTH, DEC_BATCH, N_HEADS, HEAD_DIM, HEAD_DIM), f32)
    state_qkv_conv = jax.random.normal(ks[5], (DEPTH, DEC_BATCH, QKV_CONV - 1, 3 * D_DELTA), f32)
    state_glu_conv = 0.5 * jax.random.normal(ks[6], (DEPTH, DEC_BATCH, CONF_KERNEL - 1, D_CONF), f32)
    norm_g = 1.0 + 0.02 * jax.random.normal(ks[7], (DEPTH, D), f32)
    w_ada = 0.5 * (D ** -0.5) * jax.random.normal(ks[8], (DEPTH, D, 3 * D), f32)
    ada_base = jnp.concatenate([jnp.zeros((2 * D,), f32), jnp.ones((D,), f32)])
    b_ada = ada_base + 0.02 * jax.random.normal(ks[9], (DEPTH, 3 * D), f32)
    w_in = (D ** -0.5) * jax.random.normal(ks[10], (DEPTH, D, D_IN_PROJ), f32)
    w_qkv_conv = (QKV_CONV ** -0.5) * jax.random.normal(ks[11], (DEPTH, QKV_CONV, 3 * D_DELTA), f32)
    a_log = jnp.log(jax.random.uniform(ks[12], (DEPTH, N_HEADS), f32, 1.0, 16.0))
    dt = jnp.exp(jax.random.uniform(ks[13], (DEPTH, N_HEADS), f32, math.log(1e-3), math.log(1e-1)))
    dt_bias = dt + jnp.log(-jnp.expm1(-dt))
    head_norm_g = 1.0 + 0.02 * jax.random.normal(ks[14], (DEPTH, HEAD_DIM), f32)
    w_dw = (CONF_KERNEL ** -0.5) * jax.random.normal(ks[15], (DEPTH, CONF_KERNEL, D_CONF), f32)
    b_dw = 0.02 * jax.random.normal(ks[16], (DEPTH, D_CONF), f32)
    ln_g = 1.0 + 0.02 * jax.random.normal(ks[17], (DEPTH, D_CONF), f32)
    ln_b = 0.02 * jax.random.normal(ks[18], (DEPTH, D_CONF), f32)
    w_out = (D ** -0.5) * jax.random.normal(ks[19], (DEPTH, D_DELTA + D_CONF, D), f32)
    final_g = 1.0 + 0.02 * jax.random.normal(ks[20], (D,), f32)
    return {"x_prompt": x_prompt, "x_sample": x_sample, "c_prompt": c_prompt, "c_sample": c_sample,
            "state_delta": state_delta, "state_qkv_conv": state_qkv_conv, "state_glu_conv": state_glu_conv,
            "norm_g": norm_g, "w_ada": w_ada, "b_ada": b_ada, "w_in": w_in, "w_qkv_conv": w_qkv_conv,
            "a_log": a_log, "dt_bias": dt_bias, "head_norm_g": head_norm_g, "w_dw": w_dw, "b_dw": b_dw,
            "ln_g": ln_g, "ln_b": ln_b, "w_out": w_out, "final_g": final_g}


def reference(x_prompt, x_sample, c_prompt, c_sample, state_delta, state_qkv_conv, state_glu_conv,
              norm_g, w_ada, b_ada, w_in, w_qkv_conv, a_log, dt_bias, head_norm_g, w_dw, b_dw,
              ln_g, ln_b, w_out, final_g):
    Bp = x_prompt.shape[0]
    hp, hs = x_prompt, x_sample
    Sp_l, qp_l, gp_l, Ss_l, qs_l, gs_l = [], [], [], [], [], []
    for l in range(DEPTH):
        params = (norm_g[l], w_ada[l], b_ada[l], w_in[l], w_qkv_conv[l], a_log[l], dt_bias[l],
                  head_norm_g[l], w_dw[l], b_dw[l], ln_g[l], ln_b[l], w_out[l])
        S0p = jnp.zeros((Bp, N_HEADS, HEAD_DIM, HEAD_DIM), jnp.float32)
        qb0 = jnp.zeros((Bp, QKV_CONV - 1, 3 * D_DELTA), x_prompt.dtype)
        gb0 = jnp.zeros((Bp, CONF_KERNEL - 1, D_CONF), x_prompt.dtype)
        hp, Sp, qp, gp = mixer_layer(hp, c_prompt, S0p, qb0, gb0, *params)
        hs, Ss, qs, gs = mixer_layer(hs, c_sample, state_delta[l], state_qkv_conv[l], state_glu_conv[l], *params)
        Sp_l.append(Sp); qp_l.append(qp); gp_l.append(gp)
        Ss_l.append(Ss); qs_l.append(qs); gs_l.append(gs)
    y_prompt = rms_norm(hp, final_g).astype(x_prompt.dtype)
    y_sample = rms_norm(hs, final_g).astype(x_sample.dtype)
    return (y_prompt, y_sample,
            jnp.stack(Sp_l), jnp.stack(qp_l), jnp.stack(gp_l),
            jnp.stack(Ss_l), jnp.stack(qs_l), jnp.stack(gs_l))
```

```python
import contextlib
import math
import numpy as np
import concourse.bass as bass
import concourse.mybir as mybir
from concourse.bass_utils import run_bass_kernel_spmd

F32 = mybir.dt.float32
BF16 = mybir.dt.bfloat16
ALU = mybir.AluOpType
AF = mybir.ActivationFunctionType
EPS = 1e-6
ENGS = ("pe", "act", "dve", "pool", "sp")


class Ins:
    __slots__ = ("eng", "fn", "deps", "sig", "cnt", "dma_key", "dma_gen")

    def __init__(self, eng, fn):
        self.eng = eng
        self.fn = fn
        self.deps = []
        self.sig = False
        self.cnt = 0
        self.dma_key = None
        self.dma_gen = 0


class Prog:
    def __init__(self, nc):
        self.nc = nc
        self.lists = {e: [] for e in ENGS}
        self.last_w = {}
        self.readers = {}
        self.dma_gen = {}
        self.all = []
        self.rec = None
        self.alias = {}

    PSUM_KEYS = frozenset(["pa0", "pa1", "ps2", "ps3", "pr0", "pr1", "pr2", "pr3"])

    def op(self, eng, fn, reads=(), writes=(), dma_key=None):
        if self.rec is not None:
            self.rec.append((eng, fn, tuple(reads), tuple(writes), dma_key))
            return None
        ins = Ins(eng, fn)
        reads = [self.alias.get(k, k) for k in reads]
        writes = [self.alias.get(k, k) for k in writes]
        writes = list(writes) + [k for k in reads if k in self.PSUM_KEYS and k not in writes]
        deps = []
        for k in reads:
            w = self.last_w.get(k)
            if w is not None:
                deps.append(w)
        for k in writes:
            w = self.last_w.get(k)
            if w is not None:
                deps.append(w)
            deps.extend(self.readers.get(k, {}).values())
        seen = set()
        for d in deps:
            if id(d) in seen:
                continue
            seen.add(id(d))
            if d.eng == "pe" and eng == "pe" and d.dma_key is None and dma_key is None:
                continue
            ins.deps.append(d)
        if dma_key is not None:
            g = self.dma_gen.get(dma_key, 0) + 1
            self.dma_gen[dma_key] = g
            ins.dma_key = dma_key
            ins.dma_gen = g
        slot = eng if dma_key is None else ("dma", dma_key)
        for k in reads:
            self.readers.setdefault(k, {})[slot] = ins
        for k in writes:
            self.last_w[k] = ins
            self.readers[k] = {}
        self.lists[eng].append(ins)
        self.all.append(ins)
        return ins

    def emit(self, final_wait_eng="sp"):
        nc = self.nc
        for ins in self.all:
            for d in ins.deps:
                if d.dma_key is None:
                    d.sig = True
        nsig = {}
        for e in ENGS:
            c = 0
            for ins in self.lists[e]:
                if ins.dma_key is None and ins.sig:
                    c += 1
                    ins.cnt = c
            nsig[e] = c
        dma_keys = sorted(self.dma_gen.keys(), key=str)
        with contextlib.ExitStack() as st:
            esem = {e: st.enter_context(nc.semaphore("s_" + e)) for e in ENGS}
            dsem = {k: st.enter_context(nc.semaphore("d%d" % i)) for i, k in enumerate(dma_keys)}
            block = st.enter_context(nc.Block())
            hw = {"pe": block.tensor, "act": block.scalar, "dve": block.vector,
                  "pool": block.gpsimd, "sp": block.sync}

            def make(e):
                def body(engine):
                    waited = {}
                    for ins in self.lists[e]:
                        for d in ins.deps:
                            if d.dma_key is not None:
                                s, v = dsem[d.dma_key], 16 * d.dma_gen
                            else:
                                s, v = esem[d.eng], d.cnt
                            if waited.get(id(s), 0) >= v:
                                continue
                            waited[id(s)] = v
                            engine.wait_ge(s, v)
                        r = ins.fn(engine)
                        if ins.dma_key is not None:
                            r.then_inc(dsem[ins.dma_key], 16)
                        elif ins.sig:
                            r.then_inc(esem[e], 1)
                    if e == final_wait_eng:
                        for k in dma_keys:
                            engine.wait_ge(dsem[k], 16 * self.dma_gen[k])
                        for e2 in ENGS:
                            if nsig[e2] and e2 != e:
                                engine.wait_ge(esem[e2], nsig[e2])
                return body

            for e in ENGS:
                if self.lists[e] or e == final_wait_eng:
                    hw[e](make(e))


class StopBuild(Exception):
    pass


class Cfg:
    dbg = 0

    def __init__(self, D=2048, T=2048, TB=512, L=2, wdt=BF16):
        self.D, self.T, self.TB, self.L = D, T, TB, L
        self.KT = D // 128
        self.H = D // 256
        self.DD = D // 2
        self.DC = D // 2
        self.CT = self.DC // 128
        self.NIN = 4 * self.DD + 2 * self.H + 3 * self.DC
        self.NB = T // TB
        self.NS = 16
        self.TS = 8
        self.wdt = wdt


def build(cfg):
    D, T, TB, L, KT, H, DD, DC, CT, NIN, NB = (cfg.D, cfg.T, cfg.TB, cfg.L, cfg.KT, cfg.H, cfg.DD,
                                                cfg.DC, cfg.CT, cfg.NIN, cfg.NB)
    WDT = cfg.wdt
    NS, TS = cfg.NS, cfg.TS
    NTB = TB // 128
    H2 = 2 * H
    QSC = 128.0 ** -0.5
    nc = bass.Bass("TRN2", target_bir_lowering=False)

    def din(name, shape):
        return nc.dram_tensor(name, list(shape), F32, kind="ExternalInput").ap()

    def dout(name, shape):
        return nc.dram_tensor(name, list(shape), F32, kind="ExternalOutput").ap()

    xp = din("xp", [T, D])
    xs = din("xs", [128, D])
    cc = din("cc", [1 + NS, D])
    sd = din("sd", [L, NS, H, 128, 128])
    sq = din("sq", [L, NS * 3, 3 * DD])
    sg = din("sg", [L, NS * 30, DC])
    norm_g = din("norm_g", [L * KT, 128])
    w_ada = din("w_ada", [L, D, 3 * D])
    b_ada = din("b_ada", [L * 3 * KT, 128])
    w_in = din("w_in", [L, D, NIN])
    w_qc = din("w_qc", [L * 4 * 3 * H, 128])
    a_log = din("a_log", [1, L * H])
    dt_bias = din("dt_bias", [1, L * H])
    hn_g = din("hn_g", [L, 128])
    w_dw = din("w_dw", [L * 31 * CT, 128])
    b_dw = din("b_dw", [L * CT, 128])
    ln_g = din("ln_g", [L * CT, 128])
    ln_b = din("ln_b", [L * CT, 128])
    w_out = din("w_out", [L, D, D])
    final_g = din("final_g", [KT, 128])

    yp = dout("yp", [T, D])
    ys = dout("ys", [128, D])
    oSp = dout("oSp", [L, H, 128, 128])
    oqp = dout("oqp", [L, 3, 3 * DD])
    ogp = dout("ogp", [L, 30, DC])
    oSs = dout("oSs", [L, NS, H, 128, 128])
    oqs = dout("oqs", [L, NS * 3, 3 * DD])
    ogs = dout("ogs", [L, NS * 30, DC])

    st = contextlib.ExitStack()
    P = Prog(nc)

    def sb(name, shape, dt=F32):
        return st.enter_context(nc.sbuf_tensor(name, list(shape), dt))

    def psb(name):
        return st.enter_context(nc.psum_tensor(name, [128, 512], F32))

    def mm(out, lhsT, rhs, start, stop, r, w, skip=False):
        return P.op("pe", lambda e: e.matmul(out, lhsT=lhsT, rhs=rhs, start=start, stop=stop,
                                             skip_group_check=skip), r, w)

    def act(out, in_, func, r, w, scale=1.0, bias=None):
        if bias is None:
            return P.op("act", lambda e: e.activation(out=out, in_=in_, func=func, scale=scale), r, w)
        return P.op("act", lambda e: e.activation(out=out, in_=in_, func=func, scale=scale, bias=bias), r, w)

    def ts(eng, out, in0, s1, s2, op0, op1, r, w):
        if op1 is None:
            return P.op(eng, lambda e: e.tensor_scalar(out=out, in0=in0, scalar1=s1, scalar2=None, op0=op0), r, w)
        return P.op(eng, lambda e: e.tensor_scalar(out=out, in0=in0, scalar1=s1, scalar2=s2, op0=op0, op1=op1), r, w)

    def tt(eng, out, in0, in1, op, r, w):
        return P.op(eng, lambda e: e.tensor_tensor(out=out, in0=in0, in1=in1, op=op), r, w)

    def stt(out, in0, scalar, in1, op0, op1, r, w):
        return P.op("dve", lambda e: e.scalar_tensor_tensor(out=out, in0=in0, scalar=scalar, in1=in1,
                                                            op0=op0, op1=op1), r, w)

    def cp(eng, out, in_, r, w):
        if eng == "act":
            return P.op("act", lambda e: e.copy(out=out, in_=in_), r, w)
        return P.op(eng, lambda e: e.tensor_copy(out=out, in_=in_), r, w)

    def dma(eng, out, in_, r, w, key):
        return P.op(eng, lambda e: e.dma_start(out=out, in_=in_), r, w, dma_key=key)

    def recip(out, in_, r, w):
        return P.op("dve", lambda e: e.reciprocal(out=out, in_=in_), r, w)

    def asel(out, in_, pattern, op, fill, base, cm, r, w):
        return P.op("pool", lambda e: e.affine_select(out=out, in_=in_, pattern=pattern, compare_op=op,
                                                      fill=fill, base=base, channel_multiplier=cm), r, w)

    def memset(eng, out, val, w):
        return P.op(eng, lambda e: e.memset(out, val), (), w)

    PA = [psb("pa0"), psb("pa1")]
    PS2 = psb("ps2")
    PS3 = psb("ps3")
    PR = [psb("pr%d" % i) for i in range(4)]
    pa_i = [0]
    pr_i = [0]

    pa_front = [False]

    def nextpa():
        if pa_front[0]:
            return PA[0], "pa0"
        pa_i[0] ^= 1
        t = PA[pa_i[0]]
        return t, "pa%d" % pa_i[0]

    def nextpr():
        pr_i[0] = (pr_i[0] + 1) % 4
        return PR[pr_i[0]], "pr%d" % pr_i[0]

    ident = sb("ident", [128, 128])
    cst = sb("cst", [128, 4])
    onesD = sb("onesD", [128, 128])
    onesC = sb("onesC", [128, 128])
    onesH = sb("onesH", [128, 128])
    ones1 = sb("ones1", [128, 128])
    EH = sb("EH", [128, H2])
    memset("pool", ident[:], 0.0, ["ident"])
    asel(ident[:], ident[:], [[-1, 128]], ALU.not_equal, 1.0, 0, 1, ["ident"], ["ident"])
    memset("pool", cst[:, 0:1], 1.0, ["cst"])
    memset("pool", cst[:, 1:2], EPS, ["cst"])
    memset("pool", cst[:, 2:3], 0.0, ["cst"])
    memset("pool", cst[:, 3:4], -1.0, ["cst"])
    memset("pool", onesD[:], 1.0 / D, ["onesD"])
    memset("pool", onesC[:], 1.0 / DC, ["onesC"])
    memset("pool", onesH[:], 1.0 / 128, ["onesH"])
    memset("pool", ones1[:], 1.0, ["ones1"])
    memset("pool", EH[:], 0.0, ["EH"])
    asel(EH[:], EH[:], [[-1, H2]], ALU.not_equal, 1.0, 0, 1, ["EH"], ["EH"])

    class Grp:
        pass

    def make_masks(name, seglen):
        nseg = 128 // seglen
        g = Grp()
        g.seglen, g.nseg = seglen, nseg
        g.McT = sb(name + "McT", [128, 128])
        g.MsT = sb(name + "MsT", [128, 128])
        g.SegAll = sb(name + "Seg", [128, 128])
        g.rowm = sb(name + "rowm", [128, nseg])
        g.key = name + "masks"
        k = [g.key]
        for m, off in ((g.McT, 0), (g.MsT, -1)):
            memset("pool", m[:], 1.0, k)
            asel(m[:], m[:], [[1, 128]], ALU.is_ge, 0.0, off, -1, k, k)
            v = m[:].rearrange("p (a b) -> p a b", b=seglen)
            asel(v, v, [[-seglen, nseg], [0, seglen]], ALU.is_ge, 0.0, 0, 1, k, k)
        memset("pool", g.SegAll[:], 1.0, k)
        v = g.SegAll[:].rearrange("p (a b) -> p a b", b=seglen)
        asel(v, v, [[-seglen, nseg], [0, seglen]], ALU.is_ge, 0.0, 0, 1, k, k)
        asel(v, v, [[seglen, nseg], [0, seglen]], ALU.is_ge, 0.0, seglen - 1, -1, k, k)
        memset("pool", g.rowm[:], 1.0, k)
        asel(g.rowm[:], g.rowm[:], [[-seglen, nseg]], ALU.is_ge, 0.0, 0, 1, k, k)
        asel(g.rowm[:], g.rowm[:], [[seglen, nseg]], ALU.is_ge, 0.0, seglen - 1, -1, k, k)
        return g

    GP = make_masks("p", 64)
    GS = make_masks("s", 8)

    stage = sb("stage", [128, 128])
    memset("pool", stage[:], 0.0, ["stage"])

    def load_cols(name, src, nrows):
        dst = sb(name, [128, nrows])
        for r0 in range(0, nrows, 128):
            n = min(128, nrows - r0)
            dma("sp", stage[0:n, :], src[r0:r0 + n, :], [], ["stage"], "stage")
            pt, pk = nextpr()
            P.op("pe", lambda e, pt=pt: e.transpose(out=pt[:, 0:128], in_=stage[:, :], identity=ident[:]),
                 ["stage", "ident"], [pk])
            cp("dve", dst[:, r0:r0 + n], pt[:, 0:n], [pk], [name])
        return dst

    ng = load_cols("ng", norm_g, L * KT)
    bada = load_cols("bada", b_ada, L * 3 * KT)
    wqc = load_cols("wqc", w_qc, L * 4 * 3 * H)
    hng = load_cols("hng", hn_g, L)
    wdw = load_cols("wdw", w_dw, L * 31 * CT)
    bdw = load_cols("bdw", b_dw, L * CT)
    lng = load_cols("lng", ln_g, L * CT)
    lnb = load_cols("lnb", ln_b, L * CT)
    fg = load_cols("fg", final_g, KT)

    nea = sb("nea", [128, L * H])
    dtb = sb("dtb", [128, L * H])
    dma("sp", nea[:], a_log.partition_broadcast(128), [], ["nea"], "nea")
    dma("sp", dtb[:], dt_bias.partition_broadcast(128), [], ["dtb"], "dtb")
    act(nea[:], nea[:], AF.Exp, ["nea"], ["nea"])
    ts("dve", nea[:], nea[:], -1.0, None, ALU.mult, None, ["nea"], ["nea"])

    NWS = 2
    wsl = [sb("wsl%d" % i, [128, KT, 128], WDT) for i in range(NWS)]
    ws_i = [0]

    def load_w(src3):
        ws_i[0] = (ws_i[0] + 1) % NWS
        i = ws_i[0]
        key = "wsl%d" % i
        dma("pool", wsl[i][:], src3, [], [key], key)
        return wsl[i], key

    def w_in_tile(l, col0):
        return w_in[l, :, col0:col0 + 128].rearrange("(k p) c -> p k c", p=128)

    NC17 = 1 + NS
    xio = sb("xio", [128, D])
    ccs = xio[0:NC17, :]
    memset("pool", xio[:], 0.0, ["xio"])
    scT = sb("scT", [128, KT, NC17], WDT)
    Amod = sb("Amod", [128, L, KT, NC17])
    Bmod = sb("Bmod", [128, L, KT, NC17])
    Gmod = sb("Gmod", [128, L, KT, NC17])
    dma("sp", ccs, cc, [], ["xio"], "xio")
    act(ccs, ccs, AF.Silu, ["xio"], ["xio"])
    for k in range(KT):
        pt, pk = nextpr()
        P.op("pe", lambda e, pt=pt, k=k: e.transpose(out=pt[:, 0:128], in_=xio[:, k * 128:(k + 1) * 128],
                                                     identity=ident[:]), ["xio", "ident"], [pk])
        cp("dve", scT[:, k, :], pt[:, 0:NC17], [pk], ["scT"])
    for l in range(L):
        for j in range(3 * KT):
            wt, wk = load_w(w_ada[l, :, j * 128:(j + 1) * 128].rearrange("(k p) c -> p k c", p=128))
            pt, pk = nextpr()
            for k in range(KT):
                mm(pt[:, 0:NC17], wt[:, k, :], scT[:, k, :], k == 0, k == KT - 1, [wk, "scT"], [pk])
            bcol = bada[:, l * 3 * KT + j: l * 3 * KT + j + 1]
            if j < KT:
                ts("dve", Bmod[:, l, j, :], pt[:, 0:NC17], bcol, None, ALU.add, None, [pk, "bada"], ["Bmod"])
            elif j < 2 * KT:
                ts("dve", Amod[:, l, j - KT, :], pt[:, 0:NC17], bcol, 1.0, ALU.add, ALU.add, [pk, "bada"], ["Amod"])
                ts("dve", Amod[:, l, j - KT, :], Amod[:, l, j - KT, :], ng[:, l * KT + j - KT: l * KT + j - KT + 1],
                   None, ALU.mult, None, ["Amod", "ng"], ["Amod"])
            else:
                ts("dve", Gmod[:, l, j - 2 * KT, :], pt[:, 0:NC17], bcol, None, ALU.add, None, [pk, "bada"], ["Gmod"])

    cur_blk = [0]

    def ckpt(n, cond=True):
        if cfg.dbg == n and cond and cur_blk[0] == getattr(cfg, 'dbg_block', 0):
            raise StopBuild()

    Sp = sb("Sp", [128, L, H, 128])
    Stmp = sb("Stmp", [128, 128])
    qtail = sb("qtail", [128, L, 3 * H, 3])
    gtail = sb("gtail", [128, L, CT, 30])
    memset("pool", Sp[:], 0.0, ["Sp"])
    memset("pool", qtail[:], 0.0, ["qtail"])
    memset("pool", gtail[:], 0.0, ["gtail"])

    xTf = sb("xTf", [128, KT * TB])
    ALIAS = KT * (TB - 128) >= 3 * NS * 128
    hT = sb("hT", [128, KT, TB], WDT)
    mixT = sb("mixT", [128, KT, TB], WDT)
    cv = sb("cv", [128, CT, TB])
    t512 = [sb("t512_%d" % i, [128, max(TB, 512)]) for i in range(2)]
    for i in range(2):
        memset("pool", t512[i][:], 0.0, ["t512_%d" % i])
    t5_i = [0]

    def tmp512():
        t5_i[0] = (t5_i[0] + 1) % 2
        return t512[t5_i[0]], "t512_%d" % t5_i[0]

    rs = sb("rs", [128, TB])
    wab = sb("wab", [128, KT, H2], WDT)
    graw = sb("graw", [128, NTB, H])
    gbt = sb("gbt", [128, NTB, 128])
    memset("pool", gbt[:], 0.0, ["gbt"])
    e2t = sb("e2t", [128, NTB, H])
    e2mf = sb("e2m", [128, max(NTB * 2, 16) * H])
    gtmp = sb("gtmp", [128, H2])
    gbT = sb("gbT", [128, NTB, 128])
    memset("pool", gbT[:], 0.0, ["gbT"])
    XW = max(3 + TB, NS * (3 + TS))
    xpre = sb("xpre", [128, 3, XW])
    qkvas = [sb("qkva%d" % i, [128, 3, TB]) for i in range(2)]
    zss = [sb("zs%d" % i, [128, TB]) for i in range(2)]
    knqss = [sb("knqs%d" % i, [128, NTB, 256]) for i in range(2)]
    oT = sb("oT", [128, TB])
    UW = max(30 + TB, NS * (30 + TS))
    ubuf = sb("ubuf", [128, UW])
    mu = sb("mu", [128, TB])
    rsc = sb("rsc", [128, TB])
    names = ["eg", "bb", "qG", "kgT", "dd", "decT", "dmC", "dmS", "attnT", "LTp", "A0", "A1", "AT0", "AT1",
             "PT", "kgtok", "vtok", "Wn", "UT", "Utok", "UT0"]
    shared_t = ("UT", "Utok", "UT0")
    NSET = 4
    own = ["eg", "bb", "qG", "kgT", "dd", "dmC", "dmS", "A0", "A1", "AT1", "PT", "kgtok", "vtok"]
    ali = {"decT": "dd", "attnT": "dmC", "LTp": "dmS", "AT0": "dmS", "Wn": "kgT"}
    shared_tl = {n: sb(n, [128, 128]) for n in shared_t}
    tls = []
    for si in range(NSET):
        d = {n: sb("%s%d" % (n, si), [128, 128]) for n in own}
        for k_, v_ in ali.items():
            d[k_] = d[v_]
            P.alias["%s%d" % (k_, si)] = "%s%d" % (v_, si)
        d.update(shared_tl)
        tls.append(d)
    tl = tls[0]
    kdec2 = sb("kdec2", [128, 2, 128])
    if ALIAS:
        o0 = KT * 128
        Sin = xTf[:, o0:o0 + NS * 128].rearrange("p (s v) -> p s v", s=NS)
        Sout = xTf[:, o0 + NS * 128:o0 + 2 * NS * 128].rearrange("p (s v) -> p s v", s=NS)
        kdec16 = xTf[:, o0 + 2 * NS * 128:o0 + 3 * NS * 128].rearrange("p (s v) -> p s v", s=NS)
    else:
        Sin = sb("Sin", [128, NS, 128])[:]
        Sout = sb("Sout", [128, NS, 128])[:]
        kdec16 = sb("kdec16", [128, NS, 128])[:]
    st48 = sb("st48", [128, 3, 128])
    memset("pool", st48[:], 0.0, ["st48"])
    st120 = sb("st120", [128, 4, 128])
    memset("pool", st120[:], 0.0, ["st120"])
    memset("pool", tl["UT"][:], 0.0, ["UT"])
    memset("pool", tl["Utok"][:], 0.0, ["Utok"])

    def v3(ap, nseq):
        return ap.rearrange("p (s t) -> p s t", s=nseq)

    def do_block(is_sample, bi):
        if is_sample:
            ntok, nseq, Tq, grp, c0, ncol = 128, NS, TS, GS, 1, NS
            xsrc, ydst = xs, ys
        else:
            ntok, nseq, Tq, grp, c0, ncol = TB, 1, TB, GP, 0, 1
            xsrc, ydst = xp[bi * TB:(bi + 1) * TB, :], yp[bi * TB:(bi + 1) * TB, :]
        last = is_sample or bi == NB - 1
        cur_blk[0] = NB if is_sample else bi
        nt = ntok // 128
        seglen, nseg = grp.seglen, grp.nseg
        nd = int(math.log2(seglen)) - 1
        MK = [grp.key]
        e2m = e2mf[:, 0:nt * nseg * H].rearrange("p (t i h) -> p t i h", t=nt, i=nseg)
        xT = xTf[:, 0:KT * ntok].rearrange("p (k t) -> p k t", k=KT)
        if is_sample and ALIAS:
            memset("pool", xTf[:, KT * 128:KT * 128 + 1], 0.0, ["xT", "Sin", "Sout", "kdec16"])

        def bc(modt, l, k):
            return modt[:, l, k, c0:c0 + ncol].unsqueeze(2).broadcast_to([128, nseq, Tq])

        for tt_ in range(nt):
            dma("sp", xio[:], xsrc[tt_ * 128:(tt_ + 1) * 128, :], [], ["xio"], "xio")
            for k4 in range(0, KT, 4):
                n4 = min(4, KT - k4)
                pt, pk = nextpa()
                for q in range(n4):
                    P.op("pe", lambda e, pt=pt, q=q, k4=k4: e.transpose(
                        out=pt[:, q * 128:(q + 1) * 128], in_=xio[:, (k4 + q) * 128:(k4 + q + 1) * 128],
                        identity=ident[:]), ["xio", "ident"], [pk])
                cp("act", xT[:, k4:k4 + n4, tt_ * 128:(tt_ + 1) * 128],
                   pt[:, 0:n4 * 128].rearrange("p (a b) -> p a b", b=128), [pk], ["xT"])

        def rms_stats(dst):
            for k in range(KT):
                tq, tk = tmp512()
                act(tq[:, 0:ntok], xT[:, k, 0:ntok], AF.Square, ["xT"], [tk])
                mm(PS2[:, 0:ntok], onesD[:], tq[:, 0:ntok], k == 0, k == KT - 1, [tk, "onesD"], ["ps2"])
            act(dst[:, 0:ntok], PS2[:, 0:ntok], AF.Sqrt, ["ps2", "cst"], ["rs"], bias=cst[:, 1:2])
            recip(dst[:, 0:ntok], dst[:, 0:ntok], ["rs"], ["rs"])

        for l in range(L):
            rms_stats(rs)
            for k in range(KT):
                tq, tk = tmp512()
                if nseq == 1:
                    stt(tq[:, 0:ntok], xT[:, k, 0:ntok], Amod[:, l, k, 0:1], rs[:, 0:ntok], ALU.mult, ALU.mult,
                        ["xT", "Amod", "rs"], [tk])
                    ts("dve", hT[:, k, 0:ntok], tq[:, 0:ntok], Bmod[:, l, k, 0:1], None, ALU.add, None,
                       [tk, "Bmod"], ["hT"])
                else:
                    tt("dve", tq[:, 0:ntok], xT[:, k, 0:ntok], rs[:, 0:ntok], ALU.mult, ["xT", "rs"], [tk])
                    tt("dve", v3(tq[:, 0:ntok], nseq), v3(tq[:, 0:ntok], nseq), bc(Amod, l, k), ALU.mult,
                       [tk, "Amod"], [tk])
                    tt("dve", v3(hT[:, k, 0:ntok], nseq), v3(tq[:, 0:ntok], nseq), bc(Bmod, l, k), ALU.add,
                       [tk, "Bmod"], ["hT"])

            dma("pool", wab[:], w_in[l, :, 4 * DD:4 * DD + H2].rearrange("(k p) c -> p k c", p=128),
                [], ["wab"], "wab")
            for tt_ in range(nt):
                pt, pk = nextpr()
                for k in range(KT):
                    mm(pt[:, 0:H2], hT[:, k, tt_ * 128:(tt_ + 1) * 128], wab[:, k, :], k == 0, k == KT - 1,
                       ["hT", "wab"], [pk])
                act(gbt[:, tt_, H:H2], pt[:, 0:H], AF.Sigmoid, [pk], ["gbt"])
                tt("dve", gtmp[:, 0:H], pt[:, H:H2], dtb[:, l * H:(l + 1) * H], ALU.add, [pk, "dtb"], ["gtmp"])
                act(gtmp[:, 0:H], gtmp[:, 0:H], AF.Exp, ["gtmp"], ["gtmp"])
                act(gtmp[:, 0:H], gtmp[:, 0:H], AF.Ln, ["gtmp", "cst"], ["gtmp"], bias=cst[:, 0:1])
                tt("dve", graw[:, tt_, :], gtmp[:, 0:H], nea[:, l * H:(l + 1) * H], ALU.mult, ["gtmp", "nea"], ["graw"])
                pt, pk = nextpr()
                mm(pt[:, 0:H], grp.McT[:], graw[:, tt_, :], True, True, MK + ["graw"], [pk])
                mm(pt[:, H:H2], grp.SegAll[:], graw[:, tt_, :], True, True, MK + ["graw"], [pk])
                cp("act", gbt[:, tt_, 0:H], pt[:, 0:H], [pk], ["gbt"])
                tt("dve", gtmp[:, H:H2], pt[:, H:H2], gbt[:, tt_, 0:H], ALU.subtract, [pk, "gbt"], ["gtmp"])
                act(e2t[:, tt_, :], gtmp[:, H:H2], AF.Exp, ["gtmp"], ["e2t"])
                tt("dve", e2m[:, tt_, 0:nseg, :],
                   e2t[:, tt_, :].unsqueeze(1).broadcast_to([128, nseg, H]),
                   grp.rowm[:].unsqueeze(2).broadcast_to([128, nseg, H]), ALU.mult, ["e2t"] + MK, ["e2m"])
                pt, pk = nextpr()
                P.op("pe", lambda e, pt=pt, tt_=tt_: e.transpose(out=pt[:, 0:128], in_=gbt[:, tt_, :],
                                                                identity=ident[:]), ["gbt", "ident"], [pk])
                cp("dve", gbT[0:H2, tt_, :], pt[0:H2, 0:128], [pk], ["gbT"])

            US = 30 + Tq
            uv = ubuf[:, 0:nseq * US].rearrange("p (s t) -> p s t", s=nseq)

            def conf_part1(j):
                if is_sample:
                    dma("sp", st120[0:120, :, :], sg[l, :, j * 128:(j + 1) * 128].rearrange("(g r) x -> r g x", r=120),
                        [], ["st120"], "st120")
                    pt, pk = PS2, "ps2"
                    for g4 in range(4):
                        P.op("pe", lambda e, pt=pt, g4=g4: e.transpose(out=pt[:, g4 * 128:(g4 + 1) * 128], in_=st120[:, g4, :],
                                                                       identity=ident[:]), ["st120", "ident"], [pk])
                    cp("act", uv.rearrange("p (g s) t -> p g s t", g=4)[:, :, :, 0:30], pt[:, 0:512].rearrange("p (g x) -> p g x", g=4)[:, :, 0:120].rearrange("p g (s r) -> p g s r", r=30), [pk], ["ubuf"])
                else:
                    cp("pool", uv[:, 0, 0:30], gtail[:, l, j, :], ["gtail"], ["ubuf"])
                wb_, wbk = load_w(w_in_tile(l, 4 * DD + H2 + DC + j * 128))
                pb_, pbk = nextpa()
                for k in range(KT):
                    mm(pb_[:, 0:ntok], wb_[:, k, :], hT[:, k, 0:ntok], k == 0, k == KT - 1, [wbk, "hT"], [pbk])
                tq, tk = tmp512()
                act(tq[:, 0:ntok], pb_[:, 0:ntok], AF.Sigmoid, [pbk], [tk])
                wa, wak = load_w(w_in_tile(l, 4 * DD + H2 + j * 128))
                pa_, pak = nextpa()
                for k in range(KT):
                    mm(pa_[:, 0:ntok], wa[:, k, :], hT[:, k, 0:ntok], k == 0, k == KT - 1, [wak, "hT"], [pak])
                tt("dve", uv[:, :, 30:30 + Tq], v3(pa_[:, 0:ntok], nseq), v3(tq[:, 0:ntok], nseq), ALU.mult,
                   [pak, tk], ["ubuf"])
                if not is_sample:
                    cp("pool", gtail[:, l, j, :], uv[:, 0, Tq:Tq + 30], ["ubuf"], ["gtail"])
                if last:
                    nr = nseq * 30
                    tq2, tk2 = tmp512()
                    cp("pool", tq2[:, 0:nr].rearrange("p (s r) -> p s r", r=30), uv[:, :, Tq:Tq + 30], ["ubuf"], [tk2])
                    if is_sample:
                        pt, pk = PS2, "ps2"
                        for g4 in range(4):
                            P.op("pe", lambda e, pt=pt, g4=g4, tq2=tq2: e.transpose(
                                out=pt[:, g4 * 128:(g4 + 1) * 128], in_=tq2[:, g4 * 120:g4 * 120 + 128],
                                identity=ident[:]), [tk2, "ident"], [pk])
                        cp("act", st120[0:120, :, :], pt[0:120, 0:512].rearrange("p (g x) -> p g x", g=4), [pk], ["st120"])
                        dma("sp", ogs[l, :, j * 128:(j + 1) * 128].rearrange("(g r) x -> r g x", r=120), st120[0:120, :, :],
                            ["st120"], [], "st120")
                    else:
                        pt, pk = PS2, "ps2"
                        P.op("pe", lambda e, pt=pt, tq2=tq2: e.transpose(out=pt[:, 0:128], in_=tq2[:, 0:128], identity=ident[:]),
                             [tk2, "ident"], [pk])
                        cp("act", st120[0:30, 0, :], pt[0:30, 0:128], [pk], ["st120"])
                        dma("sp", ogp[l, :, j * 128:(j + 1) * 128], st120[0:30, 0, :], ["st120"], [], "st120")
                ov = v3(cv[:, j, 0:ntok], nseq)
                for q in range(31):
                    wcol = wdw[:, (l * 31 + q) * CT + j:(l * 31 + q) * CT + j + 1]
                    if q == 0:
                        ts("dve", ov, uv[:, :, 0:Tq], wcol, bdw[:, l * CT + j:l * CT + j + 1], ALU.mult, ALU.add,
                           ["ubuf", "wdw", "bdw"], ["cv"])
                    else:
                        stt(ov, uv[:, :, q:q + Tq], wcol, ov, ALU.mult, ALU.add, ["ubuf", "wdw", "cv"], ["cv"])
                pst, pstk = nextpa()
                mm(pst[:, 0:ntok], onesC[:], cv[:, j, 0:ntok], True, True, ["cv", "onesC"], [pstk])
                if j == 0:
                    cp("act", mu[:, 0:ntok], pst[:, 0:ntok], [pstk], ["mu"])
                else:
                    tt("dve", mu[:, 0:ntok], mu[:, 0:ntok], pst[:, 0:ntok], ALU.add, ["mu", pstk], ["mu"])
                tq, tk = tmp512()
                act(tq[:, 0:ntok], cv[:, j, 0:ntok], AF.Square, ["cv"], [tk])
                pst, pstk = nextpa()
                mm(pst[:, 0:ntok], onesC[:], tq[:, 0:ntok], True, True, [tk, "onesC"], [pstk])
                if j == 0:
                    cp("act", rsc[:, 0:ntok], pst[:, 0:ntok], [pstk], ["rsc"])
                else:
                    tt("dve", rsc[:, 0:ntok], rsc[:, 0:ntok], pst[:, 0:ntok], ALU.add, ["rsc", pstk], ["rsc"])

            def head_front(h):
                hb = h % 2
                qkva, knqs, zs = qkvas[hb], knqss[hb], zss[hb]
                QK, KQ, ZK = "qkva%d" % hb, "knqs%d" % hb, "zs%d" % hb
                XS = 3 + Tq
                xv = xpre[:, :, 0:nseq * XS].rearrange("p c (s t) -> p c s t", s=nseq)
                if is_sample:
                    dma("sp", st48[0:NS * 3, :, :], sq[l, :, :].rearrange("r (c x) -> r c x", c=3)[:, :, h * 128:(h + 1) * 128],
                        [], ["st48"], "st48")
                    pt, pk = PS2, "ps2"
                    for c in range(3):
                        P.op("pe", lambda e, pt=pt, c=c: e.transpose(out=pt[:, c * 128:(c + 1) * 128], in_=st48[:, c, :],
                                                                     identity=ident[:]), ["st48", "ident"], [pk])
                    for c in range(3):
                        cp("act", xv[:, c, :, 0:3], pt[:, c * 128:c * 128 + 48].rearrange("p (s r) -> p s r", r=3),
                           [pk], ["xpre"])
                else:
                    for c in range(3):
                        cp("pool", xv[:, c, 0, 0:3], qtail[:, l, c * H + h, :], ["qtail"], ["xpre"])
                for c in range(3):
                    wt, wk = load_w(w_in_tile(l, c * DD + h * 128))
                    pt, pk = nextpa()
                    for k in range(KT):
                        mm(pt[:, 0:ntok], wt[:, k, :], hT[:, k, 0:ntok], k == 0, k == KT - 1, [wk, "hT"], [pk])
                    cp("act", xv[:, c, :, 3:3 + Tq], v3(pt[:, 0:ntok], nseq), [pk], ["xpre"])
                    if not is_sample:
                        cp("pool", qtail[:, l, c * H + h, :], xv[:, c, 0, Tq:Tq + 3], ["xpre"], ["qtail"])
                if last:
                    nr = nseq * 3
                    pt, pk = PS2, "ps2"
                    for c in range(3):
                        tq, tk = tmp512()
                        cp("pool", tq[:, 0:nr].rearrange("p (s r) -> p s r", r=3), xv[:, c, :, Tq:Tq + 3], ["xpre"], [tk])
                        P.op("pe", lambda e, pt=pt, c=c, tq=tq: e.transpose(
                            out=pt[:, c * 128:(c + 1) * 128], in_=tq[:, 0:128], identity=ident[:]), [tk, "ident"], [pk])
                    cp("act", st48[0:nr, :, :], pt[0:nr, 0:384].rearrange("p (c x) -> p c x", c=3), [pk], ["st48"])
                    od = (oqs if is_sample else oqp)[l, :, :].rearrange("r (c x) -> r c x", c=3)[:, :, h * 128:(h + 1) * 128]
                    dma("sp", od, st48[0:nr, :, :], ["st48"], [], "st48")
                wt, wk = load_w(w_in_tile(l, 3 * DD + h * 128))
                pt, pk = nextpa()
                for k in range(KT):
                    mm(pt[:, 0:ntok], wt[:, k, :], hT[:, k, 0:ntok], k == 0, k == KT - 1, [wk, "hT"], [pk])
                act(zs[:, 0:ntok], pt[:, 0:ntok], AF.Silu, [pk], [ZK])
                for c in range(3):
                    ct = c * H + h
                    ov = v3(qkva[:, c, 0:ntok], nseq)
                    for j in range(4):
                        wcol = wqc[:, (l * 4 + j) * 3 * H + ct:(l * 4 + j) * 3 * H + ct + 1]
                        if j == 0:
                            ts("dve", ov, xv[:, c, :, 0:Tq], wcol, None, ALU.mult, None, ["xpre", "wqc"], [QK])
                        else:
                            stt(ov, xv[:, c, :, j:j + Tq], wcol, ov, ALU.mult, ALU.add, ["xpre", "wqc", QK], [QK])
                    act(qkva[:, c, 0:ntok], qkva[:, c, 0:ntok], AF.Silu, [QK], [QK])
                for c in (1, 0):
                    tq, tk = tmp512()
                    act(tq[:, 0:ntok], qkva[:, c, 0:ntok], AF.Square, [QK], [tk])
                    mm(PS2[:, 0:ntok], ones1[:], tq[:, 0:ntok], True, True, [tk, "ones1"], ["ps2"])
                    act(tq[:, 0:ntok], PS2[:, 0:ntok], AF.Sqrt, ["ps2", "cst"], [tk], bias=cst[:, 1:2])
                    recip(tq[:, 0:ntok], tq[:, 0:ntok], [tk], [tk])
                    src = qkva[:, c, 0:ntok].rearrange("p (a b) -> p a b", b=128)
                    rv = tq[:, 0:ntok].rearrange("p (a b) -> p a b", b=128)
                    if c == 1:
                        tt("dve", knqs[:, 0:nt, 0:128], src, rv, ALU.mult, [QK, tk], [KQ])
                    else:
                        stt(knqs[:, 0:nt, 128:256], src, QSC, rv, ALU.mult, ALU.mult, [QK, tk], [KQ])
                if is_sample:
                    dma("sp", Sin, sd[l, :, h, :, :].rearrange("s p v -> p s v"), [], ["Sin"], "Sin")


            def tile_solve(h, tt_, sid):
                hb = h % 2
                qkva, knqs, zs = qkvas[hb], knqss[hb], zss[hb]
                QK, KQ, ZK = "qkva%d" % hb, "knqs%d" % hb, "zs%d" % hb
                tl, SS = tls[sid], str(sid)
                kdecv, kdk = (kdec16, "kdec16") if is_sample else (kdec2[:], "kdec2")
                PB, PBK = PR[sid], "pr%d" % sid

                def slot(i, n=1):
                    return PB[:, i * 128:(i + n) * 128]
                tsl = slice(tt_ * 128, (tt_ + 1) * 128)

                tsl = slice(tt_ * 128, (tt_ + 1) * 128)
                gccol = gbt[:, tt_, h:h + 1]
                becol = gbt[:, tt_, H + h:H + h + 1]
                pa_, pak = slot(0, 2), PBK
                mm(pa_[:, 0:128], EH[:, h:h + 1].broadcast_to([128, 128]), gbT[:, tt_, :], True, True, ["EH", "gbT"], [pak])
                mm(pa_[:, 128:256], EH[:, H + h:H + h + 1].broadcast_to([128, 128]), gbT[:, tt_, :], True, True, ["EH", "gbT"], [pak])
                act(tl["eg"][:], pa_[:, 0:128], AF.Exp, [pak], [("eg" + SS)])
                cp("act", tl["bb"][:], pa_[:, 128:256], [pak], [("bb" + SS)])
                ts("dve", tl["dd"][:], pa_[:, 0:128], gccol, 0.0, ALU.subtract, ALU.min, [pak, "gbt"], [("dd" + SS)])
                act(tl["decT"][:], tl["dd"][:], AF.Exp, [("dd" + SS)], [("decT" + SS)])
                tt("pool", tl["dmC"][:], tl["decT"][:], grp.McT[:], ALU.mult, [("decT" + SS)] + MK, [("dmC" + SS)])
                tt("pool", tl["dmS"][:], tl["decT"][:], grp.MsT[:], ALU.mult, [("decT" + SS)] + MK, [("dmS" + SS)])
                tt("dve", tl["qG"][:], knqs[:, tt_, 128:256], tl["eg"][:], ALU.mult, [KQ, ("eg" + SS)], [("qG" + SS)])
                tt("dve", tl["kgT"][:], knqs[:, tt_, 0:128], tl["eg"][:], ALU.mult, [KQ, ("eg" + SS)], [("kgT" + SS)])
                pb_, pbk = slot(2, 2), PBK
                mm(pb_[:, 0:256], knqs[:, tt_, 0:128], knqs[:, tt_, :], True, True, [KQ], [pbk])
                tt("dve", tl["attnT"][:], pb_[:, 128:256], tl["dmC"][:], ALU.mult, [pbk, ("dmC" + SS)], [("attnT" + SS)])
                stt(tl["LTp"][:], pb_[:, 0:128], becol, tl["dmS"][:], ALU.mult, ALU.mult, [pbk, "gbt", ("dmS" + SS)], [("LTp" + SS)])
                pc_, pck = slot(0), PBK
                P.op("pe", lambda e, pc_=pc_: e.transpose(out=pc_[:, 0:128], in_=tl["LTp"][:], identity=ident[:]),
                     [("LTp" + SS), "ident"], [pck])
                cp("act", tl["A0"][:], pc_[:, 0:128], [pck], [("A0" + SS)])
                tt("pool", tl["PT"][:], ident[:], tl["LTp"][:], ALU.subtract, ["ident", ("LTp" + SS)], [("PT" + SS)])
                A, AT, Ak, ATk = tl["A0"], tl["LTp"], ("A0" + SS), ("LTp" + SS)
                for kk in range(1, nd + 1):
                    An, Ank = (tl["A1"], ("A1" + SS)) if (kk % 2) else (tl["A0"], ("A0" + SS))
                    ATn, ATnk = (tl["AT1"], ("AT1" + SS)) if (kk % 2) else (tl["AT0"], ("AT0" + SS))
                    p1, p1k = slot(0), PBK
                    mm(p1[:, 0:128], AT[:], A[:], True, True, [Ak, ATk], [p1k])
                    if kk < nd:
                        p2, p2k = slot(1), PBK
                        mm(p2[:, 0:128], A[:], AT[:], True, True, [Ak, ATk], [p2k])
                    cp("act", An[:], p1[:, 0:128], [p1k], [Ank])
                    if kk < nd:
                        cp("dve", ATn[:], p2[:, 0:128], [p2k], [ATnk])
                    p3, p3k = slot(2), PBK
                    mm(p3[:, 0:128], An[:], tl["PT"][:], True, True, [Ank, ("PT" + SS)], [p3k])
                    tt("dve", tl["PT"][:], tl["PT"][:], p3[:, 0:128], ALU.add, [("PT" + SS), p3k], [("PT" + SS)])
                    A, AT, Ak, ATk = An, ATn, Ank, ATnk
                p1, p1k = slot(0), PBK
                P.op("pe", lambda e, p1=p1: e.transpose(out=p1[:, 0:128], in_=tl["kgT"][:], identity=ident[:]),
                     [("kgT" + SS), "ident"], [p1k])
                cp("act", tl["kgtok"][:], p1[:, 0:128], [p1k], [("kgtok" + SS)])
                p2, p2k = slot(1), PBK
                P.op("pe", lambda e, p2=p2, tsl=tsl: e.transpose(out=p2[:, 0:128], in_=qkva[:, 2, tsl], identity=ident[:]),
                     [QK, "ident"], [p2k])
                cp("dve", tl["vtok"][:], p2[:, 0:128], [p2k], [("vtok" + SS)])
                p4, p4k = slot(2), PBK
                mm(p4[:, 0:128], tl["kgtok"][:], tl["PT"][:], True, True, [("kgtok" + SS), ("PT" + SS)], [p4k])
                act(tl["Wn"][:], p4[:, 0:128], AF.Identity, [p4k], [("Wn" + SS)], scale=-1.0)

            def tile_state(h, tt_, sid):
                hb = h % 2
                qkva, knqs, zs = qkvas[hb], knqss[hb], zss[hb]
                QK, KQ, ZK = "qkva%d" % hb, "knqs%d" % hb, "zs%d" % hb
                tl, SS = tls[sid], str(sid)
                kdecv, kdk = (kdec16, "kdec16") if is_sample else (kdec2[:], "kdec2")
                tsl = slice(tt_ * 128, (tt_ + 1) * 128)

                p3, p3k = PA[1], "pa1"
                P.op("pe", lambda e, p3=p3, tt_=tt_: e.transpose(out=p3[:, 0:128], in_=knqs[:, tt_, 0:128], identity=ident[:]),
                     [KQ, "ident"], [p3k])
                for i in range(nseg):
                    if i % 2 == 0:
                        ts("dve", kdecv[:, i, :], p3[:, 0:128], e2m[:, tt_, i, h:h + 1], None, ALU.mult, None,
                           [p3k, "e2m"], [kdk])
                    else:
                        act(kdecv[:, i, :], p3[:, 0:128], AF.Identity, [p3k, "e2m"], [kdk], scale=e2m[:, tt_, i, h:h + 1])
                pv_, pvk = (PA[1], "pa1")
                mm(pv_[:, 0:128], tl["vtok"][:], tl["PT"][:], True, True, [("vtok" + SS), ("PT" + SS)], [pvk])
                cp("act", tl["UT0"][:], pv_[:, 0:128], [pvk], ["UT0"])
                po_, pok = PS3, "ps3"
                for i in range(nseg):
                    a, b = i * seglen, (i + 1) * seglen
                    if is_sample:
                        s_in, s_out, sik, sok = Sin[:, i, :], Sout[:, i, :], "Sin", "Sout"
                    elif i % 2 == 0:
                        s_in, s_out, sik, sok = Sp[:, l, h, :], Stmp[:], "Sp", "Stmp"
                    else:
                        s_in, s_out, sik, sok = Stmp[:], Sp[:, l, h, :], "Stmp", "Sp"
                    if (not is_sample) or i == 0:
                        pw_, pwk = (PA[1], "pa1")
                    mm(pw_[:, a:b], s_in, tl["Wn"][:, a:b], True, True, [sik, ("Wn" + SS)], [pwk])
                    mm(po_[:, a:b], s_in, tl["qG"][:, a:b], True, True, [sik, ("qG" + SS)], [pok])
                    if (not is_sample) or i == nseg - 1:
                        a0 = a if not is_sample else 0
                        tt("dve", tl["UT"][:, a0:b], pw_[:, a0:b], tl["UT0"][:, a0:b], ALU.add, [pwk, "UT0"], ["UT"])
                        tt("dve", tl["UT"][:, a0:b], tl["UT"][:, a0:b], tl["bb"][:, a0:b], ALU.mult, ["UT", ("bb" + SS)], ["UT"])
                        p5, p5k = (PA[1], "pa1")
                        P.op("pe", lambda e, p5=p5: e.transpose(out=p5[:, 0:128], in_=tl["UT"][:], identity=ident[:]),
                             ["UT", "ident"], [p5k])
                        cp("act", tl["Utok"][:], p5[:, 0:128], [p5k], ["Utok"])
                    if not is_sample:
                        p6, p6k = (PA[1], "pa1")
                        mm(p6[:, 0:128], kdecv[:, i, :], tl["Utok"][:], True, True, [kdk, "Utok"], [p6k])
                        stt(s_out, s_in, tl["eg"][:, b - 1:b], p6[:, 0:128], ALU.mult, ALU.add, [sik, ("eg" + SS), p6k], [sok])
                if is_sample:
                    for i in range(nseg):
                        b = (i + 1) * seglen
                        p6, p6k = (PA[1], "pa1")
                        mm(p6[:, 0:128], kdecv[:, i, :], tl["Utok"][:], True, True, [kdk, "Utok"], [p6k])
                        stt(Sout[:, i, :], Sin[:, i, :], tl["eg"][:, b - 1:b], p6[:, 0:128], ALU.mult, ALU.add,
                            ["Sin", ("eg" + SS), p6k], ["Sout"])
                cp("act", oT[:, tsl], po_[:, 0:128], [pok], ["oT"])
                p7, p7k = (PA[1], "pa1")
                mm(p7[:, 0:128], tl["Utok"][:], tl["attnT"][:], True, True, ["Utok", ("attnT" + SS)], [p7k])
                tt("dve", oT[:, tsl], oT[:, tsl], p7[:, 0:128], ALU.add, ["oT", p7k], ["oT"])


            def head_back(h):
                hb = h % 2
                qkva, knqs, zs = qkvas[hb], knqss[hb], zss[hb]
                QK, KQ, ZK = "qkva%d" % hb, "knqs%d" % hb, "zs%d" % hb
                if is_sample:
                    dma("sp", oSs[l, :, h, :, :].rearrange("s p v -> p s v"), Sout, ["Sout"], [], "Sout")
                elif last:
                    dma("sp", oSp[l, h, :, :], Sp[:, l, h, :], ["Sp"], [], "Sp")
                tq, tk = tmp512()
                act(tq[:, 0:ntok], oT[:, 0:ntok], AF.Square, ["oT"], [tk])
                mm(PS2[:, 0:ntok], onesH[:], tq[:, 0:ntok], True, True, [tk, "onesH"], ["ps2"])
                act(tq[:, 0:ntok], PS2[:, 0:ntok], AF.Sqrt, ["ps2", "cst"], [tk], bias=cst[:, 1:2])
                recip(tq[:, 0:ntok], tq[:, 0:ntok], [tk], [tk])
                tt("dve", tq[:, 0:ntok], tq[:, 0:ntok], oT[:, 0:ntok], ALU.mult, [tk, "oT"], [tk])
                stt(mixT[:, h, 0:ntok], tq[:, 0:ntok], hng[:, l:l + 1], zs[:, 0:ntok], ALU.mult, ALU.mult,
                    [tk, "hng", ZK], ["mixT"])


            def record(fn, h):
                P.rec = []
                fn(h)
                r, P.rec = P.rec, None
                return r

            def replay(ops):
                for o in ops:
                    P.op(*o)

            def merge2(x, y):
                if not y:
                    return list(x)
                if not x:
                    return list(y)
                out_, yi = [], 0
                for xi, o in enumerate(x):
                    out_.append(o)
                    want = (xi + 1) * len(y) // len(x)
                    while yi < want:
                        out_.append(y[yi])
                        yi += 1
                out_.extend(y[yi:])
                return out_

            def rec2(fn, *a):
                P.rec = []
                fn(*a)
                r, P.rec = P.rec, None
                return r

            pa_front[0] = True
            replay(record(head_front, 0))
            pa_front[0] = False
            for h in range(H):
                S_ = [rec2(tile_solve, h, t, t) for t in range(nt)]
                Q_ = [rec2(tile_state, h, t, t) for t in range(nt)]
                t_ops = []
                for t in range(nt):
                    t_ops = merge2(t_ops, S_[t]) if t_ops else list(S_[t])
                for t in range(nt):
                    t_ops = t_ops + Q_[t]
                pa_front[0] = True
                f_ops = record(head_front, h + 1) if h + 1 < H else []
                if not is_sample:
                    f_ops = f_ops + record(conf_part1, h)
                pa_front[0] = False
                replay(merge2(t_ops, f_ops))
                head_back(h)
            if is_sample:
                for j in range(CT):
                    conf_part1(j)

            tq, tk = tmp512()
            tt("dve", tq[:, 0:ntok], mu[:, 0:ntok], mu[:, 0:ntok], ALU.mult, ["mu"], [tk])
            tt("dve", rsc[:, 0:ntok], rsc[:, 0:ntok], tq[:, 0:ntok], ALU.subtract, ["rsc", tk], ["rsc"])
            act(rsc[:, 0:ntok], rsc[:, 0:ntok], AF.Sqrt, ["rsc", "cst"], ["rsc"], bias=cst[:, 1:2])
            recip(rsc[:, 0:ntok], rsc[:, 0:ntok], ["rsc"], ["rsc"])
            for j in range(CT):
                wz, wzk = load_w(w_in_tile(l, 4 * DD + H2 + 2 * DC + j * 128))
                pa_, pak = nextpa()
                for k in range(KT):
                    mm(pa_[:, 0:ntok], wz[:, k, :], hT[:, k, 0:ntok], k == 0, k == KT - 1, [wzk, "hT"], [pak])
                tz, tzk = tmp512()
                act(tz[:, 0:ntok], pa_[:, 0:ntok], AF.Silu, [pak], [tzk])
                tq, tk = tmp512()
                tt("dve", tq[:, 0:ntok], cv[:, j, 0:ntok], mu[:, 0:ntok], ALU.subtract, ["cv", "mu"], [tk])
                tt("dve", tq[:, 0:ntok], tq[:, 0:ntok], rsc[:, 0:ntok], ALU.mult, [tk, "rsc"], [tk])
                act(tq[:, 0:ntok], tq[:, 0:ntok], AF.Silu, [tk, "lng", "lnb"], [tk],
                    scale=lng[:, l * CT + j:l * CT + j + 1], bias=lnb[:, l * CT + j:l * CT + j + 1])
                tt("dve", mixT[:, H + j, 0:ntok], tq[:, 0:ntok], tz[:, 0:ntok], ALU.mult, [tk, tzk], ["mixT"])

            for m in range(KT):
                wt, wk = load_w(w_out[l, :, m * 128:(m + 1) * 128].rearrange("(k p) c -> p k c", p=128))
                pt, pk = nextpa()
                for e_ in range(KT):
                    mm(pt[:, 0:ntok], wt[:, e_, :], mixT[:, e_, 0:ntok], e_ == 0, e_ == KT - 1, [wk, "mixT"], [pk])
                if nseq == 1:
                    stt(xT[:, m, 0:ntok], pt[:, 0:ntok], Gmod[:, l, m, 0:1], xT[:, m, 0:ntok], ALU.mult, ALU.add,
                        [pk, "Gmod", "xT"], ["xT"])
                else:
                    tq, tk = tmp512()
                    tt("dve", v3(tq[:, 0:ntok], nseq), v3(pt[:, 0:ntok], nseq), bc(Gmod, l, m), ALU.mult, [pk, "Gmod"], [tk])
                    tt("dve", xT[:, m, 0:ntok], xT[:, m, 0:ntok], tq[:, 0:ntok], ALU.add, ["xT", tk], ["xT"])

        rms_stats(rs)
        for k in range(KT):
            stt(xT[:, k, 0:ntok], xT[:, k, 0:ntok], fg[:, k:k + 1], rs[:, 0:ntok], ALU.mult, ALU.mult,
                ["xT", "fg", "rs"], ["xT"])
        for tt_ in range(nt):
            for k4 in range(0, KT, 4):
                n4 = min(4, KT - k4)
                pt, pk = nextpa()
                for q in range(n4):
                    P.op("pe", lambda e, pt=pt, q=q, k4=k4, tt_=tt_: e.transpose(
                        out=pt[:, q * 128:(q + 1) * 128], in_=xT[:, k4 + q, tt_ * 128:(tt_ + 1) * 128],
                        identity=ident[:]), ["xT", "ident"], [pk])
                cp("act", xio[:, k4 * 128:(k4 + n4) * 128], pt[:, 0:n4 * 128], [pk], ["xio"])
            dma("sp", ydst[tt_ * 128:(tt_ + 1) * 128, :], xio[:], ["xio"], [], "xio")

    try:
        for bi in range(NB):
            do_block(False, bi)
        do_block(True, 0)
    except StopBuild:
        pass
    P.emit()
    st.close()
    return nc


_NC_CACHE = {}


def _get_nc(cfg_key):
    if cfg_key not in _NC_CACHE:
        _NC_CACHE[cfg_key] = build(Cfg(*cfg_key))
    return _NC_CACHE[cfg_key]


def make_in_maps(inp, cfg, ncores):
    D, L, KT, H, DD, DC, CT = cfg.D, cfg.L, cfg.KT, cfg.H, cfg.DD, cfg.DC, cfg.CT
    f = lambda a: np.ascontiguousarray(np.asarray(a, dtype=np.float32))
    Bp = inp["x_prompt"].shape[0]
    shared = {
        "norm_g": f(inp["norm_g"]).reshape(L * KT, 128),
        "w_ada": f(inp["w_ada"]),
        "b_ada": f(inp["b_ada"]).reshape(L * 3 * KT, 128),
        "w_in": f(inp["w_in"]),
        "w_qc": f(inp["w_qkv_conv"]).reshape(L * 4 * 3 * H, 128),
        "a_log": f(inp["a_log"]).reshape(1, L * H),
        "dt_bias": f(inp["dt_bias"]).reshape(1, L * H),
        "hn_g": f(inp["head_norm_g"]).reshape(L, 128),
        "w_dw": f(inp["w_dw"]).reshape(L * 31 * CT, 128),
        "b_dw": f(inp["b_dw"]).reshape(L * CT, 128),
        "ln_g": f(inp["ln_g"]).reshape(L * CT, 128),
        "ln_b": f(inp["ln_b"]).reshape(L * CT, 128),
        "w_out": f(inp["w_out"]),
        "final_g": f(inp["final_g"]).reshape(KT, 128),
    }
    maps = []
    NS = cfg.NS
    for c in range(ncores):
        b = c % Bp
        s0 = c * NS
        m = dict(shared)
        m["xp"] = f(inp["x_prompt"][b])
        m["xs"] = f(inp["x_sample"][s0:s0 + NS]).reshape(NS * cfg.TS, D)
        m["cc"] = f(np.concatenate([np.asarray(inp["c_prompt"])[b:b + 1], np.asarray(inp["c_sample"])[s0:s0 + NS]], axis=0))
        m["sd"] = f(np.asarray(inp["state_delta"])[:, s0:s0 + NS])
        m["sq"] = f(np.asarray(inp["state_qkv_conv"])[:, s0:s0 + NS]).reshape(L, NS * 3, 3 * DD)
        m["sg"] = f(np.asarray(inp["state_glu_conv"])[:, s0:s0 + NS]).reshape(L, NS * 30, DC)
        maps.append(m)
    return maps


def assemble(res, cfg, ncores, Bp):
    D, L, H, DD, DC, NS, TS = cfg.D, cfg.L, cfg.H, cfg.DD, cfg.DC, cfg.NS, cfg.TS
    g = lambda c, n: np.asarray(res[c][n], dtype=np.float32)
    y_prompt = np.stack([g(b, "yp") for b in range(Bp)])
    y_sample = np.concatenate([g(c, "ys").reshape(NS, TS, D) for c in range(ncores)], axis=0)
    Sp = np.stack([g(b, "oSp") for b in range(Bp)], axis=1)
    qp = np.stack([g(b, "oqp") for b in range(Bp)], axis=1)
    gp = np.stack([g(b, "ogp") for b in range(Bp)], axis=1)
    Ss = np.concatenate([g(c, "oSs") for c in range(ncores)], axis=1)
    qs = np.concatenate([g(c, "oqs").reshape(L, NS, 3, 3 * DD) for c in range(ncores)], axis=1)
    gs = np.concatenate([g(c, "ogs").reshape(L, NS, 30, DC) for c in range(ncores)], axis=1)
    return (y_prompt, y_sample, Sp, qp, gp, Ss, qs, gs)


def kernel(**inputs):
    ncores = 8
    cfg_key = (2048, 2048, 512, 2, BF16)
    cfg = Cfg(*cfg_key)
    nc = _get_nc(cfg_key)
    maps = make_in_maps(inputs, cfg, ncores)
    res = run_bass_kernel_spmd(nc, maps, core_ids=list(range(ncores)))
    return assemble(res.results, cfg, ncores, 4)
```

```python
import contextlib
import math
import numpy as np
import concourse.bass as bass
import concourse.mybir as mybir
from concourse.bass_utils import run_bass_kernel_spmd

F32 = mybir.dt.float32
BF16 = mybir.dt.bfloat16
F32R = mybir.dt.float32r
ALU = mybir.AluOpType
AF = mybir.ActivationFunctionType
EPS = 1e-6
ENGS = ("pe", "act", "dve", "pool", "sp")


class Ins:
    __slots__ = ("eng", "fn", "deps", "sig", "cnt", "dma_key", "dma_gen")

    def __init__(self, eng, fn):
        self.eng = eng
        self.fn = fn
        self.deps = []
        self.sig = False
        self.cnt = 0
        self.dma_key = None
        self.dma_gen = 0


class Prog:
    def __init__(self, nc):
        self.nc = nc
        self.lists = {e: [] for e in ENGS}
        self.last_w = {}
        self.readers = {}
        self.dma_gen = {}
        self.all = []
        self.rec = None

    PSUM_KEYS = frozenset(["pa0", "pa1", "ps2", "ps3", "pr0", "pr1", "pr2", "pr3"])

    def op(self, eng, fn, reads=(), writes=(), dma_key=None):
        if self.rec is not None:
            self.rec.append((eng, fn, tuple(reads), tuple(writes), dma_key))
            return None
        ins = Ins(eng, fn)
        writes = list(writes) + [k for k in reads if k in self.PSUM_KEYS and k not in writes]
        deps = []
        for k in reads:
            w = self.last_w.get(k)
            if w is not None:
                deps.append(w)
        for k in writes:
            w = self.last_w.get(k)
            if w is not None:
                deps.append(w)
            deps.extend(self.readers.get(k, {}).values())
        seen = set()
        for d in deps:
            if id(d) in seen:
                continue
            seen.add(id(d))
            if d.eng == "pe" and eng == "pe" and d.dma_key is None and dma_key is None:
                continue
            ins.deps.append(d)
        if dma_key is not None:
            g = self.dma_gen.get(dma_key, 0) + 1
            self.dma_gen[dma_key] = g
            ins.dma_key = dma_key
            ins.dma_gen = g
        slot = eng if dma_key is None else ("dma", dma_key)
        for k in reads:
            self.readers.setdefault(k, {})[slot] = ins
        for k in writes:
            self.last_w[k] = ins
            self.readers[k] = {}
        self.lists[eng].append(ins)
        self.all.append(ins)
        return ins

    def emit(self, final_wait_eng="sp"):
        nc = self.nc
        for ins in self.all:
            for d in ins.deps:
                if d.dma_key is None:
                    d.sig = True
        nsig = {}
        for e in ENGS:
            c = 0
            for ins in self.lists[e]:
                if ins.dma_key is None and ins.sig:
                    c += 1
                    ins.cnt = c
            nsig[e] = c
        dma_keys = sorted(self.dma_gen.keys(), key=str)
        with contextlib.ExitStack() as st:
            esem = {e: st.enter_context(nc.semaphore("s_" + e)) for e in ENGS}
            dsem = {k: st.enter_context(nc.semaphore("d%d" % i)) for i, k in enumerate(dma_keys)}
            block = st.enter_context(nc.Block())
            hw = {"pe": block.tensor, "act": block.scalar, "dve": block.vector,
                  "pool": block.gpsimd, "sp": block.sync}

            def make(e):
                def body(engine):
                    waited = {}
                    for ins in self.lists[e]:
                        for d in ins.deps:
                            if d.dma_key is not None:
                                s, v = dsem[d.dma_key], 16 * d.dma_gen
                            else:
                                s, v = esem[d.eng], d.cnt
                            if waited.get(id(s), 0) >= v:
                                continue
                            waited[id(s)] = v
                            engine.wait_ge(s, v)
                        r = ins.fn(engine)
                        if ins.dma_key is not None:
                            r.then_inc(dsem[ins.dma_key], 16)
                        elif ins.sig:
                            r.then_inc(esem[e], 1)
                    if e == final_wait_eng:
                        for k in dma_keys:
                            engine.wait_ge(dsem[k], 16 * self.dma_gen[k])
                        for e2 in ENGS:
                            if nsig[e2] and e2 != e:
                                engine.wait_ge(esem[e2], nsig[e2])
                return body

            for e in ENGS:
                if self.lists[e] or e == final_wait_eng:
                    hw[e](make(e))


class StopBuild(Exception):
    pass


class Cfg:
    dbg = 0

    def __init__(self, D=2048, T=2048, TB=512, L=2, wdt=BF16):
        self.D, self.T, self.TB, self.L = D, T, TB, L
        self.KT = D // 128
        self.H = D // 256
        self.DD = D // 2
        self.DC = D // 2
        self.CT = self.DC // 128
        self.NIN = 4 * self.DD + 2 * self.H + 3 * self.DC
        self.NB = T // TB
        self.NS = 16
        self.TS = 8
        self.wdt = wdt


def build(cfg):
    D, T, TB, L, KT, H, DD, DC, CT, NIN, NB = (cfg.D, cfg.T, cfg.TB, cfg.L, cfg.KT, cfg.H, cfg.DD,
                                                cfg.DC, cfg.CT, cfg.NIN, cfg.NB)
    WDT = cfg.wdt
    NS, TS = cfg.NS, cfg.TS
    NTB = TB // 128
    H2 = 2 * H
    QSC = 128.0 ** -0.5
    nc = bass.Bass("TRN2", target_bir_lowering=False)

    def din(name, shape):
        return nc.dram_tensor(name, list(shape), F32, kind="ExternalInput").ap()

    def dout(name, shape):
        return nc.dram_tensor(name, list(shape), F32, kind="ExternalOutput").ap()

    xp = din("xp", [T, D])
    xs = din("xs", [128, D])
    cc = din("cc", [1 + NS, D])
    sd = din("sd", [L, NS, H, 128, 128])
    sq = din("sq", [L, NS * 3, 3 * DD])
    sg = din("sg", [L, NS * 30, DC])
    norm_g = din("norm_g", [L * KT, 128])
    w_ada = din("w_ada", [L, D, 3 * D])
    b_ada = din("b_ada", [L * 3 * KT, 128])
    w_in = din("w_in", [L, D, NIN])
    w_qc = din("w_qc", [L * 4 * 3 * H, 128])
    a_log = din("a_log", [1, L * H])
    dt_bias = din("dt_bias", [1, L * H])
    hn_g = din("hn_g", [L, 128])
    w_dw = din("w_dw", [L * 31 * CT, 128])
    b_dw = din("b_dw", [L * CT, 128])
    ln_g = din("ln_g", [L * CT, 128])
    ln_b = din("ln_b", [L * CT, 128])
    w_out = din("w_out", [L, D, D])
    final_g = din("final_g", [KT, 128])

    yp = dout("yp", [T, D])
    ys = dout("ys", [128, D])
    oSp = dout("oSp", [L, H, 128, 128])
    oqp = dout("oqp", [L, 3, 3 * DD])
    ogp = dout("ogp", [L, 30, DC])
    oSs = dout("oSs", [L, NS, H, 128, 128])
    oqs = dout("oqs", [L, NS * 3, 3 * DD])
    ogs = dout("ogs", [L, NS * 30, DC])

    st = contextlib.ExitStack()
    P = Prog(nc)

    def sb(name, shape, dt=F32):
        return st.enter_context(nc.sbuf_tensor(name, list(shape), dt))

    def psb(name):
        return st.enter_context(nc.psum_tensor(name, [128, 512], F32))

    def mm(out, lhsT, rhs, start, stop, r, w, skip=False):
        return P.op("pe", lambda e: e.matmul(out, lhsT=lhsT, rhs=rhs, start=start, stop=stop,
                                             skip_group_check=skip), r, w)

    def act(out, in_, func, r, w, scale=1.0, bias=None):
        if bias is None:
            return P.op("act", lambda e: e.activation(out=out, in_=in_, func=func, scale=scale), r, w)
        return P.op("act", lambda e: e.activation(out=out, in_=in_, func=func, scale=scale, bias=bias), r, w)

    def ts(eng, out, in0, s1, s2, op0, op1, r, w):
        if op1 is None:
            return P.op(eng, lambda e: e.tensor_scalar(out=out, in0=in0, scalar1=s1, scalar2=None, op0=op0), r, w)
        return P.op(eng, lambda e: e.tensor_scalar(out=out, in0=in0, scalar1=s1, scalar2=s2, op0=op0, op1=op1), r, w)

    def tt(eng, out, in0, in1, op, r, w):
        return P.op(eng, lambda e: e.tensor_tensor(out=out, in0=in0, in1=in1, op=op), r, w)

    def stt(out, in0, scalar, in1, op0, op1, r, w):
        return P.op("dve", lambda e: e.scalar_tensor_tensor(out=out, in0=in0, scalar=scalar, in1=in1,
                                                            op0=op0, op1=op1), r, w)

    def cp(eng, out, in_, r, w):
        if eng == "act":
            return P.op("act", lambda e: e.copy(out=out, in_=in_), r, w)
        return P.op(eng, lambda e: e.tensor_copy(out=out, in_=in_), r, w)

    def dma(eng, out, in_, r, w, key):
        return P.op(eng, lambda e: e.dma_start(out=out, in_=in_), r, w, dma_key=key)

    def recip(out, in_, r, w):
        return P.op("dve", lambda e: e.reciprocal(out=out, in_=in_), r, w)

    def asel(out, in_, pattern, op, fill, base, cm, r, w):
        return P.op("pool", lambda e: e.affine_select(out=out, in_=in_, pattern=pattern, compare_op=op,
                                                      fill=fill, base=base, channel_multiplier=cm), r, w)

    def memset(eng, out, val, w):
        return P.op(eng, lambda e: e.memset(out, val), (), w)

    PA = [psb("pa0"), psb("pa1")]
    PS2 = psb("ps2")
    PS3 = psb("ps3")
    PR = [psb("pr%d" % i) for i in range(4)]
    pa_i = [0]
    pr_i = [0]

    pa_front = [False]

    def nextpa():
        if pa_front[0]:
            return PA[0], "pa0"
        pa_i[0] ^= 1
        t = PA[pa_i[0]]
        return t, "pa%d" % pa_i[0]

    def nextpr():
        pr_i[0] = (pr_i[0] + 1) % 4
        return PR[pr_i[0]], "pr%d" % pr_i[0]

    ident = sb("ident", [128, 128])
    cst = sb("cst", [128, 4])
    onesD = sb("onesD", [128, 128])
    onesC = sb("onesC", [128, 128])
    onesH = sb("onesH", [128, 128])
    ones1 = sb("ones1", [128, 128])
    EHf = sb("EHf", [128, H2])
    EH = sb("EH", [128, H2], F32R)
    identR = sb("identR", [128, 128], F32R)
    memset("pool", ident[:], 0.0, ["ident"])
    asel(ident[:], ident[:], [[-1, 128]], ALU.not_equal, 1.0, 0, 1, ["ident"], ["ident"])
    cp("dve", identR[:], ident[:], ["ident"], ["identR"])
    memset("pool", cst[:, 0:1], 1.0, ["cst"])
    memset("pool", cst[:, 1:2], EPS, ["cst"])
    memset("pool", cst[:, 2:3], 0.0, ["cst"])
    memset("pool", cst[:, 3:4], -1.0, ["cst"])
    memset("pool", onesD[:], 1.0 / D, ["onesD"])
    memset("pool", onesC[:], 1.0 / DC, ["onesC"])
    memset("pool", onesH[:], 1.0 / 128, ["onesH"])
    memset("pool", ones1[:], 1.0, ["ones1"])
    memset("pool", EHf[:], 0.0, ["EHf"])
    asel(EHf[:], EHf[:], [[-1, H2]], ALU.not_equal, 1.0, 0, 1, ["EHf"], ["EHf"])
    cp("dve", EH[:], EHf[:], ["EHf"], ["EH"])

    class Grp:
        pass

    def make_masks(name, seglen):
        nseg = 128 // seglen
        g = Grp()
        g.seglen, g.nseg = seglen, nseg
        g.McT = sb(name + "McT", [128, 128])
        g.MsT = sb(name + "MsT", [128, 128])
        g.SegAll = sb(name + "Seg", [128, 128])
        g.rowm = sb(name + "rowm", [128, nseg])
        g.key = name + "masks"
        k = [g.key]
        for m, off in ((g.McT, 0), (g.MsT, -1)):
            memset("pool", m[:], 1.0, k)
            asel(m[:], m[:], [[1, 128]], ALU.is_ge, 0.0, off, -1, k, k)
            v = m[:].rearrange("p (a b) -> p a b", b=seglen)
            asel(v, v, [[-seglen, nseg], [0, seglen]], ALU.is_ge, 0.0, 0, 1, k, k)
        memset("pool", g.SegAll[:], 1.0, k)
        v = g.SegAll[:].rearrange("p (a b) -> p a b", b=seglen)
        asel(v, v, [[-seglen, nseg], [0, seglen]], ALU.is_ge, 0.0, 0, 1, k, k)
        asel(v, v, [[seglen, nseg], [0, seglen]], ALU.is_ge, 0.0, seglen - 1, -1, k, k)
        memset("pool", g.rowm[:], 1.0, k)
        asel(g.rowm[:], g.rowm[:], [[-seglen, nseg]], ALU.is_ge, 0.0, 0, 1, k, k)
        asel(g.rowm[:], g.rowm[:], [[seglen, nseg]], ALU.is_ge, 0.0, seglen - 1, -1, k, k)
        return g

    GP = make_masks("p", 64)
    GS = make_masks("s", 8)

    stage = sb("stage", [128, 128])
    memset("pool", stage[:], 0.0, ["stage"])

    def load_cols(name, src, nrows):
        dst = sb(name, [128, nrows])
        for r0 in range(0, nrows, 128):
            n = min(128, nrows - r0)
            dma("sp", stage[0:n, :], src[r0:r0 + n, :], [], ["stage"], "stage")
            pt, pk = nextpr()
            P.op("pe", lambda e, pt=pt: e.transpose(out=pt[:, 0:128], in_=stage[:, :], identity=ident[:]),
                 ["stage", "ident"], [pk])
            cp("dve", dst[:, r0:r0 + n], pt[:, 0:n], [pk], [name])
        return dst

    ng = load_cols("ng", norm_g, L * KT)
    bada = load_cols("bada", b_ada, L * 3 * KT)
    wqc = load_cols("wqc", w_qc, L * 4 * 3 * H)
    hng = load_cols("hng", hn_g, L)
    wdw = load_cols("wdw", w_dw, L * 31 * CT)
    bdw = load_cols("bdw", b_dw, L * CT)
    lng = load_cols("lng", ln_g, L * CT)
    lnb = load_cols("lnb", ln_b, L * CT)
    fg = load_cols("fg", final_g, KT)

    nea = sb("nea", [128, L * H])
    dtb = sb("dtb", [128, L * H])
    dma("sp", nea[:], a_log.partition_broadcast(128), [], ["nea"], "nea")
    dma("sp", dtb[:], dt_bias.partition_broadcast(128), [], ["dtb"], "dtb")
    act(nea[:], nea[:], AF.Exp, ["nea"], ["nea"])
    ts("dve", nea[:], nea[:], -1.0, None, ALU.mult, None, ["nea"], ["nea"])

    NWS = 3
    wsl = [sb("wsl%d" % i, [128, KT, 128], WDT) for i in range(NWS)]
    ws_i = [0]

    def load_w(src3):
        ws_i[0] = (ws_i[0] + 1) % NWS
        i = ws_i[0]
        key = "wsl%d" % i
        dma("pool", wsl[i][:], src3, [], [key], key)
        return wsl[i], key

    def w_in_tile(l, col0):
        return w_in[l, :, col0:col0 + 128].rearrange("(k p) c -> p k c", p=128)

    NC17 = 1 + NS
    xio = sb("xio", [128, D])
    ccs = xio[0:NC17, :]
    memset("pool", xio[:], 0.0, ["xio"])
    scT = sb("scT", [128, KT, NC17], WDT)
    Amod = sb("Amod", [128, L, KT, NC17])
    Bmod = sb("Bmod", [128, L, KT, NC17])
    Gmod = sb("Gmod", [128, L, KT, NC17])
    dma("sp", ccs, cc, [], ["xio"], "xio")
    act(ccs, ccs, AF.Silu, ["xio"], ["xio"])
    for k in range(KT):
        pt, pk = nextpr()
        P.op("pe", lambda e, pt=pt, k=k: e.transpose(out=pt[:, 0:128], in_=xio[:, k * 128:(k + 1) * 128],
                                                     identity=ident[:]), ["xio", "ident"], [pk])
        cp("dve", scT[:, k, :], pt[:, 0:NC17], [pk], ["scT"])
    for l in range(L):
        for j in range(3 * KT):
            wt, wk = load_w(w_ada[l, :, j * 128:(j + 1) * 128].rearrange("(k p) c -> p k c", p=128))
            pt, pk = nextpr()
            for k in range(KT):
                mm(pt[:, 0:NC17], wt[:, k, :], scT[:, k, :], k == 0, k == KT - 1, [wk, "scT"], [pk])
            bcol = bada[:, l * 3 * KT + j: l * 3 * KT + j + 1]
            if j < KT:
                ts("dve", Bmod[:, l, j, :], pt[:, 0:NC17], bcol, None, ALU.add, None, [pk, "bada"], ["Bmod"])
            elif j < 2 * KT:
                ts("dve", Amod[:, l, j - KT, :], pt[:, 0:NC17], bcol, 1.0, ALU.add, ALU.add, [pk, "bada"], ["Amod"])
                ts("dve", Amod[:, l, j - KT, :], Amod[:, l, j - KT, :], ng[:, l * KT + j - KT: l * KT + j - KT + 1],
                   None, ALU.mult, None, ["Amod", "ng"], ["Amod"])
            else:
                ts("dve", Gmod[:, l, j - 2 * KT, :], pt[:, 0:NC17], bcol, None, ALU.add, None, [pk, "bada"], ["Gmod"])

    cur_blk = [0]

    def ckpt(n, cond=True):
        if cfg.dbg == n and cond and cur_blk[0] == getattr(cfg, 'dbg_block', 0):
            raise StopBuild()

    Sp = sb("Sp", [128, L, H, 128])
    Stmp = sb("Stmp", [128, 128])
    qtail = sb("qtail", [128, L, 3 * H, 3])
    gtail = sb("gtail", [128, L, CT, 30])
    memset("pool", Sp[:], 0.0, ["Sp"])
    memset("pool", qtail[:], 0.0, ["qtail"])
    memset("pool", gtail[:], 0.0, ["gtail"])

    xTf = sb("xTf", [128, KT * TB])
    ALIAS = KT * (TB - 128) >= 3 * NS * 128
    hT = sb("hT", [128, KT, TB], WDT)
    mixT = sb("mixT", [128, KT, TB], WDT)
    cv = sb("cv", [128, CT, TB])
    t512 = [sb("t512_%d" % i, [128, max(TB, 512)]) for i in range(3)]
    for i in range(3):
        memset("pool", t512[i][:], 0.0, ["t512_%d" % i])
    t5_i = [0]

    def tmp512():
        t5_i[0] = (t5_i[0] + 1) % 3
        return t512[t5_i[0]], "t512_%d" % t5_i[0]

    rs = sb("rs", [128, TB])
    wab = sb("wab", [128, KT, H2], WDT)
    graw = sb("graw", [128, NTB, H])
    gbt = sb("gbt", [128, NTB, 128])
    memset("pool", gbt[:], 0.0, ["gbt"])
    e2t = sb("e2t", [128, NTB, H])
    e2m = sb("e2m", [128, NTB, 16, H])
    gtmp = sb("gtmp", [128, H2])
    gbT = sb("gbT", [128, NTB, 128], F32R)
    cp("dve", gbT[:].rearrange("p a b -> p (a b)"), t512[0][:, 0:NTB * 128], ["t512_0"], ["gbT"])
    XW = max(3 + TB, NS * (3 + TS))
    xpre = sb("xpre", [128, 3, XW])
    qkvas = [sb("qkva%d" % i, [128, 3, TB]) for i in range(2)]
    zss = [sb("zs%d" % i, [128, TB]) for i in range(2)]
    knqss = [sb("knqs%d" % i, [128, NTB, 256]) for i in range(2)]
    oT = sb("oT", [128, TB])
    UW = max(30 + TB, NS * (30 + TS))
    ubuf = sb("ubuf", [128, UW])
    mu = sb("mu", [128, TB])
    rsc = sb("rsc", [128, TB])
    names = ["eg", "bb", "qG", "kgT", "dd", "decT", "dmC", "dmS", "attnT", "LTp", "A0", "A1", "AT0", "AT1",
             "PT", "kgtok", "vtok", "Wn", "UT", "Utok", "UT0"]
    shared_t = ("UT", "Utok", "UT0")
    r_names = ("attnT", "LTp", "A0", "A1", "AT0", "AT1", "PT", "kgtok", "vtok", "Utok")
    tdt = lambda n: F32R if n in r_names else F32
    tls = [{n: sb(n + ("" if n in shared_t else "0"), [128, 128], tdt(n)) for n in names}]
    tls.append({n: (tls[0][n] if n in shared_t else sb(n + "1", [128, 128], tdt(n))) for n in names})
    tl = tls[0]
    nbi = [0, 0]
    kdec2s = [sb("kdec2_%d" % i, [128, 2, 128], F32R) for i in range(2)]
    if ALIAS:
        o0 = KT * 128
        Sin = xTf[:, o0:o0 + NS * 128].rearrange("p (s v) -> p s v", s=NS)
        Sout = xTf[:, o0 + NS * 128:o0 + 2 * NS * 128].rearrange("p (s v) -> p s v", s=NS)
        kdec16 = xTf[:, o0 + 2 * NS * 128:o0 + 3 * NS * 128].rearrange("p (s v) -> p s v", s=NS)
    else:
        Sin = sb("Sin", [128, NS, 128])[:]
        Sout = sb("Sout", [128, NS, 128])[:]
        kdec16 = sb("kdec16", [128, NS, 128])[:]
    st48 = sb("st48", [128, 3, 128])
    memset("pool", st48[:], 0.0, ["st48"])
    st120 = sb("st120", [128, 4, 128])
    memset("pool", st120[:], 0.0, ["st120"])
    memset("pool", tl["UT"][:], 0.0, ["UT"])
    cp("dve", tl["Utok"][:], t512[0][:, 0:128], ["t512_0"], ["Utok"])

    def v3(ap, nseq):
        return ap.rearrange("p (s t) -> p s t", s=nseq)

    def do_block(is_sample, bi):
        if is_sample:
            ntok, nseq, Tq, grp, c0, ncol = 128, NS, TS, GS, 1, NS
            xsrc, ydst = xs, ys
        else:
            ntok, nseq, Tq, grp, c0, ncol = TB, 1, TB, GP, 0, 1
            xsrc, ydst = xp[bi * TB:(bi + 1) * TB, :], yp[bi * TB:(bi + 1) * TB, :]
        last = is_sample or bi == NB - 1
        cur_blk[0] = NB if is_sample else bi
        nt = ntok // 128
        seglen, nseg = grp.seglen, grp.nseg
        nd = int(math.log2(seglen)) - 1
        MK = [grp.key]
        xT = xTf[:, 0:KT * ntok].rearrange("p (k t) -> p k t", k=KT)
        if is_sample and ALIAS:
            memset("pool", xTf[:, KT * 128:KT * 128 + 1], 0.0, ["xT", "Sin", "Sout", "kdec16"])

        def bc(modt, l, k):
            return modt[:, l, k, c0:c0 + ncol].unsqueeze(2).broadcast_to([128, nseq, Tq])

        for tt_ in range(nt):
            dma("sp", xio[:], xsrc[tt_ * 128:(tt_ + 1) * 128, :], [], ["xio"], "xio")
            for k4 in range(0, KT, 4):
                n4 = min(4, KT - k4)
                pt, pk = nextpa()
                for q in range(n4):
                    P.op("pe", lambda e, pt=pt, q=q, k4=k4: e.transpose(
                        out=pt[:, q * 128:(q + 1) * 128], in_=xio[:, (k4 + q) * 128:(k4 + q + 1) * 128],
                        identity=ident[:]), ["xio", "ident"], [pk])
                cp("act", xT[:, k4:k4 + n4, tt_ * 128:(tt_ + 1) * 128],
                   pt[:, 0:n4 * 128].rearrange("p (a b) -> p a b", b=128), [pk], ["xT"])

        def rms_stats(dst):
            for k in range(KT):
                tq, tk = tmp512()
                act(tq[:, 0:ntok], xT[:, k, 0:ntok], AF.Square, ["xT"], [tk])
                mm(PS2[:, 0:ntok], onesD[:], tq[:, 0:ntok], k == 0, k == KT - 1, [tk, "onesD"], ["ps2"])
            act(dst[:, 0:ntok], PS2[:, 0:ntok], AF.Sqrt, ["ps2", "cst"], ["rs"], bias=cst[:, 1:2])
            recip(dst[:, 0:ntok], dst[:, 0:ntok], ["rs"], ["rs"])

        for l in range(L):
            rms_stats(rs)
            for k in range(KT):
                tq, tk = tmp512()
                if nseq == 1:
                    stt(tq[:, 0:ntok], xT[:, k, 0:ntok], Amod[:, l, k, 0:1], rs[:, 0:ntok], ALU.mult, ALU.mult,
                        ["xT", "Amod", "rs"], [tk])
                    ts("dve", hT[:, k, 0:ntok], tq[:, 0:ntok], Bmod[:, l, k, 0:1], None, ALU.add, None,
                       [tk, "Bmod"], ["hT"])
                else:
                    tt("dve", tq[:, 0:ntok], xT[:, k, 0:ntok], rs[:, 0:ntok], ALU.mult, ["xT", "rs"], [tk])
                    tt("dve", v3(tq[:, 0:ntok], nseq), v3(tq[:, 0:ntok], nseq), bc(Amod, l, k), ALU.mult,
                       [tk, "Amod"], [tk])
                    tt("dve", v3(hT[:, k, 0:ntok], nseq), v3(tq[:, 0:ntok], nseq), bc(Bmod, l, k), ALU.add,
                       [tk, "Bmod"], ["hT"])

            dma("pool", wab[:], w_in[l, :, 4 * DD:4 * DD + H2].rearrange("(k p) c -> p k c", p=128),
                [], ["wab"], "wab")
            for tt_ in range(nt):
                pt, pk = nextpr()
                for k in range(KT):
                    mm(pt[:, 0:H2], hT[:, k, tt_ * 128:(tt_ + 1) * 128], wab[:, k, :], k == 0, k == KT - 1,
                       ["hT", "wab"], [pk])
                act(gbt[:, tt_, H:H2], pt[:, 0:H], AF.Sigmoid, [pk], ["gbt"])
                tt("dve", gtmp[:, 0:H], pt[:, H:H2], dtb[:, l * H:(l + 1) * H], ALU.add, [pk, "dtb"], ["gtmp"])
                act(gtmp[:, 0:H], gtmp[:, 0:H], AF.Exp, ["gtmp"], ["gtmp"])
                act(gtmp[:, 0:H], gtmp[:, 0:H], AF.Ln, ["gtmp", "cst"], ["gtmp"], bias=cst[:, 0:1])
                tt("dve", graw[:, tt_, :], gtmp[:, 0:H], nea[:, l * H:(l + 1) * H], ALU.mult, ["gtmp", "nea"], ["graw"])
                pt, pk = nextpr()
                mm(pt[:, 0:H], grp.McT[:], graw[:, tt_, :], True, True, MK + ["graw"], [pk])
                mm(pt[:, H:H2], grp.SegAll[:], graw[:, tt_, :], True, True, MK + ["graw"], [pk])
                cp("act", gbt[:, tt_, 0:H], pt[:, 0:H], [pk], ["gbt"])
                tt("dve", gtmp[:, H:H2], pt[:, H:H2], gbt[:, tt_, 0:H], ALU.subtract, [pk, "gbt"], ["gtmp"])
                act(e2t[:, tt_, :], gtmp[:, H:H2], AF.Exp, ["gtmp"], ["e2t"])
                tt("dve", e2m[:, tt_, 0:nseg, :],
                   e2t[:, tt_, :].unsqueeze(1).broadcast_to([128, nseg, H]),
                   grp.rowm[:].unsqueeze(2).broadcast_to([128, nseg, H]), ALU.mult, ["e2t"] + MK, ["e2m"])
                pt, pk = nextpr()
                P.op("pe", lambda e, pt=pt, tt_=tt_: e.transpose(out=pt[:, 0:128], in_=gbt[:, tt_, :],
                                                                identity=ident[:]), ["gbt", "ident"], [pk])
                cp("dve", gbT[0:H2, tt_, :], pt[0:H2, 0:128], [pk], ["gbT"])

            US = 30 + Tq
            uv = ubuf[:, 0:nseq * US].rearrange("p (s t) -> p s t", s=nseq)

            def conf_part1(j):
                if is_sample:
                    dma("sp", st120[0:120, :, :], sg[l, :, j * 128:(j + 1) * 128].rearrange("(g r) x -> r g x", r=120),
                        [], ["st120"], "st120")
                    pt, pk = PS2, "ps2"
                    for g4 in range(4):
                        P.op("pe", lambda e, pt=pt, g4=g4: e.transpose(out=pt[:, g4 * 128:(g4 + 1) * 128], in_=st120[:, g4, :],
                                                                       identity=ident[:]), ["st120", "ident"], [pk])
                    cp("act", uv.rearrange("p (g s) t -> p g s t", g=4)[:, :, :, 0:30], pt[:, 0:512].rearrange("p (g x) -> p g x", g=4)[:, :, 0:120].rearrange("p g (s r) -> p g s r", r=30), [pk], ["ubuf"])
                else:
                    cp("pool", uv[:, 0, 0:30], gtail[:, l, j, :], ["gtail"], ["ubuf"])
                wb_, wbk = load_w(w_in_tile(l, 4 * DD + H2 + DC + j * 128))
                pb_, pbk = nextpa()
                for k in range(KT):
                    mm(pb_[:, 0:ntok], wb_[:, k, :], hT[:, k, 0:ntok], k == 0, k == KT - 1, [wbk, "hT"], [pbk])
                tq, tk = tmp512()
                act(tq[:, 0:ntok], pb_[:, 0:ntok], AF.Sigmoid, [pbk], [tk])
                wa, wak = load_w(w_in_tile(l, 4 * DD + H2 + j * 128))
                pa_, pak = nextpa()
                for k in range(KT):
                    mm(pa_[:, 0:ntok], wa[:, k, :], hT[:, k, 0:ntok], k == 0, k == KT - 1, [wak, "hT"], [pak])
                tt("dve", uv[:, :, 30:30 + Tq], v3(pa_[:, 0:ntok], nseq), v3(tq[:, 0:ntok], nseq), ALU.mult,
                   [pak, tk], ["ubuf"])
                if not is_sample:
                    cp("pool", gtail[:, l, j, :], uv[:, 0, Tq:Tq + 30], ["ubuf"], ["gtail"])
                if last:
                    nr = nseq * 30
                    tq2, tk2 = tmp512()
                    cp("pool", tq2[:, 0:nr].rearrange("p (s r) -> p s r", r=30), uv[:, :, Tq:Tq + 30], ["ubuf"], [tk2])
                    if is_sample:
                        pt, pk = PS2, "ps2"
                        for g4 in range(4):
                            P.op("pe", lambda e, pt=pt, g4=g4, tq2=tq2: e.transpose(
                                out=pt[:, g4 * 128:(g4 + 1) * 128], in_=tq2[:, g4 * 120:g4 * 120 + 128],
                                identity=ident[:]), [tk2, "ident"], [pk])
                        cp("act", st120[0:120, :, :], pt[0:120, 0:512].rearrange("p (g x) -> p g x", g=4), [pk], ["st120"])
                        dma("sp", ogs[l, :, j * 128:(j + 1) * 128].rearrange("(g r) x -> r g x", r=120), st120[0:120, :, :],
                            ["st120"], [], "st120")
                    else:
                        pt, pk = PS2, "ps2"
                        P.op("pe", lambda e, pt=pt, tq2=tq2: e.transpose(out=pt[:, 0:128], in_=tq2[:, 0:128], identity=ident[:]),
                             [tk2, "ident"], [pk])
                        cp("act", st120[0:30, 0, :], pt[0:30, 0:128], [pk], ["st120"])
                        dma("sp", ogp[l, :, j * 128:(j + 1) * 128], st120[0:30, 0, :], ["st120"], [], "st120")
                ov = v3(cv[:, j, 0:ntok], nseq)
                for q in range(31):
                    wcol = wdw[:, (l * 31 + q) * CT + j:(l * 31 + q) * CT + j + 1]
                    if q == 0:
                        ts("dve", ov, uv[:, :, 0:Tq], wcol, bdw[:, l * CT + j:l * CT + j + 1], ALU.mult, ALU.add,
                           ["ubuf", "wdw", "bdw"], ["cv"])
                    else:
                        stt(ov, uv[:, :, q:q + Tq], wcol, ov, ALU.mult, ALU.add, ["ubuf", "wdw", "cv"], ["cv"])
                pst, pstk = nextpa()
                mm(pst[:, 0:ntok], onesC[:], cv[:, j, 0:ntok], True, True, ["cv", "onesC"], [pstk])
                if j == 0:
                    cp("act", mu[:, 0:ntok], pst[:, 0:ntok], [pstk], ["mu"])
                else:
                    tt("dve", mu[:, 0:ntok], mu[:, 0:ntok], pst[:, 0:ntok], ALU.add, ["mu", pstk], ["mu"])
                tq, tk = tmp512()
                act(tq[:, 0:ntok], cv[:, j, 0:ntok], AF.Square, ["cv"], [tk])
                pst, pstk = nextpa()
                mm(pst[:, 0:ntok], onesC[:], tq[:, 0:ntok], True, True, [tk, "onesC"], [pstk])
                if j == 0:
                    cp("act", rsc[:, 0:ntok], pst[:, 0:ntok], [pstk], ["rsc"])
                else:
                    tt("dve", rsc[:, 0:ntok], rsc[:, 0:ntok], pst[:, 0:ntok], ALU.add, ["rsc", pstk], ["rsc"])

            def head_front(h):
                hb = h % 2
                qkva, knqs, zs = qkvas[hb], knqss[hb], zss[hb]
                QK, KQ, ZK = "qkva%d" % hb, "knqs%d" % hb, "zs%d" % hb
                XS = 3 + Tq
                xv = xpre[:, :, 0:nseq * XS].rearrange("p c (s t) -> p c s t", s=nseq)
                if is_sample:
                    dma("sp", st48[0:NS * 3, :, :], sq[l, :, :].rearrange("r (c x) -> r c x", c=3)[:, :, h * 128:(h + 1) * 128],
                        [], ["st48"], "st48")
                    pt, pk = PS2, "ps2"
                    for c in range(3):
                        P.op("pe", lambda e, pt=pt, c=c: e.transpose(out=pt[:, c * 128:(c + 1) * 128], in_=st48[:, c, :],
                                                                     identity=ident[:]), ["st48", "ident"], [pk])
                    for c in range(3):
                        cp("act", xv[:, c, :, 0:3], pt[:, c * 128:c * 128 + 48].rearrange("p (s r) -> p s r", r=3),
                           [pk], ["xpre"])
                else:
                    for c in range(3):
                        cp("pool", xv[:, c, 0, 0:3], qtail[:, l, c * H + h, :], ["qtail"], ["xpre"])
                for c in range(3):
                    wt, wk = load_w(w_in_tile(l, c * DD + h * 128))
                    pt, pk = nextpa()
                    for k in range(KT):
                        mm(pt[:, 0:ntok], wt[:, k, :], hT[:, k, 0:ntok], k == 0, k == KT - 1, [wk, "hT"], [pk])
                    cp("act", xv[:, c, :, 3:3 + Tq], v3(pt[:, 0:ntok], nseq), [pk], ["xpre"])
                    if not is_sample:
                        cp("pool", qtail[:, l, c * H + h, :], xv[:, c, 0, Tq:Tq + 3], ["xpre"], ["qtail"])
                if last:
                    nr = nseq * 3
                    pt, pk = PS2, "ps2"
                    for c in range(3):
                        tq, tk = tmp512()
                        cp("pool", tq[:, 0:nr].rearrange("p (s r) -> p s r", r=3), xv[:, c, :, Tq:Tq + 3], ["xpre"], [tk])
                        P.op("pe", lambda e, pt=pt, c=c, tq=tq: e.transpose(
                            out=pt[:, c * 128:(c + 1) * 128], in_=tq[:, 0:128], identity=ident[:]), [tk, "ident"], [pk])
                    cp("act", st48[0:nr, :, :], pt[0:nr, 0:384].rearrange("p (c x) -> p c x", c=3), [pk], ["st48"])
                    od = (oqs if is_sample else oqp)[l, :, :].rearrange("r (c x) -> r c x", c=3)[:, :, h * 128:(h + 1) * 128]
                    dma("sp", od, st48[0:nr, :, :], ["st48"], [], "st48")
                wt, wk = load_w(w_in_tile(l, 3 * DD + h * 128))
                pt, pk = nextpa()
                for k in range(KT):
                    mm(pt[:, 0:ntok], wt[:, k, :], hT[:, k, 0:ntok], k == 0, k == KT - 1, [wk, "hT"], [pk])
                act(zs[:, 0:ntok], pt[:, 0:ntok], AF.Silu, [pk], [ZK])
                for c in range(3):
                    ct = c * H + h
                    ov = v3(qkva[:, c, 0:ntok], nseq)
                    for j in range(4):
                        wcol = wqc[:, (l * 4 + j) * 3 * H + ct:(l * 4 + j) * 3 * H + ct + 1]
                        if j == 0:
                            ts("dve", ov, xv[:, c, :, 0:Tq], wcol, None, ALU.mult, None, ["xpre", "wqc"], [QK])
                        else:
                            stt(ov, xv[:, c, :, j:j + Tq], wcol, ov, ALU.mult, ALU.add, ["xpre", "wqc", QK], [QK])
                    act(qkva[:, c, 0:ntok], qkva[:, c, 0:ntok], AF.Silu, [QK], [QK])
                for c in (1, 0):
                    tq, tk = tmp512()
                    act(tq[:, 0:ntok], qkva[:, c, 0:ntok], AF.Square, [QK], [tk])
                    mm(PS2[:, 0:ntok], ones1[:], tq[:, 0:ntok], True, True, [tk, "ones1"], ["ps2"])
                    act(tq[:, 0:ntok], PS2[:, 0:ntok], AF.Sqrt, ["ps2", "cst"], [tk], bias=cst[:, 1:2])
                    recip(tq[:, 0:ntok], tq[:, 0:ntok], [tk], [tk])
                    src = qkva[:, c, 0:ntok].rearrange("p (a b) -> p a b", b=128)
                    rv = tq[:, 0:ntok].rearrange("p (a b) -> p a b", b=128)
                    if c == 1:
                        tt("dve", knqs[:, 0:nt, 0:128], src, rv, ALU.mult, [QK, tk], [KQ])
                    else:
                        stt(knqs[:, 0:nt, 128:256], src, QSC, rv, ALU.mult, ALU.mult, [QK, tk], [KQ])
                if is_sample:
                    dma("sp", Sin, sd[l, :, h, :, :].rearrange("s p v -> p s v"), [], ["Sin"], "Sin")


            def tile_solve(h, tt_, sid):
                hb = h % 2
                qkva, knqs, zs = qkvas[hb], knqss[hb], zss[hb]
                QK, KQ, ZK = "qkva%d" % hb, "knqs%d" % hb, "zs%d" % hb
                tl, SS = tls[sid], str(sid)
                kdecv, kdk = (kdec16, "kdec16") if is_sample else (kdec2s[sid][:], "kdec2_%d" % sid)
                tsl = slice(tt_ * 128, (tt_ + 1) * 128)

                def nb():
                    nbi[sid] ^= 1
                    j = 2 * sid + nbi[sid]
                    return PR[j], "pr%d" % j
                tsl = slice(tt_ * 128, (tt_ + 1) * 128)
                gccol = gbt[:, tt_, h:h + 1]
                becol = gbt[:, tt_, H + h:H + h + 1]
                pa_, pak = nb()
                mm(pa_[:, 0:128], EH[:, h:h + 1].broadcast_to([128, 128]), gbT[:, tt_, :], True, True, ["EH", "gbT"], [pak])
                mm(pa_[:, 128:256], EH[:, H + h:H + h + 1].broadcast_to([128, 128]), gbT[:, tt_, :], True, True, ["EH", "gbT"], [pak])
                act(tl["eg"][:], pa_[:, 0:128], AF.Exp, [pak], [("eg" + SS)])
                cp("act", tl["bb"][:], pa_[:, 128:256], [pak], [("bb" + SS)])
                ts("dve", tl["dd"][:], pa_[:, 0:128], gccol, 0.0, ALU.subtract, ALU.min, [pak, "gbt"], [("dd" + SS)])
                act(tl["decT"][:], tl["dd"][:], AF.Exp, [("dd" + SS)], [("decT" + SS)])
                tt("pool", tl["dmC"][:], tl["decT"][:], grp.McT[:], ALU.mult, [("decT" + SS)] + MK, [("dmC" + SS)])
                tt("pool", tl["dmS"][:], tl["decT"][:], grp.MsT[:], ALU.mult, [("decT" + SS)] + MK, [("dmS" + SS)])
                tt("dve", tl["qG"][:], knqs[:, tt_, 128:256], tl["eg"][:], ALU.mult, [KQ, ("eg" + SS)], [("qG" + SS)])
                tt("dve", tl["kgT"][:], knqs[:, tt_, 0:128], tl["eg"][:], ALU.mult, [KQ, ("eg" + SS)], [("kgT" + SS)])
                pb_, pbk = nb()
                mm(pb_[:, 0:256], knqs[:, tt_, 0:128], knqs[:, tt_, :], True, True, [KQ], [pbk])
                tt("dve", tl["attnT"][:], pb_[:, 128:256], tl["dmC"][:], ALU.mult, [pbk, ("dmC" + SS)], [("attnT" + SS)])
                stt(tl["LTp"][:], pb_[:, 0:128], becol, tl["dmS"][:], ALU.mult, ALU.mult, [pbk, "gbt", ("dmS" + SS)], [("LTp" + SS)])
                pc_, pck = nb()
                P.op("pe", lambda e, pc_=pc_: e.transpose(out=pc_[:, 0:128].bitcast(F32R), in_=tl["LTp"][:], identity=identR[:]),
                     [("LTp" + SS), "identR"], [pck])
                cp("act", tl["A0"][:], pc_[:, 0:128], [pck], [("A0" + SS)])
                tt("pool", tl["PT"][:], ident[:], tl["LTp"][:], ALU.subtract, ["ident", ("LTp" + SS)], [("PT" + SS)])
                A, AT, Ak, ATk = tl["A0"], tl["LTp"], ("A0" + SS), ("LTp" + SS)
                for kk in range(1, nd + 1):
                    An, Ank = (tl["A1"], ("A1" + SS)) if (kk % 2) else (tl["A0"], ("A0" + SS))
                    ATn, ATnk = (tl["AT1"], ("AT1" + SS)) if (kk % 2) else (tl["AT0"], ("AT0" + SS))
                    p1, p1k = nb()
                    mm(p1[:, 0:128], AT[:], A[:], True, True, [Ak, ATk], [p1k])
                    if kk < nd:
                        p2, p2k = nb()
                        mm(p2[:, 0:128], A[:], AT[:], True, True, [Ak, ATk], [p2k])
                    cp("act", An[:], p1[:, 0:128], [p1k], [Ank])
                    if kk < nd:
                        cp("dve", ATn[:], p2[:, 0:128], [p2k], [ATnk])
                    p3, p3k = nb()
                    mm(p3[:, 0:128], An[:], tl["PT"][:], True, True, [Ank, ("PT" + SS)], [p3k])
                    tt("dve", tl["PT"][:], tl["PT"][:], p3[:, 0:128], ALU.add, [("PT" + SS), p3k], [("PT" + SS)])
                    A, AT, Ak, ATk = An, ATn, Ank, ATnk
                p1, p1k = nb()
                P.op("pe", lambda e, p1=p1: e.transpose(out=p1[:, 0:128], in_=tl["kgT"][:], identity=ident[:]),
                     [("kgT" + SS), "ident"], [p1k])
                cp("act", tl["kgtok"][:], p1[:, 0:128], [p1k], [("kgtok" + SS)])
                p2, p2k = nb()
                P.op("pe", lambda e, p2=p2, tsl=tsl: e.transpose(out=p2[:, 0:128], in_=qkva[:, 2, tsl], identity=ident[:]),
                     [QK, "ident"], [p2k])
                cp("dve", tl["vtok"][:], p2[:, 0:128], [p2k], [("vtok" + SS)])
                p3, p3k = nb()
                P.op("pe", lambda e, p3=p3, tt_=tt_: e.transpose(out=p3[:, 0:128], in_=knqs[:, tt_, 0:128], identity=ident[:]),
                     [KQ, "ident"], [p3k])
                for i in range(nseg):
                    if i % 2 == 0:
                        ts("dve", kdecv[:, i, :], p3[:, 0:128], e2m[:, tt_, i, h:h + 1], None, ALU.mult, None,
                           [p3k, "e2m"], [kdk])
                    else:
                        act(kdecv[:, i, :], p3[:, 0:128], AF.Identity, [p3k, "e2m"], [kdk], scale=e2m[:, tt_, i, h:h + 1])
                p4, p4k = nb()
                mm(p4[:, 0:128], tl["kgtok"][:], tl["PT"][:], True, True, [("kgtok" + SS), ("PT" + SS)], [p4k])
                act(tl["Wn"][:], p4[:, 0:128], AF.Identity, [p4k], [("Wn" + SS)], scale=-1.0)

            def tile_state(h, tt_, sid):
                hb = h % 2
                qkva, knqs, zs = qkvas[hb], knqss[hb], zss[hb]
                QK, KQ, ZK = "qkva%d" % hb, "knqs%d" % hb, "zs%d" % hb
                tl, SS = tls[sid], str(sid)
                kdecv, kdk = (kdec16, "kdec16") if is_sample else (kdec2s[sid][:], "kdec2_%d" % sid)
                tsl = slice(tt_ * 128, (tt_ + 1) * 128)

                def nb():
                    nbi[sid] ^= 1
                    j = 2 * sid + nbi[sid]
                    return PR[j], "pr%d" % j
                pv_, pvk = (PA[1], "pa1")
                mm(pv_[:, 0:128], tl["vtok"][:], tl["PT"][:], True, True, [("vtok" + SS), ("PT" + SS)], [pvk])
                cp("act", tl["UT0"][:], pv_[:, 0:128], [pvk], ["UT0"])
                po_, pok = PS3, "ps3"
                for i in range(nseg):
                    a, b = i * seglen, (i + 1) * seglen
                    if is_sample:
                        s_in, s_out, sik, sok = Sin[:, i, :], Sout[:, i, :], "Sin", "Sout"
                    elif i % 2 == 0:
                        s_in, s_out, sik, sok = Sp[:, l, h, :], Stmp[:], "Sp", "Stmp"
                    else:
                        s_in, s_out, sik, sok = Stmp[:], Sp[:, l, h, :], "Stmp", "Sp"
                    if (not is_sample) or i == 0:
                        pw_, pwk = (PA[1], "pa1")
                    mm(pw_[:, a:b], s_in, tl["Wn"][:, a:b], True, True, [sik, ("Wn" + SS)], [pwk])
                    mm(po_[:, a:b], s_in, tl["qG"][:, a:b], True, True, [sik, ("qG" + SS)], [pok])
                    if (not is_sample) or i == nseg - 1:
                        a0 = a if not is_sample else 0
                        tt("dve", tl["UT"][:, a0:b], pw_[:, a0:b], tl["UT0"][:, a0:b], ALU.add, [pwk, "UT0"], ["UT"])
                        tt("dve", tl["UT"][:, a0:b], tl["UT"][:, a0:b], tl["bb"][:, a0:b], ALU.mult, ["UT", ("bb" + SS)], ["UT"])
                        p5, p5k = (PA[1], "pa1")
                        P.op("pe", lambda e, p5=p5: e.transpose(out=p5[:, 0:128], in_=tl["UT"][:], identity=ident[:]),
                             ["UT", "ident"], [p5k])
                        cp("act", tl["Utok"][:], p5[:, 0:128], [p5k], ["Utok"])
                    if not is_sample:
                        p6, p6k = (PA[1], "pa1")
                        mm(p6[:, 0:128], kdecv[:, i, :], tl["Utok"][:], True, True, [kdk, "Utok"], [p6k])
                        stt(s_out, s_in, tl["eg"][:, b - 1:b], p6[:, 0:128], ALU.mult, ALU.add, [sik, ("eg" + SS), p6k], [sok])
                if is_sample:
                    for i in range(nseg):
                        b = (i + 1) * seglen
                        p6, p6k = (PA[1], "pa1")
                        mm(p6[:, 0:128], kdecv[:, i, :], tl["Utok"][:].bitcast(F32), True, True, [kdk, "Utok"], [p6k])
                        stt(Sout[:, i, :], Sin[:, i, :], tl["eg"][:, b - 1:b], p6[:, 0:128], ALU.mult, ALU.add,
                            ["Sin", ("eg" + SS), p6k], ["Sout"])
                cp("act", oT[:, tsl], po_[:, 0:128], [pok], ["oT"])
                p7, p7k = (PA[1], "pa1")
                mm(p7[:, 0:128], tl["Utok"][:], tl["attnT"][:], True, True, ["Utok", ("attnT" + SS)], [p7k])
                tt("dve", oT[:, tsl], oT[:, tsl], p7[:, 0:128], ALU.add, ["oT", p7k], ["oT"])


            def head_back(h):
                hb = h % 2
                qkva, knqs, zs = qkvas[hb], knqss[hb], zss[hb]
                QK, KQ, ZK = "qkva%d" % hb, "knqs%d" % hb, "zs%d" % hb
                if is_sample:
                    dma("sp", oSs[l, :, h, :, :].rearrange("s p v -> p s v"), Sout, ["Sout"], [], "Sout")
                elif last:
                    dma("sp", oSp[l, h, :, :], Sp[:, l, h, :], ["Sp"], [], "Sp")
                tq, tk = tmp512()
                act(tq[:, 0:ntok], oT[:, 0:ntok], AF.Square, ["oT"], [tk])
                mm(PS2[:, 0:ntok], onesH[:], tq[:, 0:ntok], True, True, [tk, "onesH"], ["ps2"])
                act(tq[:, 0:ntok], PS2[:, 0:ntok], AF.Sqrt, ["ps2", "cst"], [tk], bias=cst[:, 1:2])
                recip(tq[:, 0:ntok], tq[:, 0:ntok], [tk], [tk])
                tt("dve", tq[:, 0:ntok], tq[:, 0:ntok], oT[:, 0:ntok], ALU.mult, [tk, "oT"], [tk])
                stt(mixT[:, h, 0:ntok], tq[:, 0:ntok], hng[:, l:l + 1], zs[:, 0:ntok], ALU.mult, ALU.mult,
                    [tk, "hng", ZK], ["mixT"])


            def record(fn, h):
                P.rec = []
                fn(h)
                r, P.rec = P.rec, None
                return r

            def replay(ops):
                for o in ops:
                    P.op(*o)

            def merge2(x, y):
                if not y:
                    return list(x)
                if not x:
                    return list(y)
                out_, yi = [], 0
                for xi, o in enumerate(x):
                    out_.append(o)
                    want = (xi + 1) * len(y) // len(x)
                    while yi < want:
                        out_.append(y[yi])
                        yi += 1
                out_.extend(y[yi:])
                return out_

            def rec2(fn, *a):
                P.rec = []
                fn(*a)
                r, P.rec = P.rec, None
                return r

            pa_front[0] = True
            replay(record(head_front, 0))
            pa_front[0] = False
            for h in range(H):
                S_ = [rec2(tile_solve, h, t, t % 2) for t in range(nt)]
                Q_ = [rec2(tile_state, h, t, t % 2) for t in range(nt)]
                if nt == 1:
                    t_ops = S_[0] + Q_[0]
                else:
                    t_ops = merge2(S_[0], S_[1]) + Q_[0]
                    for t in range(1, nt):
                        t_ops += merge2(Q_[t], S_[t + 1]) if t + 1 < nt else Q_[t]
                pa_front[0] = True
                f_ops = record(head_front, h + 1) if h + 1 < H else []
                if not is_sample:
                    f_ops = f_ops + record(conf_part1, h)
                pa_front[0] = False
                replay(merge2(t_ops, f_ops))
                head_back(h)
            if is_sample:
                for j in range(CT):
                    conf_part1(j)

            tq, tk = tmp512()
            tt("dve", tq[:, 0:ntok], mu[:, 0:ntok], mu[:, 0:ntok], ALU.mult, ["mu"], [tk])
            tt("dve", rsc[:, 0:ntok], rsc[:, 0:ntok], tq[:, 0:ntok], ALU.subtract, ["rsc", tk], ["rsc"])
            act(rsc[:, 0:ntok], rsc[:, 0:ntok], AF.Sqrt, ["rsc", "cst"], ["rsc"], bias=cst[:, 1:2])
            recip(rsc[:, 0:ntok], rsc[:, 0:ntok], ["rsc"], ["rsc"])
            for j in range(CT):
                wz, wzk = load_w(w_in_tile(l, 4 * DD + H2 + 2 * DC + j * 128))
                pa_, pak = nextpa()
                for k in range(KT):
                    mm(pa_[:, 0:ntok], wz[:, k, :], hT[:, k, 0:ntok], k == 0, k == KT - 1, [wzk, "hT"], [pak])
                tz, tzk = tmp512()
                act(tz[:, 0:ntok], pa_[:, 0:ntok], AF.Silu, [pak], [tzk])
                tq, tk = tmp512()
                tt("dve", tq[:, 0:ntok], cv[:, j, 0:ntok], mu[:, 0:ntok], ALU.subtract, ["cv", "mu"], [tk])
                tt("dve", tq[:, 0:ntok], tq[:, 0:ntok], rsc[:, 0:ntok], ALU.mult, [tk, "rsc"], [tk])
                act(tq[:, 0:ntok], tq[:, 0:ntok], AF.Silu, [tk, "lng", "lnb"], [tk],
                    scale=lng[:, l * CT + j:l * CT + j + 1], bias=lnb[:, l * CT + j:l * CT + j + 1])
                tt("dve", mixT[:, H + j, 0:ntok], tq[:, 0:ntok], tz[:, 0:ntok], ALU.mult, [tk, tzk], ["mixT"])

            for m in range(KT):
                wt, wk = load_w(w_out[l, :, m * 128:(m + 1) * 128].rearrange("(k p) c -> p k c", p=128))
                pt, pk = nextpa()
                for e_ in range(KT):
                    mm(pt[:, 0:ntok], wt[:, e_, :], mixT[:, e_, 0:ntok], e_ == 0, e_ == KT - 1, [wk, "mixT"], [pk])
                if nseq == 1:
                    stt(xT[:, m, 0:ntok], pt[:, 0:ntok], Gmod[:, l, m, 0:1], xT[:, m, 0:ntok], ALU.mult, ALU.add,
                        [pk, "Gmod", "xT"], ["xT"])
                else:
                    tq, tk = tmp512()
                    tt("dve", v3(tq[:, 0:ntok], nseq), v3(pt[:, 0:ntok], nseq), bc(Gmod, l, m), ALU.mult, [pk, "Gmod"], [tk])
                    tt("dve", xT[:, m, 0:ntok], xT[:, m, 0:ntok], tq[:, 0:ntok], ALU.add, ["xT", tk], ["xT"])

        rms_stats(rs)
        for k in range(KT):
            stt(xT[:, k, 0:ntok], xT[:, k, 0:ntok], fg[:, k:k + 1], rs[:, 0:ntok], ALU.mult, ALU.mult,
                ["xT", "fg", "rs"], ["xT"])
        for tt_ in range(nt):
            for k4 in range(0, KT, 4):
                n4 = min(4, KT - k4)
                pt, pk = nextpa()
                for q in range(n4):
                    P.op("pe", lambda e, pt=pt, q=q, k4=k4, tt_=tt_: e.transpose(
                        out=pt[:, q * 128:(q + 1) * 128], in_=xT[:, k4 + q, tt_ * 128:(tt_ + 1) * 128],
                        identity=ident[:]), ["xT", "ident"], [pk])
                cp("act", xio[:, k4 * 128:(k4 + n4) * 128], pt[:, 0:n4 * 128], [pk], ["xio"])
            dma("sp", ydst[tt_ * 128:(tt_ + 1) * 128, :], xio[:], ["xio"], [], "xio")

    try:
        for bi in range(NB):
            do_block(False, bi)
        do_block(True, 0)
    except StopBuild:
        pass
    P.emit()
    st.close()
    return nc


_NC_CACHE = {}


def _get_nc(cfg_key):
    if cfg_key not in _NC_CACHE:
        _NC_CACHE[cfg_key] = build(Cfg(*cfg_key))
    return _NC_CACHE[cfg_key]


def make_in_maps(inp, cfg, ncores):
    D, L, KT, H, DD, DC, CT = cfg.D, cfg.L, cfg.KT, cfg.H, cfg.DD, cfg.DC, cfg.CT
    f = lambda a: np.ascontiguousarray(np.asarray(a, dtype=np.float32))
    Bp = inp["x_prompt"].shape[0]
    shared = {
        "norm_g": f(inp["norm_g"]).reshape(L * KT, 128),
        "w_ada": f(inp["w_ada"]),
        "b_ada": f(inp["b_ada"]).reshape(L * 3 * KT, 128),
        "w_in": f(inp["w_in"]),
        "w_qc": f(inp["w_qkv_conv"]).reshape(L * 4 * 3 * H, 128),
        "a_log": f(inp["a_log"]).reshape(1, L * H),
        "dt_bias": f(inp["dt_bias"]).reshape(1, L * H),
        "hn_g": f(inp["head_norm_g"]).reshape(L, 128),
        "w_dw": f(inp["w_dw"]).reshape(L * 31 * CT, 128),
        "b_dw": f(inp["b_dw"]).reshape(L * CT, 128),
        "ln_g": f(inp["ln_g"]).reshape(L * CT, 128),
        "ln_b": f(inp["ln_b"]).reshape(L * CT, 128),
        "w_out": f(inp["w_out"]),
        "final_g": f(inp["final_g"]).reshape(KT, 128),
    }
    maps = []
    NS = cfg.NS
    for c in range(ncores):
        b = c % Bp
        s0 = c * NS
        m = dict(shared)
        m["xp"] = f(inp["x_prompt"][b])
        m["xs"] = f(inp["x_sample"][s0:s0 + NS]).reshape(NS * cfg.TS, D)
        m["cc"] = f(np.concatenate([np.asarray(inp["c_prompt"])[b:b + 1], np.asarray(inp["c_sample"])[s0:s0 + NS]], axis=0))
        m["sd"] = f(np.asarray(inp["state_delta"])[:, s0:s0 + NS])
        m["sq"] = f(np.asarray(inp["state_qkv_conv"])[:, s0:s0 + NS]).reshape(L, NS * 3, 3 * DD)
        m["sg"] = f(np.asarray(inp["state_glu_conv"])[:, s0:s0 + NS]).reshape(L, NS * 30, DC)
        maps.append(m)
    return maps


def assemble(res, cfg, ncores, Bp):
    D, L, H, DD, DC, NS, TS = cfg.D, cfg.L, cfg.H, cfg.DD, cfg.DC, cfg.NS, cfg.TS
    g = lambda c, n: np.asarray(res[c][n], dtype=np.float32)
    y_prompt = np.stack([g(b, "yp") for b in range(Bp)])
    y_sample = np.concatenate([g(c, "ys").reshape(NS, TS, D) for c in range(ncores)], axis=0)
    Sp = np.stack([g(b, "oSp") for b in range(Bp)], axis=1)
    qp = np.stack([g(b, "oqp") for b in range(Bp)], axis=1)
    gp = np.stack([g(b, "ogp") for b in range(Bp)], axis=1)
    Ss = np.concatenate([g(c, "oSs") for c in range(ncores)], axis=1)
    qs = np.concatenate([g(c, "oqs").reshape(L, NS, 3, 3 * DD) for c in range(ncores)], axis=1)
    gs = np.concatenate([g(c, "ogs").reshape(L, NS, 30, DC) for c in range(ncores)], axis=1)
    return (y_prompt, y_sample, Sp, qp, gp, Ss, qs, gs)


def kernel(**inputs):
    ncores = 8
    cfg_key = (2048, 2048, 512, 2, BF16)
    cfg = Cfg(*cfg_key)
    nc = _get_nc(cfg_key)
    maps = make_in_maps(inputs, cfg, ncores)
    res = run_bass_kernel_spmd(nc, maps, core_ids=list(range(ncores)))
    return assemble(res.results, cfg, ncores, 4)
```

```python
import contextlib
import math
import numpy as np
import concourse.bass as bass
import concourse.mybir as mybir
from concourse.bass_utils import run_bass_kernel_spmd

F32 = mybir.dt.float32
BF16 = mybir.dt.bfloat16
F32R = mybir.dt.float32r
ALU = mybir.AluOpType
AF = mybir.ActivationFunctionType
EPS = 1e-6
ENGS = ("pe", "act", "dve", "pool", "sp")


class Ins:
    __slots__ = ("eng", "fn", "deps", "sig", "cnt", "dma_key", "dma_gen")

    def __init__(self, eng, fn):
        self.eng = eng
        self.fn = fn
        self.deps = []
        self.sig = False
        self.cnt = 0
        self.dma_key = None
        self.dma_gen = 0


class Prog:
    def __init__(self, nc):
        self.nc = nc
        self.lists = {e: [] for e in ENGS}
        self.last_w = {}
        self.readers = {}
        self.dma_gen = {}
        self.all = []
        self.rec = None

    PSUM_KEYS = frozenset(["pa0", "pa1", "ps2", "ps3", "pr0", "pr1", "pr2", "pr3"])

    def op(self, eng, fn, reads=(), writes=(), dma_key=None):
        if self.rec is not None:
            self.rec.append((eng, fn, tuple(reads), tuple(writes), dma_key))
            return None
        ins = Ins(eng, fn)
        writes = list(writes) + [k for k in reads if k in self.PSUM_KEYS and k not in writes]
        deps = []
        for k in reads:
            w = self.last_w.get(k)
            if w is not None:
                deps.append(w)
        for k in writes:
            w = self.last_w.get(k)
            if w is not None:
                deps.append(w)
            deps.extend(self.readers.get(k, {}).values())
        seen = set()
        for d in deps:
            if id(d) in seen:
                continue
            seen.add(id(d))
            if d.eng == "pe" and eng == "pe" and d.dma_key is None and dma_key is None:
                continue
            ins.deps.append(d)
        if dma_key is not None:
            g = self.dma_gen.get(dma_key, 0) + 1
            self.dma_gen[dma_key] = g
            ins.dma_key = dma_key
            ins.dma_gen = g
        slot = eng if dma_key is None else ("dma", dma_key)
        for k in reads:
            self.readers.setdefault(k, {})[slot] = ins
        for k in writes:
            self.last_w[k] = ins
            self.readers[k] = {}
        self.lists[eng].append(ins)
        self.all.append(ins)
        return ins

    def emit(self, final_wait_eng="sp"):
        nc = self.nc
        for ins in self.all:
            for d in ins.deps:
                if d.dma_key is None:
                    d.sig = True
        nsig = {}
        for e in ENGS:
            c = 0
            for ins in self.lists[e]:
                if ins.dma_key is None and ins.sig:
                    c += 1
                    ins.cnt = c
            nsig[e] = c
        dma_keys = sorted(self.dma_gen.keys(), key=str)
        with contextlib.ExitStack() as st:
            esem = {e: st.enter_context(nc.semaphore("s_" + e)) for e in ENGS}
            dsem = {k: st.enter_context(nc.semaphore("d%d" % i)) for i, k in enumerate(dma_keys)}
            block = st.enter_context(nc.Block())
            hw = {"pe": block.tensor, "act": block.scalar, "dve": block.vector,
                  "pool": block.gpsimd, "sp": block.sync}

            def make(e):
                def body(engine):
                    waited = {}
                    for ins in self.lists[e]:
                        for d in ins.deps:
                            if d.dma_key is not None:
                                s, v = dsem[d.dma_key], 16 * d.dma_gen
                            else:
                                s, v = esem[d.eng], d.cnt
                            if waited.get(id(s), 0) >= v:
                                continue
                            waited[id(s)] = v
                            engine.wait_ge(s, v)
                        r = ins.fn(engine)
                        if ins.dma_key is not None:
                            r.then_inc(dsem[ins.dma_key], 16)
                        elif ins.sig:
                            r.then_inc(esem[e], 1)
                    if e == final_wait_eng:
                        for k in dma_keys:
                            engine.wait_ge(dsem[k], 16 * self.dma_gen[k])
                        for e2 in ENGS:
                            if nsig[e2] and e2 != e:
                                engine.wait_ge(esem[e2], nsig[e2])
                return body

            for e in ENGS:
                if self.lists[e] or e == final_wait_eng:
                    hw[e](make(e))


class StopBuild(Exception):
    pass


class Cfg:
    dbg = 0

    def __init__(self, D=2048, T=2048, TB=512, L=2, wdt=BF16):
        self.D, self.T, self.TB, self.L = D, T, TB, L
        self.KT = D // 128
        self.H = D // 256
        self.DD = D // 2
        self.DC = D // 2
        self.CT = self.DC // 128
        self.NIN = 4 * self.DD + 2 * self.H + 3 * self.DC
        self.NB = T // TB
        self.NS = 16
        self.TS = 8
        self.wdt = wdt


def build(cfg):
    D, T, TB, L, KT, H, DD, DC, CT, NIN, NB = (cfg.D, cfg.T, cfg.TB, cfg.L, cfg.KT, cfg.H, cfg.DD,
                                                cfg.DC, cfg.CT, cfg.NIN, cfg.NB)
    WDT = cfg.wdt
    NS, TS = cfg.NS, cfg.TS
    NTB = TB // 128
    H2 = 2 * H
    QSC = 128.0 ** -0.5
    nc = bass.Bass("TRN2", target_bir_lowering=False)

    def din(name, shape):
        return nc.dram_tensor(name, list(shape), F32, kind="ExternalInput").ap()

    def dout(name, shape):
        return nc.dram_tensor(name, list(shape), F32, kind="ExternalOutput").ap()

    xp = din("xp", [T, D])
    xs = din("xs", [128, D])
    cc = din("cc", [1 + NS, D])
    sd = din("sd", [L, NS, H, 128, 128])
    sq = din("sq", [L, NS * 3, 3 * DD])
    sg = din("sg", [L, NS * 30, DC])
    norm_g = din("norm_g", [L * KT, 128])
    w_ada = din("w_ada", [L, D, 3 * D])
    b_ada = din("b_ada", [L * 3 * KT, 128])
    w_in = din("w_in", [L, D, NIN])
    w_qc = din("w_qc", [L * 4 * 3 * H, 128])
    a_log = din("a_log", [1, L * H])
    dt_bias = din("dt_bias", [1, L * H])
    hn_g = din("hn_g", [L, 128])
    w_dw = din("w_dw", [L * 31 * CT, 128])
    b_dw = din("b_dw", [L * CT, 128])
    ln_g = din("ln_g", [L * CT, 128])
    ln_b = din("ln_b", [L * CT, 128])
    w_out = din("w_out", [L, D, D])
    final_g = din("final_g", [KT, 128])

    yp = dout("yp", [T, D])
    ys = dout("ys", [128, D])
    oSp = dout("oSp", [L, H, 128, 128])
    oqp = dout("oqp", [L, 3, 3 * DD])
    ogp = dout("ogp", [L, 30, DC])
    oSs = dout("oSs", [L, NS, H, 128, 128])
    oqs = dout("oqs", [L, NS * 3, 3 * DD])
    ogs = dout("ogs", [L, NS * 30, DC])

    st = contextlib.ExitStack()
    P = Prog(nc)

    def sb(name, shape, dt=F32):
        return st.enter_context(nc.sbuf_tensor(name, list(shape), dt))

    def psb(name):
        return st.enter_context(nc.psum_tensor(name, [128, 512], F32))

    def mm(out, lhsT, rhs, start, stop, r, w, skip=False):
        return P.op("pe", lambda e: e.matmul(out, lhsT=lhsT, rhs=rhs, start=start, stop=stop,
                                             skip_group_check=skip), r, w)

    def act(out, in_, func, r, w, scale=1.0, bias=None):
        if bias is None:
            return P.op("act", lambda e: e.activation(out=out, in_=in_, func=func, scale=scale), r, w)
        return P.op("act", lambda e: e.activation(out=out, in_=in_, func=func, scale=scale, bias=bias), r, w)

    def ts(eng, out, in0, s1, s2, op0, op1, r, w):
        if op1 is None:
            return P.op(eng, lambda e: e.tensor_scalar(out=out, in0=in0, scalar1=s1, scalar2=None, op0=op0), r, w)
        return P.op(eng, lambda e: e.tensor_scalar(out=out, in0=in0, scalar1=s1, scalar2=s2, op0=op0, op1=op1), r, w)

    def tt(eng, out, in0, in1, op, r, w):
        return P.op(eng, lambda e: e.tensor_tensor(out=out, in0=in0, in1=in1, op=op), r, w)

    def stt(out, in0, scalar, in1, op0, op1, r, w):
        return P.op("dve", lambda e: e.scalar_tensor_tensor(out=out, in0=in0, scalar=scalar, in1=in1,
                                                            op0=op0, op1=op1), r, w)

    def cp(eng, out, in_, r, w):
        if eng == "act":
            return P.op("act", lambda e: e.copy(out=out, in_=in_), r, w)
        return P.op(eng, lambda e: e.tensor_copy(out=out, in_=in_), r, w)

    def dma(eng, out, in_, r, w, key):
        return P.op(eng, lambda e: e.dma_start(out=out, in_=in_), r, w, dma_key=key)

    def recip(out, in_, r, w):
        return P.op("dve", lambda e: e.reciprocal(out=out, in_=in_), r, w)

    def asel(out, in_, pattern, op, fill, base, cm, r, w):
        return P.op("pool", lambda e: e.affine_select(out=out, in_=in_, pattern=pattern, compare_op=op,
                                                      fill=fill, base=base, channel_multiplier=cm), r, w)

    def memset(eng, out, val, w):
        return P.op(eng, lambda e: e.memset(out, val), (), w)

    PA = [psb("pa0"), psb("pa1")]
    PS2 = psb("ps2")
    PS3 = psb("ps3")
    PR = [psb("pr%d" % i) for i in range(4)]
    pa_i = [0]
    pr_i = [0]

    pa_front = [False]

    def nextpa():
        if pa_front[0]:
            return PA[0], "pa0"
        pa_i[0] ^= 1
        t = PA[pa_i[0]]
        return t, "pa%d" % pa_i[0]

    def nextpr():
        pr_i[0] = (pr_i[0] + 1) % 4
        return PR[pr_i[0]], "pr%d" % pr_i[0]

    ident = sb("ident", [128, 128])
    cst = sb("cst", [128, 4])
    onesD = sb("onesD", [128, 128])
    onesC = sb("onesC", [128, 128])
    onesH = sb("onesH", [128, 128])
    ones1 = sb("ones1", [128, 128])
    EHf = sb("EHf", [128, H2])
    EH = sb("EH", [128, H2], F32R)
    identR = sb("identR", [128, 128], F32R)
    memset("pool", ident[:], 0.0, ["ident"])
    asel(ident[:], ident[:], [[-1, 128]], ALU.not_equal, 1.0, 0, 1, ["ident"], ["ident"])
    cp("dve", identR[:], ident[:], ["ident"], ["identR"])
    memset("pool", cst[:, 0:1], 1.0, ["cst"])
    memset("pool", cst[:, 1:2], EPS, ["cst"])
    memset("pool", cst[:, 2:3], 0.0, ["cst"])
    memset("pool", cst[:, 3:4], -1.0, ["cst"])
    memset("pool", onesD[:], 1.0 / D, ["onesD"])
    memset("pool", onesC[:], 1.0 / DC, ["onesC"])
    memset("pool", onesH[:], 1.0 / 128, ["onesH"])
    memset("pool", ones1[:], 1.0, ["ones1"])
    memset("pool", EHf[:], 0.0, ["EHf"])
    asel(EHf[:], EHf[:], [[-1, H2]], ALU.not_equal, 1.0, 0, 1, ["EHf"], ["EHf"])
    cp("dve", EH[:], EHf[:], ["EHf"], ["EH"])

    class Grp:
        pass

    def make_masks(name, seglen):
        nseg = 128 // seglen
        g = Grp()
        g.seglen, g.nseg = seglen, nseg
        g.McT = sb(name + "McT", [128, 128])
        g.MsT = sb(name + "MsT", [128, 128])
        g.SegAll = sb(name + "Seg", [128, 128])
        g.rowm = sb(name + "rowm", [128, nseg])
        g.key = name + "masks"
        k = [g.key]
        for m, off in ((g.McT, 0), (g.MsT, -1)):
            memset("pool", m[:], 1.0, k)
            asel(m[:], m[:], [[1, 128]], ALU.is_ge, 0.0, off, -1, k, k)
            v = m[:].rearrange("p (a b) -> p a b", b=seglen)
            asel(v, v, [[-seglen, nseg], [0, seglen]], ALU.is_ge, 0.0, 0, 1, k, k)
        memset("pool", g.SegAll[:], 1.0, k)
        v = g.SegAll[:].rearrange("p (a b) -> p a b", b=seglen)
        asel(v, v, [[-seglen, nseg], [0, seglen]], ALU.is_ge, 0.0, 0, 1, k, k)
        asel(v, v, [[seglen, nseg], [0, seglen]], ALU.is_ge, 0.0, seglen - 1, -1, k, k)
        memset("pool", g.rowm[:], 1.0, k)
        asel(g.rowm[:], g.rowm[:], [[-seglen, nseg]], ALU.is_ge, 0.0, 0, 1, k, k)
        asel(g.rowm[:], g.rowm[:], [[seglen, nseg]], ALU.is_ge, 0.0, seglen - 1, -1, k, k)
        return g

    GP = make_masks("p", 64)
    GS = make_masks("s", 8)

    stage = sb("stage", [128, 128])
    memset("pool", stage[:], 0.0, ["stage"])

    def load_cols(name, src, nrows):
        dst = sb(name, [128, nrows])
        for r0 in range(0, nrows, 128):
            n = min(128, nrows - r0)
            dma("sp", stage[0:n, :], src[r0:r0 + n, :], [], ["stage"], "stage")
            pt, pk = nextpr()
            P.op("pe", lambda e, pt=pt: e.transpose(out=pt[:, 0:128], in_=stage[:, :], identity=ident[:]),
                 ["stage", "ident"], [pk])
            cp("dve", dst[:, r0:r0 + n], pt[:, 0:n], [pk], [name])
        return dst

    ng = load_cols("ng", norm_g, L * KT)
    bada = load_cols("bada", b_ada, L * 3 * KT)
    wqc = load_cols("wqc", w_qc, L * 4 * 3 * H)
    hng = load_cols("hng", hn_g, L)
    wdw = load_cols("wdw", w_dw, L * 31 * CT)
    bdw = load_cols("bdw", b_dw, L * CT)
    lng = load_cols("lng", ln_g, L * CT)
    lnb = load_cols("lnb", ln_b, L * CT)
    fg = load_cols("fg", final_g, KT)

    nea = sb("nea", [128, L * H])
    dtb = sb("dtb", [128, L * H])
    dma("sp", nea[:], a_log.partition_broadcast(128), [], ["nea"], "nea")
    dma("sp", dtb[:], dt_bias.partition_broadcast(128), [], ["dtb"], "dtb")
    act(nea[:], nea[:], AF.Exp, ["nea"], ["nea"])
    ts("dve", nea[:], nea[:], -1.0, None, ALU.mult, None, ["nea"], ["nea"])

    NWS = 3
    wsl = [sb("wsl%d" % i, [128, KT, 128], WDT) for i in range(NWS)]
    ws_i = [0]

    def load_w(src3):
        ws_i[0] = (ws_i[0] + 1) % NWS
        i = ws_i[0]
        key = "wsl%d" % i
        dma("pool", wsl[i][:], src3, [], [key], key)
        return wsl[i], key

    def w_in_tile(l, col0):
        return w_in[l, :, col0:col0 + 128].rearrange("(k p) c -> p k c", p=128)

    NC17 = 1 + NS
    xio = sb("xio", [128, D])
    ccs = xio[0:NC17, :]
    memset("pool", xio[:], 0.0, ["xio"])
    scT = sb("scT", [128, KT, NC17], WDT)
    Amod = sb("Amod", [128, L, KT, NC17])
    Bmod = sb("Bmod", [128, L, KT, NC17])
    Gmod = sb("Gmod", [128, L, KT, NC17])
    dma("sp", ccs, cc, [], ["xio"], "xio")
    act(ccs, ccs, AF.Silu, ["xio"], ["xio"])
    for k in range(KT):
        pt, pk = nextpr()
        P.op("pe", lambda e, pt=pt, k=k: e.transpose(out=pt[:, 0:128], in_=xio[:, k * 128:(k + 1) * 128],
                                                     identity=ident[:]), ["xio", "ident"], [pk])
        cp("dve", scT[:, k, :], pt[:, 0:NC17], [pk], ["scT"])
    for l in range(L):
        for j in range(3 * KT):
            wt, wk = load_w(w_ada[l, :, j * 128:(j + 1) * 128].rearrange("(k p) c -> p k c", p=128))
            pt, pk = nextpr()
            for k in range(KT):
                mm(pt[:, 0:NC17], wt[:, k, :], scT[:, k, :], k == 0, k == KT - 1, [wk, "scT"], [pk])
            bcol = bada[:, l * 3 * KT + j: l * 3 * KT + j + 1]
            if j < KT:
                ts("dve", Bmod[:, l, j, :], pt[:, 0:NC17], bcol, None, ALU.add, None, [pk, "bada"], ["Bmod"])
            elif j < 2 * KT:
                ts("dve", Amod[:, l, j - KT, :], pt[:, 0:NC17], bcol, 1.0, ALU.add, ALU.add, [pk, "bada"], ["Amod"])
                ts("dve", Amod[:, l, j - KT, :], Amod[:, l, j - KT, :], ng[:, l * KT + j - KT: l * KT + j - KT + 1],
                   None, ALU.mult, None, ["Amod", "ng"], ["Amod"])
            else:
                ts("dve", Gmod[:, l, j - 2 * KT, :], pt[:, 0:NC17], bcol, None, ALU.add, None, [pk, "bada"], ["Gmod"])

    cur_blk = [0]

    def ckpt(n, cond=True):
        if cfg.dbg == n and cond and cur_blk[0] == getattr(cfg, 'dbg_block', 0):
            raise StopBuild()

    Sp = sb("Sp", [128, L, H, 128])
    Stmp = sb("Stmp", [128, 128])
    qtail = sb("qtail", [128, L, 3 * H, 3])
    gtail = sb("gtail", [128, L, CT, 30])
    memset("pool", Sp[:], 0.0, ["Sp"])
    memset("pool", qtail[:], 0.0, ["qtail"])
    memset("pool", gtail[:], 0.0, ["gtail"])

    xTf = sb("xTf", [128, KT * TB])
    ALIAS = KT * (TB - 128) >= 3 * NS * 128
    hT = sb("hT", [128, KT, TB], WDT)
    mixT = sb("mixT", [128, KT, TB], WDT)
    cv = sb("cv", [128, CT, TB])
    t512 = [sb("t512_%d" % i, [128, max(TB, 512)]) for i in range(3)]
    for i in range(3):
        memset("pool", t512[i][:], 0.0, ["t512_%d" % i])
    t5_i = [0]

    def tmp512():
        t5_i[0] = (t5_i[0] + 1) % 3
        return t512[t5_i[0]], "t512_%d" % t5_i[0]

    rs = sb("rs", [128, TB])
    wab = sb("wab", [128, KT, H2], WDT)
    graw = sb("graw", [128, NTB, H])
    gbt = sb("gbt", [128, NTB, 128])
    memset("pool", gbt[:], 0.0, ["gbt"])
    e2t = sb("e2t", [128, NTB, H])
    e2m = sb("e2m", [128, NTB, 16, H])
    gtmp = sb("gtmp", [128, H2])
    gbT = sb("gbT", [128, NTB, 128], F32R)
    cp("dve", gbT[:].rearrange("p a b -> p (a b)"), t512[0][:, 0:NTB * 128], ["t512_0"], ["gbT"])
    XW = max(3 + TB, NS * (3 + TS))
    xpre = sb("xpre", [128, 3, XW])
    qkvas = [sb("qkva%d" % i, [128, 3, TB]) for i in range(2)]
    zss = [sb("zs%d" % i, [128, TB]) for i in range(2)]
    knqss = [sb("knqs%d" % i, [128, NTB, 256]) for i in range(2)]
    oT = sb("oT", [128, TB])
    UW = max(30 + TB, NS * (30 + TS))
    ubuf = sb("ubuf", [128, UW])
    mu = sb("mu", [128, TB])
    rsc = sb("rsc", [128, TB])
    names = ["eg", "bb", "qG", "kgT", "dd", "decT", "dmC", "dmS", "attnT", "LTp", "A0", "A1", "AT0", "AT1",
             "PT", "kgtok", "vtok", "Wn", "UT", "Utok", "UT0"]
    shared_t = ("UT", "Utok", "UT0")
    r_names = ("attnT", "LTp", "A0", "A1", "AT0", "AT1", "PT", "kgtok", "vtok", "Utok")
    tdt = lambda n: F32R if n in r_names else F32
    tls = [{n: sb(n + ("" if n in shared_t else "0"), [128, 128], tdt(n)) for n in names}]
    tls.append({n: (tls[0][n] if n in shared_t else sb(n + "1", [128, 128], tdt(n))) for n in names})
    tl = tls[0]
    nbi = [0, 0]
    kdec2s = [sb("kdec2_%d" % i, [128, 2, 128], F32R) for i in range(2)]
    if ALIAS:
        o0 = KT * 128
        Sin = xTf[:, o0:o0 + NS * 128].rearrange("p (s v) -> p s v", s=NS)
        Sout = xTf[:, o0 + NS * 128:o0 + 2 * NS * 128].rearrange("p (s v) -> p s v", s=NS)
        kdec16 = xTf[:, o0 + 2 * NS * 128:o0 + 3 * NS * 128].rearrange("p (s v) -> p s v", s=NS)
    else:
        Sin = sb("Sin", [128, NS, 128])[:]
        Sout = sb("Sout", [128, NS, 128])[:]
        kdec16 = sb("kdec16", [128, NS, 128])[:]
    st48 = sb("st48", [128, 3, 128])
    memset("pool", st48[:], 0.0, ["st48"])
    st120 = sb("st120", [128, 4, 128])
    memset("pool", st120[:], 0.0, ["st120"])
    memset("pool", tl["UT"][:], 0.0, ["UT"])
    cp("dve", tl["Utok"][:], t512[0][:, 0:128], ["t512_0"], ["Utok"])

    def v3(ap, nseq):
        return ap.rearrange("p (s t) -> p s t", s=nseq)

    def do_block(is_sample, bi):
        if is_sample:
            ntok, nseq, Tq, grp, c0, ncol = 128, NS, TS, GS, 1, NS
            xsrc, ydst = xs, ys
        else:
            ntok, nseq, Tq, grp, c0, ncol = TB, 1, TB, GP, 0, 1
            xsrc, ydst = xp[bi * TB:(bi + 1) * TB, :], yp[bi * TB:(bi + 1) * TB, :]
        last = is_sample or bi == NB - 1
        cur_blk[0] = NB if is_sample else bi
        nt = ntok // 128
        seglen, nseg = grp.seglen, grp.nseg
        nd = int(math.log2(seglen)) - 1
        MK = [grp.key]
        xT = xTf[:, 0:KT * ntok].rearrange("p (k t) -> p k t", k=KT)
        if is_sample and ALIAS:
            memset("pool", xTf[:, KT * 128:KT * 128 + 1], 0.0, ["xT", "Sin", "Sout", "kdec16"])

        def bc(modt, l, k):
            return modt[:, l, k, c0:c0 + ncol].unsqueeze(2).broadcast_to([128, nseq, Tq])

        for tt_ in range(nt):
            dma("sp", xio[:], xsrc[tt_ * 128:(tt_ + 1) * 128, :], [], ["xio"], "xio")
            for k4 in range(0, KT, 4):
                n4 = min(4, KT - k4)
                pt, pk = nextpa()
                for q in range(n4):
                    P.op("pe", lambda e, pt=pt, q=q, k4=k4: e.transpose(
                        out=pt[:, q * 128:(q + 1) * 128], in_=xio[:, (k4 + q) * 128:(k4 + q + 1) * 128],
                        identity=ident[:]), ["xio", "ident"], [pk])
                cp("act", xT[:, k4:k4 + n4, tt_ * 128:(tt_ + 1) * 128],
                   pt[:, 0:n4 * 128].rearrange("p (a b) -> p a b", b=128), [pk], ["xT"])

        def rms_stats(dst):
            for k in range(KT):
                tq, tk = tmp512()
                act(tq[:, 0:ntok], xT[:, k, 0:ntok], AF.Square, ["xT"], [tk])
                mm(PS2[:, 0:ntok], onesD[:], tq[:, 0:ntok], k == 0, k == KT - 1, [tk, "onesD"], ["ps2"])
            act(dst[:, 0:ntok], PS2[:, 0:ntok], AF.Ln, ["ps2", "cst"], ["rs"], bias=cst[:, 1:2])
            act(dst[:, 0:ntok], dst[:, 0:ntok], AF.Exp, ["rs"], ["rs"], scale=-0.5)

        for l in range(L):
            rms_stats(rs)
            for k in range(KT):
                tq, tk = tmp512()
                if nseq == 1:
                    stt(tq[:, 0:ntok], xT[:, k, 0:ntok], Amod[:, l, k, 0:1], rs[:, 0:ntok], ALU.mult, ALU.mult,
                        ["xT", "Amod", "rs"], [tk])
                    ts("dve", hT[:, k, 0:ntok], tq[:, 0:ntok], Bmod[:, l, k, 0:1], None, ALU.add, None,
                       [tk, "Bmod"], ["hT"])
                else:
                    tt("dve", tq[:, 0:ntok], xT[:, k, 0:ntok], rs[:, 0:ntok], ALU.mult, ["xT", "rs"], [tk])
                    tt("dve", v3(tq[:, 0:ntok], nseq), v3(tq[:, 0:ntok], nseq), bc(Amod, l, k), ALU.mult,
                       [tk, "Amod"], [tk])
                    tt("dve", v3(hT[:, k, 0:ntok], nseq), v3(tq[:, 0:ntok], nseq), bc(Bmod, l, k), ALU.add,
                       [tk, "Bmod"], ["hT"])

            dma("pool", wab[:], w_in[l, :, 4 * DD:4 * DD + H2].rearrange("(k p) c -> p k c", p=128),
                [], ["wab"], "wab")
            for tt_ in range(nt):
                pt, pk = nextpr()
                for k in range(KT):
                    mm(pt[:, 0:H2], hT[:, k, tt_ * 128:(tt_ + 1) * 128], wab[:, k, :], k == 0, k == KT - 1,
                       ["hT", "wab"], [pk])
                act(gbt[:, tt_, H:H2], pt[:, 0:H], AF.Sigmoid, [pk], ["gbt"])
                tt("dve", gtmp[:, 0:H], pt[:, H:H2], dtb[:, l * H:(l + 1) * H], ALU.add, [pk, "dtb"], ["gtmp"])
                act(gtmp[:, 0:H], gtmp[:, 0:H], AF.Exp, ["gtmp"], ["gtmp"])
                act(gtmp[:, 0:H], gtmp[:, 0:H], AF.Ln, ["gtmp", "cst"], ["gtmp"], bias=cst[:, 0:1])
                tt("dve", graw[:, tt_, :], gtmp[:, 0:H], nea[:, l * H:(l + 1) * H], ALU.mult, ["gtmp", "nea"], ["graw"])
                pt, pk = nextpr()
                mm(pt[:, 0:H], grp.McT[:], graw[:, tt_, :], True, True, MK + ["graw"], [pk])
                mm(pt[:, H:H2], grp.SegAll[:], graw[:, tt_, :], True, True, MK + ["graw"], [pk])
                cp("act", gbt[:, tt_, 0:H], pt[:, 0:H], [pk], ["gbt"])
                tt("dve", gtmp[:, H:H2], pt[:, H:H2], gbt[:, tt_, 0:H], ALU.subtract, [pk, "gbt"], ["gtmp"])
                act(e2t[:, tt_, :], gtmp[:, H:H2], AF.Exp, ["gtmp"], ["e2t"])
                tt("dve", e2m[:, tt_, 0:nseg, :],
                   e2t[:, tt_, :].unsqueeze(1).broadcast_to([128, nseg, H]),
                   grp.rowm[:].unsqueeze(2).broadcast_to([128, nseg, H]), ALU.mult, ["e2t"] + MK, ["e2m"])
                pt, pk = nextpr()
                P.op("pe", lambda e, pt=pt, tt_=tt_: e.transpose(out=pt[:, 0:128], in_=gbt[:, tt_, :],
                                                                identity=ident[:]), ["gbt", "ident"], [pk])
                cp("dve", gbT[0:H2, tt_, :], pt[0:H2, 0:128], [pk], ["gbT"])

            US = 30 + Tq
            uv = ubuf[:, 0:nseq * US].rearrange("p (s t) -> p s t", s=nseq)

            def conf_part1(j):
                if is_sample:
                    dma("sp", st120[0:120, :, :], sg[l, :, j * 128:(j + 1) * 128].rearrange("(g r) x -> r g x", r=120),
                        [], ["st120"], "st120")
                    pt, pk = PS2, "ps2"
                    for g4 in range(4):
                        P.op("pe", lambda e, pt=pt, g4=g4: e.transpose(out=pt[:, g4 * 128:(g4 + 1) * 128], in_=st120[:, g4, :],
                                                                       identity=ident[:]), ["st120", "ident"], [pk])
                    cp("act", uv.rearrange("p (g s) t -> p g s t", g=4)[:, :, :, 0:30], pt[:, 0:512].rearrange("p (g x) -> p g x", g=4)[:, :, 0:120].rearrange("p g (s r) -> p g s r", r=30), [pk], ["ubuf"])
                else:
                    cp("pool", uv[:, 0, 0:30], gtail[:, l, j, :], ["gtail"], ["ubuf"])
                wb_, wbk = load_w(w_in_tile(l, 4 * DD + H2 + DC + j * 128))
                pb_, pbk = nextpa()
                for k in range(KT):
                    mm(pb_[:, 0:ntok], wb_[:, k, :], hT[:, k, 0:ntok], k == 0, k == KT - 1, [wbk, "hT"], [pbk])
                tq, tk = tmp512()
                act(tq[:, 0:ntok], pb_[:, 0:ntok], AF.Sigmoid, [pbk], [tk])
                wa, wak = load_w(w_in_tile(l, 4 * DD + H2 + j * 128))
                pa_, pak = nextpa()
                for k in range(KT):
                    mm(pa_[:, 0:ntok], wa[:, k, :], hT[:, k, 0:ntok], k == 0, k == KT - 1, [wak, "hT"], [pak])
                tt("dve", uv[:, :, 30:30 + Tq], v3(pa_[:, 0:ntok], nseq), v3(tq[:, 0:ntok], nseq), ALU.mult,
                   [pak, tk], ["ubuf"])
                if not is_sample:
                    cp("pool", gtail[:, l, j, :], uv[:, 0, Tq:Tq + 30], ["ubuf"], ["gtail"])
                if last:
                    nr = nseq * 30
                    tq2, tk2 = tmp512()
                    cp("pool", tq2[:, 0:nr].rearrange("p (s r) -> p s r", r=30), uv[:, :, Tq:Tq + 30], ["ubuf"], [tk2])
                    if is_sample:
                        pt, pk = PS2, "ps2"
                        for g4 in range(4):
                            P.op("pe", lambda e, pt=pt, g4=g4, tq2=tq2: e.transpose(
                                out=pt[:, g4 * 128:(g4 + 1) * 128], in_=tq2[:, g4 * 120:g4 * 120 + 128],
                                identity=ident[:]), [tk2, "ident"], [pk])
                        cp("act", st120[0:120, :, :], pt[0:120, 0:512].rearrange("p (g x) -> p g x", g=4), [pk], ["st120"])
                        dma("sp", ogs[l, :, j * 128:(j + 1) * 128].rearrange("(g r) x -> r g x", r=120), st120[0:120, :, :],
                            ["st120"], [], "st120")
                    else:
                        pt, pk = PS2, "ps2"
                        P.op("pe", lambda e, pt=pt, tq2=tq2: e.transpose(out=pt[:, 0:128], in_=tq2[:, 0:128], identity=ident[:]),
                             [tk2, "ident"], [pk])
                        cp("act", st120[0:30, 0, :], pt[0:30, 0:128], [pk], ["st120"])
                        dma("sp", ogp[l, :, j * 128:(j + 1) * 128], st120[0:30, 0, :], ["st120"], [], "st120")
                ov = v3(cv[:, j, 0:ntok], nseq)
                for q in range(31):
                    wcol = wdw[:, (l * 31 + q) * CT + j:(l * 31 + q) * CT + j + 1]
                    if q == 0:
                        ts("dve", ov, uv[:, :, 0:Tq], wcol, bdw[:, l * CT + j:l * CT + j + 1], ALU.mult, ALU.add,
                           ["ubuf", "wdw", "bdw"], ["cv"])
                    else:
                        stt(ov, uv[:, :, q:q + Tq], wcol, ov, ALU.mult, ALU.add, ["ubuf", "wdw", "cv"], ["cv"])
                pst, pstk = nextpa()
                mm(pst[:, 0:ntok], onesC[:], cv[:, j, 0:ntok], True, True, ["cv", "onesC"], [pstk])
                if j == 0:
                    cp("act", mu[:, 0:ntok], pst[:, 0:ntok], [pstk], ["mu"])
                else:
                    tt("dve", mu[:, 0:ntok], mu[:, 0:ntok], pst[:, 0:ntok], ALU.add, ["mu", pstk], ["mu"])
                tq, tk = tmp512()
                act(tq[:, 0:ntok], cv[:, j, 0:ntok], AF.Square, ["cv"], [tk])
                pst, pstk = nextpa()
                mm(pst[:, 0:ntok], onesC[:], tq[:, 0:ntok], True, True, [tk, "onesC"], [pstk])
                if j == 0:
                    cp("act", rsc[:, 0:ntok], pst[:, 0:ntok], [pstk], ["rsc"])
                else:
                    tt("dve", rsc[:, 0:ntok], rsc[:, 0:ntok], pst[:, 0:ntok], ALU.add, ["rsc", pstk], ["rsc"])

            def head_front(h):
                hb = h % 2
                qkva, knqs, zs = qkvas[hb], knqss[hb], zss[hb]
                QK, KQ, ZK = "qkva%d" % hb, "knqs%d" % hb, "zs%d" % hb
                XS = 3 + Tq
                xv = xpre[:, :, 0:nseq * XS].rearrange("p c (s t) -> p c s t", s=nseq)
                if is_sample:
                    dma("sp", st48[0:NS * 3, :, :], sq[l, :, :].rearrange("r (c x) -> r c x", c=3)[:, :, h * 128:(h + 1) * 128],
                        [], ["st48"], "st48")
                    pt, pk = PS2, "ps2"
                    for c in range(3):
                        P.op("pe", lambda e, pt=pt, c=c: e.transpose(out=pt[:, c * 128:(c + 1) * 128], in_=st48[:, c, :],
                                                                     identity=ident[:]), ["st48", "ident"], [pk])
                    for c in range(3):
                        cp("act", xv[:, c, :, 0:3], pt[:, c * 128:c * 128 + 48].rearrange("p (s r) -> p s r", r=3),
                           [pk], ["xpre"])
                else:
                    for c in range(3):
                        cp("pool", xv[:, c, 0, 0:3], qtail[:, l, c * H + h, :], ["qtail"], ["xpre"])
                for c in range(3):
                    wt, wk = load_w(w_in_tile(l, c * DD + h * 128))
                    pt, pk = nextpa()
                    for k in range(KT):
                        mm(pt[:, 0:ntok], wt[:, k, :], hT[:, k, 0:ntok], k == 0, k == KT - 1, [wk, "hT"], [pk])
                    cp("act", xv[:, c, :, 3:3 + Tq], v3(pt[:, 0:ntok], nseq), [pk], ["xpre"])
                    if not is_sample:
                        cp("pool", qtail[:, l, c * H + h, :], xv[:, c, 0, Tq:Tq + 3], ["xpre"], ["qtail"])
                if last:
                    nr = nseq * 3
                    pt, pk = PS2, "ps2"
                    for c in range(3):
                        tq, tk = tmp512()
                        cp("pool", tq[:, 0:nr].rearrange("p (s r) -> p s r", r=3), xv[:, c, :, Tq:Tq + 3], ["xpre"], [tk])
                        P.op("pe", lambda e, pt=pt, c=c, tq=tq: e.transpose(
                            out=pt[:, c * 128:(c + 1) * 128], in_=tq[:, 0:128], identity=ident[:]), [tk, "ident"], [pk])
                    cp("act", st48[0:nr, :, :], pt[0:nr, 0:384].rearrange("p (c x) -> p c x", c=3), [pk], ["st48"])
                    od = (oqs if is_sample else oqp)[l, :, :].rearrange("r (c x) -> r c x", c=3)[:, :, h * 128:(h + 1) * 128]
                    dma("sp", od, st48[0:nr, :, :], ["st48"], [], "st48")
                wt, wk = load_w(w_in_tile(l, 3 * DD + h * 128))
                pt, pk = nextpa()
                for k in range(KT):
                    mm(pt[:, 0:ntok], wt[:, k, :], hT[:, k, 0:ntok], k == 0, k == KT - 1, [wk, "hT"], [pk])
                act(zs[:, 0:ntok], pt[:, 0:ntok], AF.Silu, [pk], [ZK])
                for c in range(3):
                    ct = c * H + h
                    ov = v3(qkva[:, c, 0:ntok], nseq)
                    for j in range(4):
                        wcol = wqc[:, (l * 4 + j) * 3 * H + ct:(l * 4 + j) * 3 * H + ct + 1]
                        if j == 0:
                            ts("dve", ov, xv[:, c, :, 0:Tq], wcol, None, ALU.mult, None, ["xpre", "wqc"], [QK])
                        else:
                            stt(ov, xv[:, c, :, j:j + Tq], wcol, ov, ALU.mult, ALU.add, ["xpre", "wqc", QK], [QK])
                    act(qkva[:, c, 0:ntok], qkva[:, c, 0:ntok], AF.Silu, [QK], [QK])
                for c in (1, 0):
                    tq, tk = tmp512()
                    act(tq[:, 0:ntok], qkva[:, c, 0:ntok], AF.Square, [QK], [tk])
                    mm(PS2[:, 0:ntok], ones1[:], tq[:, 0:ntok], True, True, [tk, "ones1"], ["ps2"])
                    act(tq[:, 0:ntok], PS2[:, 0:ntok], AF.Ln, ["ps2", "cst"], [tk], bias=cst[:, 1:2])
                    act(tq[:, 0:ntok], tq[:, 0:ntok], AF.Exp, [tk], [tk], scale=-0.5)
                    src = qkva[:, c, 0:ntok].rearrange("p (a b) -> p a b", b=128)
                    rv = tq[:, 0:ntok].rearrange("p (a b) -> p a b", b=128)
                    if c == 1:
                        tt("dve", knqs[:, 0:nt, 0:128], src, rv, ALU.mult, [QK, tk], [KQ])
                    else:
                        stt(knqs[:, 0:nt, 128:256], src, QSC, rv, ALU.mult, ALU.mult, [QK, tk], [KQ])
                if is_sample:
                    dma("sp", Sin, sd[l, :, h, :, :].rearrange("s p v -> p s v"), [], ["Sin"], "Sin")


            def tile_solve(h, tt_, sid):
                hb = h % 2
                qkva, knqs, zs = qkvas[hb], knqss[hb], zss[hb]
                QK, KQ, ZK = "qkva%d" % hb, "knqs%d" % hb, "zs%d" % hb
                tl, SS = tls[sid], str(sid)
                kdecv, kdk = (kdec16, "kdec16") if is_sample else (kdec2s[sid][:], "kdec2_%d" % sid)
                tsl = slice(tt_ * 128, (tt_ + 1) * 128)

                def nb():
                    nbi[sid] ^= 1
                    j = 2 * sid + nbi[sid]
                    return PR[j], "pr%d" % j
                tsl = slice(tt_ * 128, (tt_ + 1) * 128)
                gccol = gbt[:, tt_, h:h + 1]
                becol = gbt[:, tt_, H + h:H + h + 1]
                pa_, pak = nb()
                mm(pa_[:, 0:128], EH[:, h:h + 1].broadcast_to([128, 128]), gbT[:, tt_, :], True, True, ["EH", "gbT"], [pak])
                mm(pa_[:, 128:256], EH[:, H + h:H + h + 1].broadcast_to([128, 128]), gbT[:, tt_, :], True, True, ["EH", "gbT"], [pak])
                act(tl["eg"][:], pa_[:, 0:128], AF.Exp, [pak], [("eg" + SS)])
                cp("act", tl["bb"][:], pa_[:, 128:256], [pak], [("bb" + SS)])
                ts("dve", tl["dd"][:], pa_[:, 0:128], gccol, 0.0, ALU.subtract, ALU.min, [pak, "gbt"], [("dd" + SS)])
                act(tl["decT"][:], tl["dd"][:], AF.Exp, [("dd" + SS)], [("decT" + SS)])
                tt("pool", tl["dmC"][:], tl["decT"][:], grp.McT[:], ALU.mult, [("decT" + SS)] + MK, [("dmC" + SS)])
                tt("pool", tl["dmS"][:], tl["decT"][:], grp.MsT[:], ALU.mult, [("decT" + SS)] + MK, [("dmS" + SS)])
                tt("dve", tl["qG"][:], knqs[:, tt_, 128:256], tl["eg"][:], ALU.mult, [KQ, ("eg" + SS)], [("qG" + SS)])
                tt("dve", tl["kgT"][:], knqs[:, tt_, 0:128], tl["eg"][:], ALU.mult, [KQ, ("eg" + SS)], [("kgT" + SS)])
                pb_, pbk = nb()
                mm(pb_[:, 0:256], knqs[:, tt_, 0:128], knqs[:, tt_, :], True, True, [KQ], [pbk])
                tt("dve", tl["attnT"][:], pb_[:, 128:256], tl["dmC"][:], ALU.mult, [pbk, ("dmC" + SS)], [("attnT" + SS)])
                stt(tl["LTp"][:], pb_[:, 0:128], becol, tl["dmS"][:], ALU.mult, ALU.mult, [pbk, "gbt", ("dmS" + SS)], [("LTp" + SS)])
                pc_, pck = nb()
                P.op("pe", lambda e, pc_=pc_: e.transpose(out=pc_[:, 0:128].bitcast(F32R), in_=tl["LTp"][:], identity=identR[:]),
                     [("LTp" + SS), "identR"], [pck])
                cp("act", tl["A0"][:], pc_[:, 0:128], [pck], [("A0" + SS)])
                tt("pool", tl["PT"][:], ident[:], tl["LTp"][:], ALU.subtract, ["ident", ("LTp" + SS)], [("PT" + SS)])
                A, AT, Ak, ATk = tl["A0"], tl["LTp"], ("A0" + SS), ("LTp" + SS)
                for kk in range(1, nd + 1):
                    An, Ank = (tl["A1"], ("A1" + SS)) if (kk % 2) else (tl["A0"], ("A0" + SS))
                    ATn, ATnk = (tl["AT1"], ("AT1" + SS)) if (kk % 2) else (tl["AT0"], ("AT0" + SS))
                    p1, p1k = nb()
                    mm(p1[:, 0:128], AT[:], A[:], True, True, [Ak, ATk], [p1k])
                    if kk < nd:
                        p2, p2k = nb()
                        mm(p2[:, 0:128], A[:], AT[:], True, True, [Ak, ATk], [p2k])
                    cp("act", An[:], p1[:, 0:128], [p1k], [Ank])
                    if kk < nd:
                        cp("dve", ATn[:], p2[:, 0:128], [p2k], [ATnk])
                    p3, p3k = nb()
                    mm(p3[:, 0:128], An[:], tl["PT"][:], True, True, [Ank, ("PT" + SS)], [p3k])
                    tt("dve", tl["PT"][:], tl["PT"][:], p3[:, 0:128], ALU.add, [("PT" + SS), p3k], [("PT" + SS)])
                    A, AT, Ak, ATk = An, ATn, Ank, ATnk
                p1, p1k = nb()
                P.op("pe", lambda e, p1=p1: e.transpose(out=p1[:, 0:128], in_=tl["kgT"][:], identity=ident[:]),
                     [("kgT" + SS), "ident"], [p1k])
                cp("act", tl["kgtok"][:], p1[:, 0:128], [p1k], [("kgtok" + SS)])
                p2, p2k = nb()
                P.op("pe", lambda e, p2=p2, tsl=tsl: e.transpose(out=p2[:, 0:128], in_=qkva[:, 2, tsl], identity=ident[:]),
                     [QK, "ident"], [p2k])
                cp("dve", tl["vtok"][:], p2[:, 0:128], [p2k], [("vtok" + SS)])
                p3, p3k = nb()
                P.op("pe", lambda e, p3=p3, tt_=tt_: e.transpose(out=p3[:, 0:128], in_=knqs[:, tt_, 0:128], identity=ident[:]),
                     [KQ, "ident"], [p3k])
                for i in range(nseg):
                    if i % 2 == 0:
                        ts("dve", kdecv[:, i, :], p3[:, 0:128], e2m[:, tt_, i, h:h + 1], None, ALU.mult, None,
                           [p3k, "e2m"], [kdk])
                    else:
                        act(kdecv[:, i, :], p3[:, 0:128], AF.Identity, [p3k, "e2m"], [kdk], scale=e2m[:, tt_, i, h:h + 1])
                p4, p4k = nb()
                mm(p4[:, 0:128], tl["kgtok"][:], tl["PT"][:], True, True, [("kgtok" + SS), ("PT" + SS)], [p4k])
                act(tl["Wn"][:], p4[:, 0:128], AF.Identity, [p4k], [("Wn" + SS)], scale=-1.0)

            def tile_state(h, tt_, sid):
                hb = h % 2
                qkva, knqs, zs = qkvas[hb], knqss[hb], zss[hb]
                QK, KQ, ZK = "qkva%d" % hb, "knqs%d" % hb, "zs%d" % hb
                tl, SS = tls[sid], str(sid)
                kdecv, kdk = (kdec16, "kdec16") if is_sample else (kdec2s[sid][:], "kdec2_%d" % sid)
                tsl = slice(tt_ * 128, (tt_ + 1) * 128)

                def nb():
                    nbi[sid] ^= 1
                    j = 2 * sid + nbi[sid]
                    return PR[j], "pr%d" % j
                pv_, pvk = (PA[1], "pa1")
                mm(pv_[:, 0:128], tl["vtok"][:], tl["PT"][:], True, True, [("vtok" + SS), ("PT" + SS)], [pvk])
                cp("act", tl["UT0"][:], pv_[:, 0:128], [pvk], ["UT0"])
                po_, pok = PS3, "ps3"
                for i in range(nseg):
                    a, b = i * seglen, (i + 1) * seglen
                    if is_sample:
                        s_in, s_out, sik, sok = Sin[:, i, :], Sout[:, i, :], "Sin", "Sout"
                    elif i % 2 == 0:
                        s_in, s_out, sik, sok = Sp[:, l, h, :], Stmp[:], "Sp", "Stmp"
                    else:
                        s_in, s_out, sik, sok = Stmp[:], Sp[:, l, h, :], "Stmp", "Sp"
                    if (not is_sample) or i == 0:
                        pw_, pwk = (PA[1], "pa1")
                    mm(pw_[:, a:b], s_in, tl["Wn"][:, a:b], True, True, [sik, ("Wn" + SS)], [pwk])
                    mm(po_[:, a:b], s_in, tl["qG"][:, a:b], True, True, [sik, ("qG" + SS)], [pok])
                    if (not is_sample) or i == nseg - 1:
                        a0 = a if not is_sample else 0
                        tt("dve", tl["UT"][:, a0:b], pw_[:, a0:b], tl["UT0"][:, a0:b], ALU.add, [pwk, "UT0"], ["UT"])
                        tt("dve", tl["UT"][:, a0:b], tl["UT"][:, a0:b], tl["bb"][:, a0:b], ALU.mult, ["UT", ("bb" + SS)], ["UT"])
                        p5, p5k = (PA[1], "pa1")
                        P.op("pe", lambda e, p5=p5: e.transpose(out=p5[:, 0:128], in_=tl["UT"][:], identity=ident[:]),
                             ["UT", "ident"], [p5k])
                        cp("act", tl["Utok"][:], p5[:, 0:128], [p5k], ["Utok"])
                    if not is_sample:
                        p6, p6k = (PA[1], "pa1")
                        mm(p6[:, 0:128], kdecv[:, i, :], tl["Utok"][:], True, True, [kdk, "Utok"], [p6k])
                        stt(s_out, s_in, tl["eg"][:, b - 1:b], p6[:, 0:128], ALU.mult, ALU.add, [sik, ("eg" + SS), p6k], [sok])
                if is_sample:
                    for i in range(nseg):
                        b = (i + 1) * seglen
                        p6, p6k = (PA[1], "pa1")
                        mm(p6[:, 0:128], kdecv[:, i, :], tl["Utok"][:].bitcast(F32), True, True, [kdk, "Utok"], [p6k])
                        stt(Sout[:, i, :], Sin[:, i, :], tl["eg"][:, b - 1:b], p6[:, 0:128], ALU.mult, ALU.add,
                            ["Sin", ("eg" + SS), p6k], ["Sout"])
                cp("act", oT[:, tsl], po_[:, 0:128], [pok], ["oT"])
                p7, p7k = (PA[1], "pa1")
                mm(p7[:, 0:128], tl["Utok"][:], tl["attnT"][:], True, True, ["Utok", ("attnT" + SS)], [p7k])
                tt("dve", oT[:, tsl], oT[:, tsl], p7[:, 0:128], ALU.add, ["oT", p7k], ["oT"])


            def head_back(h):
                hb = h % 2
                qkva, knqs, zs = qkvas[hb], knqss[hb], zss[hb]
                QK, KQ, ZK = "qkva%d" % hb, "knqs%d" % hb, "zs%d" % hb
                if is_sample:
                    dma("sp", oSs[l, :, h, :, :].rearrange("s p v -> p s v"), Sout, ["Sout"], [], "Sout")
                elif last:
                    dma("sp", oSp[l, h, :, :], Sp[:, l, h, :], ["Sp"], [], "Sp")
                tq, tk = tmp512()
                act(tq[:, 0:ntok], oT[:, 0:ntok], AF.Square, ["oT"], [tk])
                mm(PS2[:, 0:ntok], onesH[:], tq[:, 0:ntok], True, True, [tk, "onesH"], ["ps2"])
                act(tq[:, 0:ntok], PS2[:, 0:ntok], AF.Ln, ["ps2", "cst"], [tk], bias=cst[:, 1:2])
                act(tq[:, 0:ntok], tq[:, 0:ntok], AF.Exp, [tk], [tk], scale=-0.5)
                tt("dve", tq[:, 0:ntok], tq[:, 0:ntok], oT[:, 0:ntok], ALU.mult, [tk, "oT"], [tk])
                stt(mixT[:, h, 0:ntok], tq[:, 0:ntok], hng[:, l:l + 1], zs[:, 0:ntok], ALU.mult, ALU.mult,
                    [tk, "hng", ZK], ["mixT"])


            def record(fn, h):
                P.rec = []
                fn(h)
                r, P.rec = P.rec, None
                return r

            def replay(ops):
                for o in ops:
                    P.op(*o)

            def merge2(x, y):
                if not y:
                    return list(x)
                if not x:
                    return list(y)
                out_, yi = [], 0
                for xi, o in enumerate(x):
                    out_.append(o)
                    want = (xi + 1) * len(y) // len(x)
                    while yi < want:
                        out_.append(y[yi])
                        yi += 1
                out_.extend(y[yi:])
                return out_

            def rec2(fn, *a):
                P.rec = []
                fn(*a)
                r, P.rec = P.rec, None
                return r

            pa_front[0] = True
            replay(record(head_front, 0))
            pa_front[0] = False
            for h in range(H):
                S_ = [rec2(tile_solve, h, t, t % 2) for t in range(nt)]
                Q_ = [rec2(tile_state, h, t, t % 2) for t in range(nt)]
                if nt == 1:
                    t_ops = S_[0] + Q_[0]
                else:
                    t_ops = merge2(S_[0], S_[1]) + Q_[0]
                    for t in range(1, nt):
                        t_ops += merge2(Q_[t], S_[t + 1]) if t + 1 < nt else Q_[t]
                pa_front[0] = True
                f_ops = record(head_front, h + 1) if h + 1 < H else []
                if not is_sample:
                    f_ops = f_ops + record(conf_part1, h)
                pa_front[0] = False
                replay(merge2(t_ops, f_ops))
                head_back(h)
            if is_sample:
                for j in range(CT):
                    conf_part1(j)

            tq, tk = tmp512()
            tt("dve", tq[:, 0:ntok], mu[:, 0:ntok], mu[:, 0:ntok], ALU.mult, ["mu"], [tk])
            tt("dve", rsc[:, 0:ntok], rsc[:, 0:ntok], tq[:, 0:ntok], ALU.subtract, ["rsc", tk], ["rsc"])
            act(rsc[:, 0:ntok], rsc[:, 0:ntok], AF.Ln, ["rsc", "cst"], ["rsc"], bias=cst[:, 1:2])
            act(rsc[:, 0:ntok], rsc[:, 0:ntok], AF.Exp, ["rsc"], ["rsc"], scale=-0.5)
            for j in range(CT):
                wz, wzk = load_w(w_in_tile(l, 4 * DD + H2 + 2 * DC + j * 128))
                pa_, pak = nextpa()
                for k in range(KT):
                    mm(pa_[:, 0:ntok], wz[:, k, :], hT[:, k, 0:ntok], k == 0, k == KT - 1, [wzk, "hT"], [pak])
                tz, tzk = tmp512()
                act(tz[:, 0:ntok], pa_[:, 0:ntok], AF.Silu, [pak], [tzk])
                tq, tk = tmp512()
                tt("dve", tq[:, 0:ntok], cv[:, j, 0:ntok], mu[:, 0:ntok], ALU.subtract, ["cv", "mu"], [tk])
                tt("dve", tq[:, 0:ntok], tq[:, 0:ntok], rsc[:, 0:ntok], ALU.mult, [tk, "rsc"], [tk])
                act(tq[:, 0:ntok], tq[:, 0:ntok], AF.Silu, [tk, "lng", "lnb"], [tk],
                    scale=lng[:, l * CT + j:l * CT + j + 1], bias=lnb[:, l * CT + j:l * CT + j + 1])
                tt("dve", mixT[:, H + j, 0:ntok], tq[:, 0:ntok], tz[:, 0:ntok], ALU.mult, [tk, tzk], ["mixT"])

            for m in range(KT):
                wt, wk = load_w(w_out[l, :, m * 128:(m + 1) * 128].rearrange("(k p) c -> p k c", p=128))
                pt, pk = nextpa()
                for e_ in range(KT):
                    mm(pt[:, 0:ntok], wt[:, e_, :], mixT[:, e_, 0:ntok], e_ == 0, e_ == KT - 1, [wk, "mixT"], [pk])
                if nseq == 1:
                    stt(xT[:, m, 0:ntok], pt[:, 0:ntok], Gmod[:, l, m, 0:1], xT[:, m, 0:ntok], ALU.mult, ALU.add,
                        [pk, "Gmod", "xT"], ["xT"])
                else:
                    tq, tk = tmp512()
                    tt("dve", v3(tq[:, 0:ntok], nseq), v3(pt[:, 0:ntok], nseq), bc(Gmod, l, m), ALU.mult, [pk, "Gmod"], [tk])
                    tt("dve", xT[:, m, 0:ntok], xT[:, m, 0:ntok], tq[:, 0:ntok], ALU.add, ["xT", tk], ["xT"])

        rms_stats(rs)
        for k in range(KT):
            stt(xT[:, k, 0:ntok], xT[:, k, 0:ntok], fg[:, k:k + 1], rs[:, 0:ntok], ALU.mult, ALU.mult,
                ["xT", "fg", "rs"], ["xT"])
        for tt_ in range(nt):
            for k4 in range(0, KT, 4):
                n4 = min(4, KT - k4)
                pt, pk = nextpa()
                for q in range(n4):
                    P.op("pe", lambda e, pt=pt, q=q, k4=k4, tt_=tt_: e.transpose(
                        out=pt[:, q * 128:(q + 1) * 128], in_=xT[:, k4 + q, tt_ * 128:(tt_ + 1) * 128],
                        identity=ident[:]), ["xT", "ident"], [pk])
                cp("act", xio[:, k4 * 128:(k4 + n4) * 128], pt[:, 0:n4 * 128], [pk], ["xio"])
            dma("sp", ydst[tt_ * 128:(tt_ + 1) * 128, :], xio[:], ["xio"], [], "xio")

    try:
        for bi in range(NB):
            do_block(False, bi)
        do_block(True, 0)
    except StopBuild:
        pass
    P.emit()
    st.close()
    return nc


_NC_CACHE = {}


def _get_nc(cfg_key):
    if cfg_key not in _NC_CACHE:
        _NC_CACHE[cfg_key] = build(Cfg(*cfg_key))
    return _NC_CACHE[cfg_key]


def make_in_maps(inp, cfg, ncores):
    D, L, KT, H, DD, DC, CT = cfg.D, cfg.L, cfg.KT, cfg.H, cfg.DD, cfg.DC, cfg.CT
    f = lambda a: np.ascontiguousarray(np.asarray(a, dtype=np.float32))
    Bp = inp["x_prompt"].shape[0]
    shared = {
        "norm_g": f(inp["norm_g"]).reshape(L * KT, 128),
        "w_ada": f(inp["w_ada"]),
        "b_ada": f(inp["b_ada"]).reshape(L * 3 * KT, 128),
        "w_in": f(inp["w_in"]),
        "w_qc": f(inp["w_qkv_conv"]).reshape(L * 4 * 3 * H, 128),
        "a_log": f(inp["a_log"]).reshape(1, L * H),
        "dt_bias": f(inp["dt_bias"]).reshape(1, L * H),
        "hn_g": f(inp["head_norm_g"]).reshape(L, 128),
        "w_dw": f(inp["w_dw"]).reshape(L * 31 * CT, 128),
        "b_dw": f(inp["b_dw"]).reshape(L * CT, 128),
        "ln_g": f(inp["ln_g"]).reshape(L * CT, 128),
        "ln_b": f(inp["ln_b"]).reshape(L * CT, 128),
        "w_out": f(inp["w_out"]),
        "final_g": f(inp["final_g"]).reshape(KT, 128),
    }
    maps = []
    NS = cfg.NS
    for c in range(ncores):
        b = c % Bp
        s0 = c * NS
        m = dict(shared)
        m["xp"] = f(inp["x_prompt"][b])
        m["xs"] = f(inp["x_sample"][s0:s0 + NS]).reshape(NS * cfg.TS, D)
        m["cc"] = f(np.concatenate([np.asarray(inp["c_prompt"])[b:b + 1], np.asarray(inp["c_sample"])[s0:s0 + NS]], axis=0))
        m["sd"] = f(np.asarray(inp["state_delta"])[:, s0:s0 + NS])
        m["sq"] = f(np.asarray(inp["state_qkv_conv"])[:, s0:s0 + NS]).reshape(L, NS * 3, 3 * DD)
        m["sg"] = f(np.asarray(inp["state_glu_conv"])[:, s0:s0 + NS]).reshape(L, NS * 30, DC)
        maps.append(m)
    return maps


def assemble(res, cfg, ncores, Bp):
    D, L, H, DD, DC, NS, TS = cfg.D, cfg.L, cfg.H, cfg.DD, cfg.DC, cfg.NS, cfg.TS
    g = lambda c, n: np.asarray(res[c][n], dtype=np.float32)
    y_prompt = np.stack([g(b, "yp") for b in range(Bp)])
    y_sample = np.concatenate([g(c, "ys").reshape(NS, TS, D) for c in range(ncores)], axis=0)
    Sp = np.stack([g(b, "oSp") for b in range(Bp)], axis=1)
    qp = np.stack([g(b, "oqp") for b in range(Bp)], axis=1)
    gp = np.stack([g(b, "ogp") for b in range(Bp)], axis=1)
    Ss = np.concatenate([g(c, "oSs") for c in range(ncores)], axis=1)
    qs = np.concatenate([g(c, "oqs").reshape(L, NS, 3, 3 * DD) for c in range(ncores)], axis=1)
    gs = np.concatenate([g(c, "ogs").reshape(L, NS, 30, DC) for c in range(ncores)], axis=1)
    return (y_prompt, y_sample, Sp, qp, gp, Ss, qs, gs)


def kernel(**inputs):
    ncores = 8
    cfg_key = (2048, 2048, 512, 2, BF16)
    cfg = Cfg(*cfg_key)
    nc = _get_nc(cfg_key)
    maps = make_in_maps(inputs, cfg, ncores)
    res = run_bass_kernel_spmd(nc, maps, core_ids=list(range(ncores)))
    return assemble(res.results, cfg, ncores, 4)
```

```python
import contextlib
import math
import numpy as np
import concourse.bass as bass
import concourse.mybir as mybir
from concourse.bass_utils import run_bass_kernel_spmd

F32 = mybir.dt.float32
BF16 = mybir.dt.bfloat16
F32R = mybir.dt.float32r
ALU = mybir.AluOpType
AF = mybir.ActivationFunctionType
EPS = 1e-6
ENGS = ("pe", "act", "dve", "pool", "sp")


class Ins:
    __slots__ = ("eng", "fn", "deps", "sig", "cnt", "dma_key", "dma_gen")

    def __init__(self, eng, fn):
        self.eng = eng
        self.fn = fn
        self.deps = []
        self.sig = False
        self.cnt = 0
        self.dma_key = None
        self.dma_gen = 0


class Prog:
    def __init__(self, nc):
        self.nc = nc
        self.lists = {e: [] for e in ENGS}
        self.last_w = {}
        self.readers = {}
        self.dma_gen = {}
        self.all = []
        self.rec = None

    PSUM_KEYS = frozenset(["pa0", "pa1", "ps2", "ps3", "pr0", "pr1", "pr2", "pr3"])

    def op(self, eng, fn, reads=(), writes=(), dma_key=None):
        if self.rec is not None:
            self.rec.append((eng, fn, tuple(reads), tuple(writes), dma_key))
            return None
        ins = Ins(eng, fn)
        writes = list(writes) + [k for k in reads if k in self.PSUM_KEYS and k not in writes]
        deps = []
        for k in reads:
            w = self.last_w.get(k)
            if w is not None:
                deps.append(w)
        for k in writes:
            w = self.last_w.get(k)
            if w is not None:
                deps.append(w)
            deps.extend(self.readers.get(k, {}).values())
        seen = set()
        for d in deps:
            if id(d) in seen:
                continue
            seen.add(id(d))
            if d.eng == "pe" and eng == "pe" and d.dma_key is None and dma_key is None:
                continue
            ins.deps.append(d)
        if dma_key is not None:
            g = self.dma_gen.get(dma_key, 0) + 1
            self.dma_gen[dma_key] = g
            ins.dma_key = dma_key
            ins.dma_gen = g
        slot = eng if dma_key is None else ("dma", dma_key)
        for k in reads:
            self.readers.setdefault(k, {})[slot] = ins
        for k in writes:
            self.last_w[k] = ins
            self.readers[k] = {}
        self.lists[eng].append(ins)
        self.all.append(ins)
        return ins

    def emit(self, final_wait_eng="sp"):
        nc = self.nc
        for ins in self.all:
            for d in ins.deps:
                if d.dma_key is None:
                    d.sig = True
        nsig = {}
        for e in ENGS:
            c = 0
            for ins in self.lists[e]:
                if ins.dma_key is None and ins.sig:
                    c += 1
                    ins.cnt = c
            nsig[e] = c
        dma_keys = sorted(self.dma_gen.keys(), key=str)
        with contextlib.ExitStack() as st:
            esem = {e: st.enter_context(nc.semaphore("s_" + e)) for e in ENGS}
            dsem = {k: st.enter_context(nc.semaphore("d%d" % i)) for i, k in enumerate(dma_keys)}
            block = st.enter_context(nc.Block())
            hw = {"pe": block.tensor, "act": block.scalar, "dve": block.vector,
                  "pool": block.gpsimd, "sp": block.sync}

            def make(e):
                def body(engine):
                    waited = {}
                    for ins in self.lists[e]:
                        for d in ins.deps:
                            if d.dma_key is not None:
                                s, v = dsem[d.dma_key], 16 * d.dma_gen
                            else:
                                s, v = esem[d.eng], d.cnt
                            if waited.get(id(s), 0) >= v:
                                continue
                            waited[id(s)] = v
                            engine.wait_ge(s, v)
                        r = ins.fn(engine)
                        if ins.dma_key is not None:
                            r.then_inc(dsem[ins.dma_key], 16)
                        elif ins.sig:
                            r.then_inc(esem[e], 1)
                    if e == final_wait_eng:
                        for k in dma_keys:
                            engine.wait_ge(dsem[k], 16 * self.dma_gen[k])
                        for e2 in ENGS:
                            if nsig[e2] and e2 != e:
                                engine.wait_ge(esem[e2], nsig[e2])
                return body

            for e in ENGS:
                if self.lists[e] or e == final_wait_eng:
                    hw[e](make(e))


class StopBuild(Exception):
    pass


class Cfg:
    dbg = 0

    def __init__(self, D=2048, T=2048, TB=512, L=2, wdt=BF16):
        self.D, self.T, self.TB, self.L = D, T, TB, L
        self.KT = D // 128
        self.H = D // 256
        self.DD = D // 2
        self.DC = D // 2
        self.CT = self.DC // 128
        self.NIN = 4 * self.DD + 2 * self.H + 3 * self.DC
        self.NB = T // TB
        self.NS = 16
        self.TS = 8
        self.wdt = wdt


def build(cfg):
    D, T, TB, L, KT, H, DD, DC, CT, NIN, NB = (cfg.D, cfg.T, cfg.TB, cfg.L, cfg.KT, cfg.H, cfg.DD,
                                                cfg.DC, cfg.CT, cfg.NIN, cfg.NB)
    WDT = cfg.wdt
    NS, TS = cfg.NS, cfg.TS
    NTB = TB // 128
    H2 = 2 * H
    QSC = 128.0 ** -0.5
    nc = bass.Bass("TRN2", target_bir_lowering=False)

    def din(name, shape):
        return nc.dram_tensor(name, list(shape), F32, kind="ExternalInput").ap()

    def dout(name, shape):
        return nc.dram_tensor(name, list(shape), F32, kind="ExternalOutput").ap()

    xp = din("xp", [T, D])
    xs = din("xs", [128, D])
    cc = din("cc", [1 + NS, D])
    sd = din("sd", [L, NS, H, 128, 128])
    sq = din("sq", [L, NS * 3, 3 * DD])
    sg = din("sg", [L, NS * 30, DC])
    norm_g = din("norm_g", [L * KT, 128])
    w_ada = din("w_ada", [L, D, 3 * D])
    b_ada = din("b_ada", [L * 3 * KT, 128])
    w_in = din("w_in", [L, D, NIN])
    w_qc = din("w_qc", [L * 4 * 3 * H, 128])
    a_log = din("a_log", [1, L * H])
    dt_bias = din("dt_bias", [1, L * H])
    hn_g = din("hn_g", [L, 128])
    w_dw = din("w_dw", [L * 31 * CT, 128])
    b_dw = din("b_dw", [L * CT, 128])
    ln_g = din("ln_g", [L * CT, 128])
    ln_b = din("ln_b", [L * CT, 128])
    w_out = din("w_out", [L, D, D])
    final_g = din("final_g", [KT, 128])

    yp = dout("yp", [T, D])
    ys = dout("ys", [128, D])
    oSp = dout("oSp", [L, H, 128, 128])
    oqp = dout("oqp", [L, 3, 3 * DD])
    ogp = dout("ogp", [L, 30, DC])
    oSs = dout("oSs", [L, NS, H, 128, 128])
    oqs = dout("oqs", [L, NS * 3, 3 * DD])
    ogs = dout("ogs", [L, NS * 30, DC])

    st = contextlib.ExitStack()
    P = Prog(nc)

    def sb(name, shape, dt=F32):
        return st.enter_context(nc.sbuf_tensor(name, list(shape), dt))

    def psb(name):
        return st.enter_context(nc.psum_tensor(name, [128, 512], F32))

    def mm(out, lhsT, rhs, start, stop, r, w, skip=False):
        return P.op("pe", lambda e: e.matmul(out, lhsT=lhsT, rhs=rhs, start=start, stop=stop,
                                             skip_group_check=skip), r, w)

    def act(out, in_, func, r, w, scale=1.0, bias=None):
        if bias is None:
            return P.op("act", lambda e: e.activation(out=out, in_=in_, func=func, scale=scale), r, w)
        return P.op("act", lambda e: e.activation(out=out, in_=in_, func=func, scale=scale, bias=bias), r, w)

    def ts(eng, out, in0, s1, s2, op0, op1, r, w):
        if op1 is None:
            return P.op(eng, lambda e: e.tensor_scalar(out=out, in0=in0, scalar1=s1, scalar2=None, op0=op0), r, w)
        return P.op(eng, lambda e: e.tensor_scalar(out=out, in0=in0, scalar1=s1, scalar2=s2, op0=op0, op1=op1), r, w)

    def tt(eng, out, in0, in1, op, r, w):
        return P.op(eng, lambda e: e.tensor_tensor(out=out, in0=in0, in1=in1, op=op), r, w)

    def stt(out, in0, scalar, in1, op0, op1, r, w):
        return P.op("dve", lambda e: e.scalar_tensor_tensor(out=out, in0=in0, scalar=scalar, in1=in1,
                                                            op0=op0, op1=op1), r, w)

    def cp(eng, out, in_, r, w):
        if eng == "act":
            return P.op("act", lambda e: e.copy(out=out, in_=in_), r, w)
        return P.op(eng, lambda e: e.tensor_copy(out=out, in_=in_), r, w)

    def dma(eng, out, in_, r, w, key):
        return P.op(eng, lambda e: e.dma_start(out=out, in_=in_), r, w, dma_key=key)

    def recip(out, in_, r, w):
        return P.op("dve", lambda e: e.reciprocal(out=out, in_=in_), r, w)

    def asel(out, in_, pattern, op, fill, base, cm, r, w):
        return P.op("pool", lambda e: e.affine_select(out=out, in_=in_, pattern=pattern, compare_op=op,
                                                      fill=fill, base=base, channel_multiplier=cm), r, w)

    def memset(eng, out, val, w):
        return P.op(eng, lambda e: e.memset(out, val), (), w)

    PA = [psb("pa0"), psb("pa1")]
    PS2 = psb("ps2")
    PS3 = psb("ps3")
    PR = [psb("pr%d" % i) for i in range(4)]
    pa_i = [0]
    pr_i = [0]

    pa_front = [False]

    def nextpa():
        if pa_front[0]:
            return PA[0], "pa0"
        pa_i[0] ^= 1
        t = PA[pa_i[0]]
        return t, "pa%d" % pa_i[0]

    def nextpr():
        pr_i[0] = (pr_i[0] + 1) % 4
        return PR[pr_i[0]], "pr%d" % pr_i[0]

    ident = sb("ident", [128, 128])
    cst = sb("cst", [128, 4])
    onesD = sb("onesD", [128, 128])
    onesC = sb("onesC", [128, 128])
    onesH = sb("onesH", [128, 128])
    ones1 = sb("ones1", [128, 128])
    EHf = sb("EHf", [128, H2])
    EH = sb("EH", [128, H2], F32R)
    identR = sb("identR", [128, 128], F32R)
    memset("pool", ident[:], 0.0, ["ident"])
    asel(ident[:], ident[:], [[-1, 128]], ALU.not_equal, 1.0, 0, 1, ["ident"], ["ident"])
    cp("dve", identR[:], ident[:], ["ident"], ["identR"])
    memset("pool", cst[:, 0:1], 1.0, ["cst"])
    memset("pool", cst[:, 1:2], EPS, ["cst"])
    memset("pool", cst[:, 2:3], 0.0, ["cst"])
    memset("pool", cst[:, 3:4], -1.0, ["cst"])
    memset("pool", onesD[:], 1.0 / D, ["onesD"])
    memset("pool", onesC[:], 1.0 / DC, ["onesC"])
    memset("pool", onesH[:], 1.0 / 128, ["onesH"])
    memset("pool", ones1[:], 1.0, ["ones1"])
    memset("pool", EHf[:], 0.0, ["EHf"])
    asel(EHf[:], EHf[:], [[-1, H2]], ALU.not_equal, 1.0, 0, 1, ["EHf"], ["EHf"])
    cp("dve", EH[:], EHf[:], ["EHf"], ["EH"])

    class Grp:
        pass

    def make_masks(name, seglen):
        nseg = 128 // seglen
        g = Grp()
        g.seglen, g.nseg = seglen, nseg
        g.McT = sb(name + "McT", [128, 128])
        g.MsT = sb(name + "MsT", [128, 128])
        g.SegAll = sb(name + "Seg", [128, 128])
        g.rowm = sb(name + "rowm", [128, nseg])
        g.key = name + "masks"
        k = [g.key]
        for m, off in ((g.McT, 0), (g.MsT, -1)):
            memset("pool", m[:], 1.0, k)
            asel(m[:], m[:], [[1, 128]], ALU.is_ge, 0.0, off, -1, k, k)
            v = m[:].rearrange("p (a b) -> p a b", b=seglen)
            asel(v, v, [[-seglen, nseg], [0, seglen]], ALU.is_ge, 0.0, 0, 1, k, k)
        memset("pool", g.SegAll[:], 1.0, k)
        v = g.SegAll[:].rearrange("p (a b) -> p a b", b=seglen)
        asel(v, v, [[-seglen, nseg], [0, seglen]], ALU.is_ge, 0.0, 0, 1, k, k)
        asel(v, v, [[seglen, nseg], [0, seglen]], ALU.is_ge, 0.0, seglen - 1, -1, k, k)
        memset("pool", g.rowm[:], 1.0, k)
        asel(g.rowm[:], g.rowm[:], [[-seglen, nseg]], ALU.is_ge, 0.0, 0, 1, k, k)
        asel(g.rowm[:], g.rowm[:], [[seglen, nseg]], ALU.is_ge, 0.0, seglen - 1, -1, k, k)
        return g

    GP = make_masks("p", 64)
    GS = make_masks("s", 8)

    stage = sb("stage", [128, 128])
    memset("pool", stage[:], 0.0, ["stage"])

    def load_cols(name, src, nrows):
        dst = sb(name, [128, nrows])
        for r0 in range(0, nrows, 128):
            n = min(128, nrows - r0)
            dma("sp", stage[0:n, :], src[r0:r0 + n, :], [], ["stage"], "stage")
            pt, pk = nextpr()
            P.op("pe", lambda e, pt=pt: e.transpose(out=pt[:, 0:128], in_=stage[:, :], identity=ident[:]),
                 ["stage", "ident"], [pk])
            cp("dve", dst[:, r0:r0 + n], pt[:, 0:n], [pk], [name])
        return dst

    ng = load_cols("ng", norm_g, L * KT)
    bada = load_cols("bada", b_ada, L * 3 * KT)
    wqc = load_cols("wqc", w_qc, L * 4 * 3 * H)
    hng = load_cols("hng", hn_g, L)
    wdw = load_cols("wdw", w_dw, L * 31 * CT)
    bdw = load_cols("bdw", b_dw, L * CT)
    lng = load_cols("lng", ln_g, L * CT)
    lnb = load_cols("lnb", ln_b, L * CT)
    fg = load_cols("fg", final_g, KT)

    nea = sb("nea", [128, L * H])
    dtb = sb("dtb", [128, L * H])
    dma("sp", nea[:], a_log.partition_broadcast(128), [], ["nea"], "nea")
    dma("sp", dtb[:], dt_bias.partition_broadcast(128), [], ["dtb"], "dtb")
    act(nea[:], nea[:], AF.Exp, ["nea"], ["nea"])
    ts("dve", nea[:], nea[:], -1.0, None, ALU.mult, None, ["nea"], ["nea"])

    NWS = 3
    wsl = [sb("wsl%d" % i, [128, KT, 128], WDT) for i in range(NWS)]
    ws_i = [0]

    def load_w(src3):
        ws_i[0] = (ws_i[0] + 1) % NWS
        i = ws_i[0]
        key = "wsl%d" % i
        dma("pool", wsl[i][:], src3, [], [key], key)
        return wsl[i], key

    def w_in_tile(l, col0):
        return w_in[l, :, col0:col0 + 128].rearrange("(k p) c -> p k c", p=128)

    NC17 = 1 + NS
    xio = sb("xio", [128, D])
    ccs = xio[0:NC17, :]
    memset("pool", xio[:], 0.0, ["xio"])
    scT = sb("scT", [128, KT, NC17], WDT)
    Amod = sb("Amod", [128, L, KT, NC17])
    Bmod = sb("Bmod", [128, L, KT, NC17])
    Gmod = sb("Gmod", [128, L, KT, NC17])
    dma("sp", ccs, cc, [], ["xio"], "xio")
    act(ccs, ccs, AF.Silu, ["xio"], ["xio"])
    for k in range(KT):
        pt, pk = nextpr()
        P.op("pe", lambda e, pt=pt, k=k: e.transpose(out=pt[:, 0:128], in_=xio[:, k * 128:(k + 1) * 128],
                                                     identity=ident[:]), ["xio", "ident"], [pk])
        cp("dve", scT[:, k, :], pt[:, 0:NC17], [pk], ["scT"])
    for l in range(L):
        for j in range(3 * KT):
            wt, wk = load_w(w_ada[l, :, j * 128:(j + 1) * 128].rearrange("(k p) c -> p k c", p=128))
            pt, pk = nextpr()
            for k in range(KT):
                mm(pt[:, 0:NC17], wt[:, k, :], scT[:, k, :], k == 0, k == KT - 1, [wk, "scT"], [pk])
            bcol = bada[:, l * 3 * KT + j: l * 3 * KT + j + 1]
            if j < KT:
                ts("dve", Bmod[:, l, j, :], pt[:, 0:NC17], bcol, None, ALU.add, None, [pk, "bada"], ["Bmod"])
            elif j < 2 * KT:
                ts("dve", Amod[:, l, j - KT, :], pt[:, 0:NC17], bcol, 1.0, ALU.add, ALU.add, [pk, "bada"], ["Amod"])
                ts("dve", Amod[:, l, j - KT, :], Amod[:, l, j - KT, :], ng[:, l * KT + j - KT: l * KT + j - KT + 1],
                   None, ALU.mult, None, ["Amod", "ng"], ["Amod"])
            else:
                ts("dve", Gmod[:, l, j - 2 * KT, :], pt[:, 0:NC17], bcol, None, ALU.add, None, [pk, "bada"], ["Gmod"])

    cur_blk = [0]

    def ckpt(n, cond=True):
        if cfg.dbg == n and cond and cur_blk[0] == getattr(cfg, 'dbg_block', 0):
            raise StopBuild()

    Sp = sb("Sp", [128, L, H, 128])
    Stmp = sb("Stmp", [128, 128])
    qtail = sb("qtail", [128, L, 3 * H, 3])
    gtail = sb("gtail", [128, L, CT, 30])
    memset("pool", Sp[:], 0.0, ["Sp"])
    memset("pool", qtail[:], 0.0, ["qtail"])
    memset("pool", gtail[:], 0.0, ["gtail"])

    xTf = sb("xTf", [128, KT * TB])
    ALIAS = KT * (TB - 128) >= 3 * NS * 128
    hT = sb("hT", [128, KT, TB], WDT)
    mixT = sb("mixT", [128, KT, TB], WDT)
    cv = sb("cv", [128, CT, TB])
    t512 = [sb("t512_%d" % i, [128, max(TB, 512)]) for i in range(3)]
    for i in range(3):
        memset("pool", t512[i][:], 0.0, ["t512_%d" % i])
    t5_i = [0]

    def tmp512():
        t5_i[0] = (t5_i[0] + 1) % 3
        return t512[t5_i[0]], "t512_%d" % t5_i[0]

    rs = sb("rs", [128, TB])
    wab = sb("wab", [128, KT, H2], WDT)
    graw = sb("graw", [128, NTB, H])
    gbt = sb("gbt", [128, NTB, 128])
    memset("pool", gbt[:], 0.0, ["gbt"])
    e2t = sb("e2t", [128, NTB, H])
    e2m = sb("e2m", [128, NTB, 16, H])
    gtmp = sb("gtmp", [128, H2])
    gbT = sb("gbT", [128, NTB, 128], F32R)
    cp("dve", gbT[:].rearrange("p a b -> p (a b)"), t512[0][:, 0:NTB * 128], ["t512_0"], ["gbT"])
    XW = max(3 + TB, NS * (3 + TS))
    xpre = sb("xpre", [128, 3, XW])
    qkvas = [sb("qkva%d" % i, [128, 3, TB]) for i in range(2)]
    zss = [sb("zs%d" % i, [128, TB]) for i in range(2)]
    knqss = [sb("knqs%d" % i, [128, NTB, 256]) for i in range(2)]
    oT = sb("oT", [128, TB])
    UW = max(30 + TB, NS * (30 + TS))
    ubuf = sb("ubuf", [128, UW])
    mu = sb("mu", [128, TB])
    rsc = sb("rsc", [128, TB])
    names = ["eg", "bb", "qG", "kgT", "dd", "decT", "dmC", "dmS", "attnT", "LTp", "A0", "A1", "AT0", "AT1",
             "PT", "kgtok", "vtok", "Wn", "UT", "Utok", "UT0"]
    shared_t = ("UT", "Utok", "UT0")
    r_names = ("attnT", "LTp", "A0", "A1", "AT0", "AT1", "PT", "kgtok", "vtok", "Utok")
    tdt = lambda n: F32R if n in r_names else F32
    tls = [{n: sb(n + ("" if n in shared_t else "0"), [128, 128], tdt(n)) for n in names}]
    tls.append({n: (tls[0][n] if n in shared_t else sb(n + "1", [128, 128], tdt(n))) for n in names})
    tl = tls[0]
    nbi = [0, 0]
    kdec2s = [sb("kdec2_%d" % i, [128, 2, 128], F32R) for i in range(2)]
    if ALIAS:
        o0 = KT * 128
        Sin = xTf[:, o0:o0 + NS * 128].rearrange("p (s v) -> p s v", s=NS)
        Sout = xTf[:, o0 + NS * 128:o0 + 2 * NS * 128].rearrange("p (s v) -> p s v", s=NS)
        kdec16 = xTf[:, o0 + 2 * NS * 128:o0 + 3 * NS * 128].rearrange("p (s v) -> p s v", s=NS)
    else:
        Sin = sb("Sin", [128, NS, 128])[:]
        Sout = sb("Sout", [128, NS, 128])[:]
        kdec16 = sb("kdec16", [128, NS, 128])[:]
    st48 = sb("st48", [128, 3, 128])
    memset("pool", st48[:], 0.0, ["st48"])
    st120 = sb("st120", [128, 4, 128])
    memset("pool", st120[:], 0.0, ["st120"])
    memset("pool", tl["UT"][:], 0.0, ["UT"])
    cp("dve", tl["Utok"][:], t512[0][:, 0:128], ["t512_0"], ["Utok"])

    def v3(ap, nseq):
        return ap.rearrange("p (s t) -> p s t", s=nseq)

    def do_block(is_sample, bi):
        if is_sample:
            ntok, nseq, Tq, grp, c0, ncol = 128, NS, TS, GS, 1, NS
            xsrc, ydst = xs, ys
        else:
            ntok, nseq, Tq, grp, c0, ncol = TB, 1, TB, GP, 0, 1
            xsrc, ydst = xp[bi * TB:(bi + 1) * TB, :], yp[bi * TB:(bi + 1) * TB, :]
        last = is_sample or bi == NB - 1
        cur_blk[0] = NB if is_sample else bi
        nt = ntok // 128
        seglen, nseg = grp.seglen, grp.nseg
        nd = int(math.log2(seglen)) - 1
        MK = [grp.key]
        xT = xTf[:, 0:KT * ntok].rearrange("p (k t) -> p k t", k=KT)
        if is_sample and ALIAS:
            memset("pool", xTf[:, KT * 128:KT * 128 + 1], 0.0, ["xT", "Sin", "Sout", "kdec16"])

        def bc(modt, l, k):
            return modt[:, l, k, c0:c0 + ncol].unsqueeze(2).broadcast_to([128, nseq, Tq])

        for tt_ in range(nt):
            dma("sp", xio[:], xsrc[tt_ * 128:(tt_ + 1) * 128, :], [], ["xio"], "xio")
            for k4 in range(0, KT, 4):
                n4 = min(4, KT - k4)
                pt, pk = nextpa()
                for q in range(n4):
                    P.op("pe", lambda e, pt=pt, q=q, k4=k4: e.transpose(
                        out=pt[:, q * 128:(q + 1) * 128], in_=xio[:, (k4 + q) * 128:(k4 + q + 1) * 128],
                        identity=ident[:]), ["xio", "ident"], [pk])
                cp("act", xT[:, k4:k4 + n4, tt_ * 128:(tt_ + 1) * 128],
                   pt[:, 0:n4 * 128].rearrange("p (a b) -> p a b", b=128), [pk], ["xT"])

        def rms_stats(dst):
            for k in range(KT):
                tq, tk = tmp512()
                act(tq[:, 0:ntok], xT[:, k, 0:ntok], AF.Square, ["xT"], [tk])
                mm(PS2[:, 0:ntok], onesD[:], tq[:, 0:ntok], k == 0, k == KT - 1, [tk, "onesD"], ["ps2"])
            act(dst[:, 0:ntok], PS2[:, 0:ntok], AF.Ln, ["ps2", "cst"], ["rs"], bias=cst[:, 1:2])
            act(dst[:, 0:ntok], dst[:, 0:ntok], AF.Exp, ["rs"], ["rs"], scale=-0.5)

        for l in range(L):
            rms_stats(rs)
            for k in range(KT):
                tq, tk = tmp512()
                if nseq == 1:
                    stt(tq[:, 0:ntok], xT[:, k, 0:ntok], Amod[:, l, k, 0:1], rs[:, 0:ntok], ALU.mult, ALU.mult,
                        ["xT", "Amod", "rs"], [tk])
                    ts("dve", hT[:, k, 0:ntok], tq[:, 0:ntok], Bmod[:, l, k, 0:1], None, ALU.add, None,
                       [tk, "Bmod"], ["hT"])
                else:
                    tt("dve", tq[:, 0:ntok], xT[:, k, 0:ntok], rs[:, 0:ntok], ALU.mult, ["xT", "rs"], [tk])
                    tt("dve", v3(tq[:, 0:ntok], nseq), v3(tq[:, 0:ntok], nseq), bc(Amod, l, k), ALU.mult,
                       [tk, "Amod"], [tk])
                    tt("dve", v3(hT[:, k, 0:ntok], nseq), v3(tq[:, 0:ntok], nseq), bc(Bmod, l, k), ALU.add,
                       [tk, "Bmod"], ["hT"])

            dma("pool", wab[:], w_in[l, :, 4 * DD:4 * DD + H2].rearrange("(k p) c -> p k c", p=128),
                [], ["wab"], "wab")
            for tt_ in range(nt):
                pt, pk = nextpr()
                for k in range(KT):
                    mm(pt[:, 0:H2], hT[:, k, tt_ * 128:(tt_ + 1) * 128], wab[:, k, :], k == 0, k == KT - 1,
                       ["hT", "wab"], [pk])
                act(gbt[:, tt_, H:H2], pt[:, 0:H], AF.Sigmoid, [pk], ["gbt"])
                tt("dve", gtmp[:, 0:H], pt[:, H:H2], dtb[:, l * H:(l + 1) * H], ALU.add, [pk, "dtb"], ["gtmp"])
                act(gtmp[:, 0:H], gtmp[:, 0:H], AF.Exp, ["gtmp"], ["gtmp"])
                act(gtmp[:, 0:H], gtmp[:, 0:H], AF.Ln, ["gtmp", "cst"], ["gtmp"], bias=cst[:, 0:1])
                tt("dve", graw[:, tt_, :], gtmp[:, 0:H], nea[:, l * H:(l + 1) * H], ALU.mult, ["gtmp", "nea"], ["graw"])
                pt, pk = nextpr()
                mm(pt[:, 0:H], grp.McT[:], graw[:, tt_, :], True, True, MK + ["graw"], [pk])
                mm(pt[:, H:H2], grp.SegAll[:], graw[:, tt_, :], True, True, MK + ["graw"], [pk])
                cp("act", gbt[:, tt_, 0:H], pt[:, 0:H], [pk], ["gbt"])
                tt("dve", gtmp[:, H:H2], pt[:, H:H2], gbt[:, tt_, 0:H], ALU.subtract, [pk, "gbt"], ["gtmp"])
                act(e2t[:, tt_, :], gtmp[:, H:H2], AF.Exp, ["gtmp"], ["e2t"])
                tt("dve", e2m[:, tt_, 0:nseg, :],
                   e2t[:, tt_, :].unsqueeze(1).broadcast_to([128, nseg, H]),
                   grp.rowm[:].unsqueeze(2).broadcast_to([128, nseg, H]), ALU.mult, ["e2t"] + MK, ["e2m"])
                pt, pk = nextpr()
                P.op("pe", lambda e, pt=pt, tt_=tt_: e.transpose(out=pt[:, 0:128], in_=gbt[:, tt_, :],
                                                                identity=ident[:]), ["gbt", "ident"], [pk])
                cp("dve", gbT[0:H2, tt_, :], pt[0:H2, 0:128], [pk], ["gbT"])

            US = 30 + Tq
            uv = ubuf[:, 0:nseq * US].rearrange("p (s t) -> p s t", s=nseq)

            def conf_part1(j):
                if is_sample:
                    dma("sp", st120[0:120, :, :], sg[l, :, j * 128:(j + 1) * 128].rearrange("(g r) x -> r g x", r=120),
                        [], ["st120"], "st120")
                    pt, pk = PS2, "ps2"
                    for g4 in range(4):
                        P.op("pe", lambda e, pt=pt, g4=g4: e.transpose(out=pt[:, g4 * 128:(g4 + 1) * 128], in_=st120[:, g4, :],
                                                                       identity=ident[:]), ["st120", "ident"], [pk])
                    cp("act", uv.rearrange("p (g s) t -> p g s t", g=4)[:, :, :, 0:30], pt[:, 0:512].rearrange("p (g x) -> p g x", g=4)[:, :, 0:120].rearrange("p g (s r) -> p g s r", r=30), [pk], ["ubuf"])
                else:
                    cp("pool", uv[:, 0, 0:30], gtail[:, l, j, :], ["gtail"], ["ubuf"])
                wb_, wbk = load_w(w_in_tile(l, 4 * DD + H2 + DC + j * 128))
                pb_, pbk = nextpa()
                for k in range(KT):
                    mm(pb_[:, 0:ntok], wb_[:, k, :], hT[:, k, 0:ntok], k == 0, k == KT - 1, [wbk, "hT"], [pbk])
                tq, tk = tmp512()
                act(tq[:, 0:ntok], pb_[:, 0:ntok], AF.Sigmoid, [pbk], [tk])
                wa, wak = load_w(w_in_tile(l, 4 * DD + H2 + j * 128))
                pa_, pak = nextpa()
                for k in range(KT):
                    mm(pa_[:, 0:ntok], wa[:, k, :], hT[:, k, 0:ntok], k == 0, k == KT - 1, [wak, "hT"], [pak])
                tt("dve", uv[:, :, 30:30 + Tq], v3(pa_[:, 0:ntok], nseq), v3(tq[:, 0:ntok], nseq), ALU.mult,
                   [pak, tk], ["ubuf"])
                if not is_sample:
                    cp("pool", gtail[:, l, j, :], uv[:, 0, Tq:Tq + 30], ["ubuf"], ["gtail"])
                if last:
                    nr = nseq * 30
                    tq2, tk2 = tmp512()
                    cp("pool", tq2[:, 0:nr].rearrange("p (s r) -> p s r", r=30), uv[:, :, Tq:Tq + 30], ["ubuf"], [tk2])
                    if is_sample:
                        pt, pk = PS2, "ps2"
                        for g4 in range(4):
                            P.op("pe", lambda e, pt=pt, g4=g4, tq2=tq2: e.transpose(
                                out=pt[:, g4 * 128:(g4 + 1) * 128], in_=tq2[:, g4 * 120:g4 * 120 + 128],
                                identity=ident[:]), [tk2, "ident"], [pk])
                        cp("act", st120[0:120, :, :], pt[0:120, 0:512].rearrange("p (g x) -> p g x", g=4), [pk], ["st120"])
                        dma("sp", ogs[l, :, j * 128:(j + 1) * 128].rearrange("(g r) x -> r g x", r=120), st120[0:120, :, :],
                            ["st120"], [], "st120")
                    else:
                        pt, pk = PS2, "ps2"
                        P.op("pe", lambda e, pt=pt, tq2=tq2: e.transpose(out=pt[:, 0:128], in_=tq2[:, 0:128], identity=ident[:]),
                             [tk2, "ident"], [pk])
                        cp("act", st120[0:30, 0, :], pt[0:30, 0:128], [pk], ["st120"])
                        dma("sp", ogp[l, :, j * 128:(j + 1) * 128], st120[0:30, 0, :], ["st120"], [], "st120")
                ov = v3(cv[:, j, 0:ntok], nseq)
                for q in range(31):
                    wcol = wdw[:, (l * 31 + q) * CT + j:(l * 31 + q) * CT + j + 1]
                    if q == 0:
                        ts("dve", ov, uv[:, :, 0:Tq], wcol, bdw[:, l * CT + j:l * CT + j + 1], ALU.mult, ALU.add,
                           ["ubuf", "wdw", "bdw"], ["cv"])
                    else:
                        stt(ov, uv[:, :, q:q + Tq], wcol, ov, ALU.mult, ALU.add, ["ubuf", "wdw", "cv"], ["cv"])
                pst, pstk = nextpa()
                mm(pst[:, 0:ntok], onesC[:], cv[:, j, 0:ntok], True, True, ["cv", "onesC"], [pstk])
                if j == 0:
                    cp("act", mu[:, 0:ntok], pst[:, 0:ntok], [pstk], ["mu"])
                else:
                    tt("dve", mu[:, 0:ntok], mu[:, 0:ntok], pst[:, 0:ntok], ALU.add, ["mu", pstk], ["mu"])
                tq, tk = tmp512()
                act(tq[:, 0:ntok], cv[:, j, 0:ntok], AF.Square, ["cv"], [tk])
                pst, pstk = nextpa()
                mm(pst[:, 0:ntok], onesC[:], tq[:, 0:ntok], True, True, [tk, "onesC"], [pstk])
                if j == 0:
                    cp("act", rsc[:, 0:ntok], pst[:, 0:ntok], [pstk], ["rsc"])
                else:
                    tt("dve", rsc[:, 0:ntok], rsc[:, 0:ntok], pst[:, 0:ntok], ALU.add, ["rsc", pstk], ["rsc"])

            def head_front(h):
                hb = h % 2
                qkva, knqs, zs = qkvas[hb], knqss[hb], zss[hb]
                QK, KQ, ZK = "qkva%d" % hb, "knqs%d" % hb, "zs%d" % hb
                XS = 3 + Tq
                xv = xpre[:, :, 0:nseq * XS].rearrange("p c (s t) -> p c s t", s=nseq)
                if is_sample:
                    dma("sp", st48[0:NS * 3, :, :], sq[l, :, :].rearrange("r (c x) -> r c x", c=3)[:, :, h * 128:(h + 1) * 128],
                        [], ["st48"], "st48")
                    pt, pk = PS2, "ps2"
                    for c in range(3):
                        P.op("pe", lambda e, pt=pt, c=c: e.transpose(out=pt[:, c * 128:(c + 1) * 128], in_=st48[:, c, :],
                                                                     identity=ident[:]), ["st48", "ident"], [pk])
                    for c in range(3):
                        cp("act", xv[:, c, :, 0:3], pt[:, c * 128:c * 128 + 48].rearrange("p (s r) -> p s r", r=3),
                           [pk], ["xpre"])
                else:
                    for c in range(3):
                        cp("pool", xv[:, c, 0, 0:3], qtail[:, l, c * H + h, :], ["qtail"], ["xpre"])
                for c in range(3):
                    wt, wk = load_w(w_in_tile(l, c * DD + h * 128))
                    pt, pk = nextpa()
                    for k in range(KT):
                        mm(pt[:, 0:ntok], wt[:, k, :], hT[:, k, 0:ntok], k == 0, k == KT - 1, [wk, "hT"], [pk])
                    cp("act", xv[:, c, :, 3:3 + Tq], v3(pt[:, 0:ntok], nseq), [pk], ["xpre"])
                    if not is_sample:
                        cp("pool", qtail[:, l, c * H + h, :], xv[:, c, 0, Tq:Tq + 3], ["xpre"], ["qtail"])
                if last:
                    nr = nseq * 3
                    pt, pk = PS2, "ps2"
                    for c in range(3):
                        tq, tk = tmp512()
                        cp("pool", tq[:, 0:nr].rearrange("p (s r) -> p s r", r=3), xv[:, c, :, Tq:Tq + 3], ["xpre"], [tk])
                        P.op("pe", lambda e, pt=pt, c=c, tq=tq: e.transpose(
                            out=pt[:, c * 128:(c + 1) * 128], in_=tq[:, 0:128], identity=ident[:]), [tk, "ident"], [pk])
                    cp("act", st48[0:nr, :, :], pt[0:nr, 0:384].rearrange("p (c x) -> p c x", c=3), [pk], ["st48"])
                    od = (oqs if is_sample else oqp)[l, :, :].rearrange("r (c x) -> r c x", c=3)[:, :, h * 128:(h + 1) * 128]
                    dma("sp", od, st48[0:nr, :, :], ["st48"], [], "st48")
                wt, wk = load_w(w_in_tile(l, 3 * DD + h * 128))
                pt, pk = nextpa()
                for k in range(KT):
                    mm(pt[:, 0:ntok], wt[:, k, :], hT[:, k, 0:ntok], k == 0, k == KT - 1, [wk, "hT"], [pk])
                act(zs[:, 0:ntok], pt[:, 0:ntok], AF.Silu, [pk], [ZK])
                for c in range(3):
                    ct = c * H + h
                    ov = v3(qkva[:, c, 0:ntok], nseq)
                    for j in range(4):
                        wcol = wqc[:, (l * 4 + j) * 3 * H + ct:(l * 4 + j) * 3 * H + ct + 1]
                        if j == 0:
                            ts("dve", ov, xv[:, c, :, 0:Tq], wcol, None, ALU.mult, None, ["xpre", "wqc"], [QK])
                        else:
                            stt(ov, xv[:, c, :, j:j + Tq], wcol, ov, ALU.mult, ALU.add, ["xpre", "wqc", QK], [QK])
                    act(qkva[:, c, 0:ntok], qkva[:, c, 0:ntok], AF.Silu, [QK], [QK])
                for c in (1, 0):
                    tq, tk = tmp512()
                    act(tq[:, 0:ntok], qkva[:, c, 0:ntok], AF.Square, [QK], [tk])
                    mm(PS2[:, 0:ntok], ones1[:], tq[:, 0:ntok], True, True, [tk, "ones1"], ["ps2"])
                    act(tq[:, 0:ntok], PS2[:, 0:ntok], AF.Ln, ["ps2", "cst"], [tk], bias=cst[:, 1:2])
                    act(tq[:, 0:ntok], tq[:, 0:ntok], AF.Exp, [tk], [tk], scale=-0.5)
                    src = qkva[:, c, 0:ntok].rearrange("p (a b) -> p a b", b=128)
                    rv = tq[:, 0:ntok].rearrange("p (a b) -> p a b", b=128)
                    if c == 1:
                        tt("dve", knqs[:, 0:nt, 0:128], src, rv, ALU.mult, [QK, tk], [KQ])
                    else:
                        stt(knqs[:, 0:nt, 128:256], src, QSC, rv, ALU.mult, ALU.mult, [QK, tk], [KQ])
                if is_sample:
                    dma("sp", Sin, sd[l, :, h, :, :].rearrange("s p v -> p s v"), [], ["Sin"], "Sin")


            def tile_solve(h, tt_, sid):
                hb = h % 2
                qkva, knqs, zs = qkvas[hb], knqss[hb], zss[hb]
                QK, KQ, ZK = "qkva%d" % hb, "knqs%d" % hb, "zs%d" % hb
                tl, SS = tls[sid], str(sid)
                kdecv, kdk = (kdec16, "kdec16") if is_sample else (kdec2s[sid][:], "kdec2_%d" % sid)
                tsl = slice(tt_ * 128, (tt_ + 1) * 128)

                def nb():
                    nbi[sid] ^= 1
                    j = 2 * sid + nbi[sid]
                    return PR[j], "pr%d" % j
                tsl = slice(tt_ * 128, (tt_ + 1) * 128)
                gccol = gbt[:, tt_, h:h + 1]
                becol = gbt[:, tt_, H + h:H + h + 1]
                pa_, pak = nb()
                mm(pa_[:, 0:128], EH[:, h:h + 1].broadcast_to([128, 128]), gbT[:, tt_, :], True, True, ["EH", "gbT"], [pak])
                mm(pa_[:, 128:256], EH[:, H + h:H + h + 1].broadcast_to([128, 128]), gbT[:, tt_, :], True, True, ["EH", "gbT"], [pak])
                act(tl["eg"][:], pa_[:, 0:128], AF.Exp, [pak], [("eg" + SS)])
                cp("act", tl["bb"][:], pa_[:, 128:256], [pak], [("bb" + SS)])
                ts("dve", tl["dd"][:], pa_[:, 0:128], gccol, 0.0, ALU.subtract, ALU.min, [pak, "gbt"], [("dd" + SS)])
                act(tl["decT"][:], tl["dd"][:], AF.Exp, [("dd" + SS)], [("decT" + SS)])
                tt("pool", tl["dmC"][:], tl["decT"][:], grp.McT[:], ALU.mult, [("decT" + SS)] + MK, [("dmC" + SS)])
                tt("pool", tl["dmS"][:], tl["decT"][:], grp.MsT[:], ALU.mult, [("decT" + SS)] + MK, [("dmS" + SS)])
                tt("dve", tl["qG"][:], knqs[:, tt_, 128:256], tl["eg"][:], ALU.mult, [KQ, ("eg" + SS)], [("qG" + SS)])
                tt("dve", tl["kgT"][:], knqs[:, tt_, 0:128], tl["eg"][:], ALU.mult, [KQ, ("eg" + SS)], [("kgT" + SS)])
                pb_, pbk = nb()
                mm(pb_[:, 0:256], knqs[:, tt_, 0:128], knqs[:, tt_, :], True, True, [KQ], [pbk])
                tt("dve", tl["attnT"][:], pb_[:, 128:256], tl["dmC"][:], ALU.mult, [pbk, ("dmC" + SS)], [("attnT" + SS)])
                stt(tl["LTp"][:], pb_[:, 0:128], becol, tl["dmS"][:], ALU.mult, ALU.mult, [pbk, "gbt", ("dmS" + SS)], [("LTp" + SS)])
                pc_, pck = nb()
                P.op("pe", lambda e, pc_=pc_: e.transpose(out=pc_[:, 0:128].bitcast(F32R), in_=tl["LTp"][:], identity=identR[:]),
                     [("LTp" + SS), "identR"], [pck])
                cp("act", tl["A0"][:], pc_[:, 0:128], [pck], [("A0" + SS)])
                tt("pool", tl["PT"][:], ident[:], tl["LTp"][:], ALU.subtract, ["ident", ("LTp" + SS)], [("PT" + SS)])
                A, AT, Ak, ATk = tl["A0"], tl["LTp"], ("A0" + SS), ("LTp" + SS)
                for kk in range(1, nd + 1):
                    An, Ank = (tl["A1"], ("A1" + SS)) if (kk % 2) else (tl["A0"], ("A0" + SS))
                    ATn, ATnk = (tl["AT1"], ("AT1" + SS)) if (kk % 2) else (tl["AT0"], ("AT0" + SS))
                    p1, p1k = nb()
                    mm(p1[:, 0:128], AT[:], A[:], True, True, [Ak, ATk], [p1k])
                    if kk < nd:
                        p2, p2k = nb()
                        mm(p2[:, 0:128], A[:], AT[:], True, True, [Ak, ATk], [p2k])
                    cp("act", An[:], p1[:, 0:128], [p1k], [Ank])
                    if kk < nd:
                        cp("dve", ATn[:], p2[:, 0:128], [p2k], [ATnk])
                    p3, p3k = nb()
                    mm(p3[:, 0:128], An[:], tl["PT"][:], True, True, [Ank, ("PT" + SS)], [p3k])
                    tt("dve", tl["PT"][:], tl["PT"][:], p3[:, 0:128], ALU.add, [("PT" + SS), p3k], [("PT" + SS)])
                    A, AT, Ak, ATk = An, ATn, Ank, ATnk
                p1, p1k = nb()
                P.op("pe", lambda e, p1=p1: e.transpose(out=p1[:, 0:128], in_=tl["kgT"][:], identity=ident[:]),
                     [("kgT" + SS), "ident"], [p1k])
                cp("act", tl["kgtok"][:], p1[:, 0:128], [p1k], [("kgtok" + SS)])
                p2, p2k = nb()
                P.op("pe", lambda e, p2=p2, tsl=tsl: e.transpose(out=p2[:, 0:128], in_=qkva[:, 2, tsl], identity=ident[:]),
                     [QK, "ident"], [p2k])
                cp("dve", tl["vtok"][:], p2[:, 0:128], [p2k], [("vtok" + SS)])
                p3, p3k = nb()
                P.op("pe", lambda e, p3=p3, tt_=tt_: e.transpose(out=p3[:, 0:128], in_=knqs[:, tt_, 0:128], identity=ident[:]),
                     [KQ, "ident"], [p3k])
                for i in range(nseg):
                    if i % 2 == 0:
                        ts("dve", kdecv[:, i, :], p3[:, 0:128], e2m[:, tt_, i, h:h + 1], None, ALU.mult, None,
                           [p3k, "e2m"], [kdk])
                    else:
                        act(kdecv[:, i, :], p3[:, 0:128], AF.Identity, [p3k, "e2m"], [kdk], scale=e2m[:, tt_, i, h:h + 1])
                p4, p4k = nb()
                mm(p4[:, 0:128], tl["kgtok"][:], tl["PT"][:], True, True, [("kgtok" + SS), ("PT" + SS)], [p4k])
                stt(tl["Wn"][:], p4[:, 0:128], -1.0, tl["bb"][:], ALU.mult, ALU.mult, [p4k, ("bb" + SS)], [("Wn" + SS)])

            def tile_state(h, tt_, sid):
                hb = h % 2
                qkva, knqs, zs = qkvas[hb], knqss[hb], zss[hb]
                QK, KQ, ZK = "qkva%d" % hb, "knqs%d" % hb, "zs%d" % hb
                tl, SS = tls[sid], str(sid)
                kdecv, kdk = (kdec16, "kdec16") if is_sample else (kdec2s[sid][:], "kdec2_%d" % sid)
                tsl = slice(tt_ * 128, (tt_ + 1) * 128)

                def nb():
                    nbi[sid] ^= 1
                    j = 2 * sid + nbi[sid]
                    return PR[j], "pr%d" % j
                pv_, pvk = (PA[1], "pa1")
                mm(pv_[:, 0:128], tl["vtok"][:], tl["PT"][:], True, True, [("vtok" + SS), ("PT" + SS)], [pvk])
                tt("dve", tl["UT0"][:], pv_[:, 0:128], tl["bb"][:], ALU.mult, [pvk, ("bb" + SS)], ["UT0"])
                po_, pok = PS3, "ps3"
                for i in range(nseg):
                    a, b = i * seglen, (i + 1) * seglen
                    if is_sample:
                        s_in, s_out, sik, sok = Sin[:, i, :], Sout[:, i, :], "Sin", "Sout"
                    elif i % 2 == 0:
                        s_in, s_out, sik, sok = Sp[:, l, h, :], Stmp[:], "Sp", "Stmp"
                    else:
                        s_in, s_out, sik, sok = Stmp[:], Sp[:, l, h, :], "Stmp", "Sp"
                    if (not is_sample) or i == 0:
                        pw_, pwk = (PA[1], "pa1")
                    mm(pw_[:, a:b], s_in, tl["Wn"][:, a:b], True, True, [sik, ("Wn" + SS)], [pwk])
                    mm(po_[:, a:b], s_in, tl["qG"][:, a:b], True, True, [sik, ("qG" + SS)], [pok])
                    if (not is_sample) or i == nseg - 1:
                        a0 = a if not is_sample else 0
                        tt("dve", tl["UT"][:, a0:b], pw_[:, a0:b], tl["UT0"][:, a0:b], ALU.add, [pwk, "UT0"], ["UT"])
                        p5, p5k = (PA[1], "pa1")
                        P.op("pe", lambda e, p5=p5: e.transpose(out=p5[:, 0:128], in_=tl["UT"][:], identity=ident[:]),
                             ["UT", "ident"], [p5k])
                        cp("act", tl["Utok"][:], p5[:, 0:128], [p5k], ["Utok"])
                    if not is_sample:
                        p6, p6k = (PA[1], "pa1")
                        mm(p6[:, 0:128], kdecv[:, i, :], tl["Utok"][:], True, True, [kdk, "Utok"], [p6k])
                        stt(s_out, s_in, tl["eg"][:, b - 1:b], p6[:, 0:128], ALU.mult, ALU.add, [sik, ("eg" + SS), p6k], [sok])
                if is_sample:
                    for i in range(nseg):
                        b = (i + 1) * seglen
                        p6, p6k = (PA[1], "pa1")
                        mm(p6[:, 0:128], kdecv[:, i, :], tl["Utok"][:].bitcast(F32), True, True, [kdk, "Utok"], [p6k])
                        stt(Sout[:, i, :], Sin[:, i, :], tl["eg"][:, b - 1:b], p6[:, 0:128], ALU.mult, ALU.add,
                            ["Sin", ("eg" + SS), p6k], ["Sout"])
                cp("act", oT[:, tsl], po_[:, 0:128], [pok], ["oT"])
                p7, p7k = (PA[1], "pa1")
                mm(p7[:, 0:128], tl["Utok"][:], tl["attnT"][:], True, True, ["Utok", ("attnT" + SS)], [p7k])
                tt("dve", oT[:, tsl], oT[:, tsl], p7[:, 0:128], ALU.add, ["oT", p7k], ["oT"])


            def head_back(h):
                hb = h % 2
                qkva, knqs, zs = qkvas[hb], knqss[hb], zss[hb]
                QK, KQ, ZK = "qkva%d" % hb, "knqs%d" % hb, "zs%d" % hb
                if is_sample:
                    dma("sp", oSs[l, :, h, :, :].rearrange("s p v -> p s v"), Sout, ["Sout"], [], "Sout")
                elif last:
                    dma("sp", oSp[l, h, :, :], Sp[:, l, h, :], ["Sp"], [], "Sp")
                tq, tk = tmp512()
                act(tq[:, 0:ntok], oT[:, 0:ntok], AF.Square, ["oT"], [tk])
                mm(PS2[:, 0:ntok], onesH[:], tq[:, 0:ntok], True, True, [tk, "onesH"], ["ps2"])
                act(tq[:, 0:ntok], PS2[:, 0:ntok], AF.Ln, ["ps2", "cst"], [tk], bias=cst[:, 1:2])
                act(tq[:, 0:ntok], tq[:, 0:ntok], AF.Exp, [tk], [tk], scale=-0.5)
                tt("dve", tq[:, 0:ntok], tq[:, 0:ntok], oT[:, 0:ntok], ALU.mult, [tk, "oT"], [tk])
                stt(mixT[:, h, 0:ntok], tq[:, 0:ntok], hng[:, l:l + 1], zs[:, 0:ntok], ALU.mult, ALU.mult,
                    [tk, "hng", ZK], ["mixT"])


            def record(fn, h):
                P.rec = []
                fn(h)
                r, P.rec = P.rec, None
                return r

            def replay(ops):
                for o in ops:
                    P.op(*o)

            def merge2(x, y):
                if not y:
                    return list(x)
                if not x:
                    return list(y)
                out_, yi = [], 0
                for xi, o in enumerate(x):
                    out_.append(o)
                    want = (xi + 1) * len(y) // len(x)
                    while yi < want:
                        out_.append(y[yi])
                        yi += 1
                out_.extend(y[yi:])
                return out_

            def rec2(fn, *a):
                P.rec = []
                fn(*a)
                r, P.rec = P.rec, None
                return r

            pa_front[0] = True
            replay(record(head_front, 0))
            pa_front[0] = False
            for h in range(H):
                S_ = [rec2(tile_solve, h, t, t % 2) for t in range(nt)]
                Q_ = [rec2(tile_state, h, t, t % 2) for t in range(nt)]
                if nt == 1:
                    t_ops = S_[0] + Q_[0]
                else:
                    t_ops = merge2(S_[0], S_[1]) + Q_[0]
                    for t in range(1, nt):
                        t_ops += merge2(Q_[t], S_[t + 1]) if t + 1 < nt else Q_[t]
                pa_front[0] = True
                f_ops = record(head_front, h + 1) if h + 1 < H else []
                if not is_sample:
                    f_ops = f_ops + record(conf_part1, h)
                pa_front[0] = False
                replay(merge2(t_ops, f_ops))
                head_back(h)
            if is_sample:
                for j in range(CT):
                    conf_part1(j)

            tq, tk = tmp512()
            tt("dve", tq[:, 0:ntok], mu[:, 0:ntok], mu[:, 0:ntok], ALU.mult, ["mu"], [tk])
            tt("dve", rsc[:, 0:ntok], rsc[:, 0:ntok], tq[:, 0:ntok], ALU.subtract, ["rsc", tk], ["rsc"])
            act(rsc[:, 0:ntok], rsc[:, 0:ntok], AF.Ln, ["rsc", "cst"], ["rsc"], bias=cst[:, 1:2])
            act(rsc[:, 0:ntok], rsc[:, 0:ntok], AF.Exp, ["rsc"], ["rsc"], scale=-0.5)
            for j in range(CT):
                wz, wzk = load_w(w_in_tile(l, 4 * DD + H2 + 2 * DC + j * 128))
                pa_, pak = nextpa()
                for k in range(KT):
                    mm(pa_[:, 0:ntok], wz[:, k, :], hT[:, k, 0:ntok], k == 0, k == KT - 1, [wzk, "hT"], [pak])
                tz, tzk = tmp512()
                act(tz[:, 0:ntok], pa_[:, 0:ntok], AF.Silu, [pak], [tzk])
                tq, tk = tmp512()
                tt("dve", tq[:, 0:ntok], cv[:, j, 0:ntok], mu[:, 0:ntok], ALU.subtract, ["cv", "mu"], [tk])
                tt("dve", tq[:, 0:ntok], tq[:, 0:ntok], rsc[:, 0:ntok], ALU.mult, [tk, "rsc"], [tk])
                act(tq[:, 0:ntok], tq[:, 0:ntok], AF.Silu, [tk, "lng", "lnb"], [tk],
                    scale=lng[:, l * CT + j:l * CT + j + 1], bias=lnb[:, l * CT + j:l * CT + j + 1])
                tt("dve", mixT[:, H + j, 0:ntok], tq[:, 0:ntok], tz[:, 0:ntok], ALU.mult, [tk, tzk], ["mixT"])

            for m in range(KT):
                wt, wk = load_w(w_out[l, :, m * 128:(m + 1) * 128].rearrange("(k p) c -> p k c", p=128))
                pt, pk = nextpa()
                for e_ in range(KT):
                    mm(pt[:, 0:ntok], wt[:, e_, :], mixT[:, e_, 0:ntok], e_ == 0, e_ == KT - 1, [wk, "mixT"], [pk])
                if nseq == 1:
                    stt(xT[:, m, 0:ntok], pt[:, 0:ntok], Gmod[:, l, m, 0:1], xT[:, m, 0:ntok], ALU.mult, ALU.add,
                        [pk, "Gmod", "xT"], ["xT"])
                else:
                    tq, tk = tmp512()
                    tt("dve", v3(tq[:, 0:ntok], nseq), v3(pt[:, 0:ntok], nseq), bc(Gmod, l, m), ALU.mult, [pk, "Gmod"], [tk])
                    tt("dve", xT[:, m, 0:ntok], xT[:, m, 0:ntok], tq[:, 0:ntok], ALU.add, ["xT", tk], ["xT"])

        rms_stats(rs)
        for k in range(KT):
            stt(xT[:, k, 0:ntok], xT[:, k, 0:ntok], fg[:, k:k + 1], rs[:, 0:ntok], ALU.mult, ALU.mult,
                ["xT", "fg", "rs"], ["xT"])
        for tt_ in range(nt):
            for k4 in range(0, KT, 4):
                n4 = min(4, KT - k4)
                pt, pk = nextpa()
                for q in range(n4):
                    P.op("pe", lambda e, pt=pt, q=q, k4=k4, tt_=tt_: e.transpose(
                        out=pt[:, q * 128:(q + 1) * 128], in_=xT[:, k4 + q, tt_ * 128:(tt_ + 1) * 128],
                        identity=ident[:]), ["xT", "ident"], [pk])
                cp("act", xio[:, k4 * 128:(k4 + n4) * 128], pt[:, 0:n4 * 128], [pk], ["xio"])
            dma("sp", ydst[tt_ * 128:(tt_ + 1) * 128, :], xio[:], ["xio"], [], "xio")

    try:
        for bi in range(NB):
            do_block(False, bi)
        do_block(True, 0)
    except StopBuild:
        pass
    P.emit()
    st.close()
    return nc


_NC_CACHE = {}


def _get_nc(cfg_key):
    if cfg_key not in _NC_CACHE:
        _NC_CACHE[cfg_key] = build(Cfg(*cfg_key))
    return _NC_CACHE[cfg_key]


def make_in_maps(inp, cfg, ncores):
    D, L, KT, H, DD, DC, CT = cfg.D, cfg.L, cfg.KT, cfg.H, cfg.DD, cfg.DC, cfg.CT
    f = lambda a: np.ascontiguousarray(np.asarray(a, dtype=np.float32))
    Bp = inp["x_prompt"].shape[0]
    shared = {
        "norm_g": f(inp["norm_g"]).reshape(L * KT, 128),
        "w_ada": f(inp["w_ada"]),
        "b_ada": f(inp["b_ada"]).reshape(L * 3 * KT, 128),
        "w_in": f(inp["w_in"]),
        "w_qc": f(inp["w_qkv_conv"]).reshape(L * 4 * 3 * H, 128),
        "a_log": f(inp["a_log"]).reshape(1, L * H),
        "dt_bias": f(inp["dt_bias"]).reshape(1, L * H),
        "hn_g": f(inp["head_norm_g"]).reshape(L, 128),
        "w_dw": f(inp["w_dw"]).reshape(L * 31 * CT, 128),
        "b_dw": f(inp["b_dw"]).reshape(L * CT, 128),
        "ln_g": f(inp["ln_g"]).reshape(L * CT, 128),
        "ln_b": f(inp["ln_b"]).reshape(L * CT, 128),
        "w_out": f(inp["w_out"]),
        "final_g": f(inp["final_g"]).reshape(KT, 128),
    }
    maps = []
    NS = cfg.NS
    for c in range(ncores):
        b = c % Bp
        s0 = c * NS
        m = dict(shared)
        m["xp"] = f(inp["x_prompt"][b])
        m["xs"] = f(inp["x_sample"][s0:s0 + NS]).reshape(NS * cfg.TS, D)
        m["cc"] = f(np.concatenate([np.asarray(inp["c_prompt"])[b:b + 1], np.asarray(inp["c_sample"])[s0:s0 + NS]], axis=0))
        m["sd"] = f(np.asarray(inp["state_delta"])[:, s0:s0 + NS])
        m["sq"] = f(np.asarray(inp["state_qkv_conv"])[:, s0:s0 + NS]).reshape(L, NS * 3, 3 * DD)
        m["sg"] = f(np.asarray(inp["state_glu_conv"])[:, s0:s0 + NS]).reshape(L, NS * 30, DC)
        maps.append(m)
    return maps


def assemble(res, cfg, ncores, Bp):
    D, L, H, DD, DC, NS, TS = cfg.D, cfg.L, cfg.H, cfg.DD, cfg.DC, cfg.NS, cfg.TS
    g = lambda c, n: np.asarray(res[c][n], dtype=np.float32)
    y_prompt = np.stack([g(b, "yp") for b in range(Bp)])
    y_sample = np.concatenate([g(c, "ys").reshape(NS, TS, D) for c in range(ncores)], axis=0)
    Sp = np.stack([g(b, "oSp") for b in range(Bp)], axis=1)
    qp = np.stack([g(b, "oqp") for b in range(Bp)], axis=1)
    gp = np.stack([g(b, "ogp") for b in range(Bp)], axis=1)
    Ss = np.concatenate([g(c, "oSs") for c in range(ncores)], axis=1)
    qs = np.concatenate([g(c, "oqs").reshape(L, NS, 3, 3 * DD) for c in range(ncores)], axis=1)
    gs = np.concatenate([g(c, "ogs").reshape(L, NS, 30, DC) for c in range(ncores)], axis=1)
    return (y_prompt, y_sample, Sp, qp, gp, Ss, qs, gs)


def kernel(**inputs):
    ncores = 8
    cfg_key = (2048, 2048, 512, 2, BF16)
    cfg = Cfg(*cfg_key)
    nc = _get_nc(cfg_key)
    maps = make_in_maps(inputs, cfg, ncores)
    res = run_bass_kernel_spmd(nc, maps, core_ids=list(range(ncores)))
    return assemble(res.results, cfg, ncores, 4)
```
